# Optimizing a Trainium2 kernel written in Bass

```python
import jax, jax.numpy as jnp
from jax import lax
import numpy as np

D_MODEL = 2048
BATCH = 4
SEQ = 4096
DEPTH = 2

N_MIXERS = 2
N_GMLP_LAYERS = (DEPTH + 1) // 2
N_SWA_LAYERS = DEPTH // 2
D_FF = 5632
FFN_RESIDUAL_WEIGHT = 0.5
PLE_DIM = 256
RMS_EPS = 1e-6
LN_EPS = 1e-5
CHUNK = 128
GMLP_WIDTH = 2 * D_MODEL
GMLP_GROUPS = 16
GMLP_GROUP_DIM = GMLP_WIDTH // GMLP_GROUPS
N_Q_HEADS = 32
N_KV_HEADS = 4
HEAD_DIM = 64
Q_PER_KV = N_Q_HEADS // N_KV_HEADS
WINDOW = 128
ROPE_THETA = 500000.0
ROPE_DIM = HEAD_DIM // 4

kernel_name = "hybrid_gmlp_swa_sink_macaron"


def rms_norm(x, g):
    xf = x.astype(jnp.float32)
    y = xf * lax.rsqrt(jnp.mean(xf * xf, axis=-1, keepdims=True) + RMS_EPS)
    return (y * g.astype(jnp.float32)).astype(x.dtype)


def layer_norm(x, g, b):
    xf = x.astype(jnp.float32)
    mu = jnp.mean(xf, axis=-1, keepdims=True)
    var = jnp.mean(jnp.square(xf - mu), axis=-1, keepdims=True)
    y = (xf - mu) * lax.rsqrt(var + LN_EPS)
    return (y * g.astype(jnp.float32) + b.astype(jnp.float32)).astype(x.dtype)


def swiglu(h, w1, w3, w2):
    return (jax.nn.silu(h @ w1) * (h @ w3)) @ w2


def gmlp_chunk_mixer(h, w_in, ln_g, ln_b, w_s, b_s, w_out):
    B, S, _ = h.shape
    z = jax.nn.gelu(h @ w_in, approximate=False)
    u, v = jnp.split(z, 2, axis=-1)
    v = layer_norm(v, ln_g, ln_b)
    v = v.reshape(B, S // CHUNK, CHUNK, GMLP_GROUPS, GMLP_GROUP_DIM)
    causal = jnp.tril(jnp.ones((CHUNK, CHUNK), dtype=bool))
    w = jnp.where(causal[None], w_s, jnp.zeros((), w_s.dtype))
    s = jnp.einsum('gts,bcsgd->bctgd', w, v) + b_s.T[:, :, None]
    gated = u * s.reshape(B, S, GMLP_WIDTH)
    return gated @ w_out


def rope_tables(S):
    inv_freq = ROPE_THETA ** (-jnp.arange(0, ROPE_DIM, 2, dtype=jnp.float32) / ROPE_DIM)
    ang = jnp.arange(S, dtype=jnp.float32)[:, None] * inv_freq[None, :]
    return jnp.cos(ang), jnp.sin(ang)


def apply_partial_rope(x, cos, sin):
    half = ROPE_DIM // 2
    c = cos[:, None, :].astype(x.dtype)
    s = sin[:, None, :].astype(x.dtype)
    x1 = x[..., :half]
    x2 = x[..., half:ROPE_DIM]
    return jnp.concatenate([x1 * c - x2 * s, x2 * c + x1 * s, x[..., ROPE_DIM:]], axis=-1)


def swa_sink_attention(h, wq, bq, wk, bk, wv, bv, sinks, wo, bo):
    B, S, _ = h.shape
    NB = S // WINDOW
    q = (h @ wq + bq).reshape(B, S, N_Q_HEADS, HEAD_DIM)
    k = (h @ wk + bk).reshape(B, S, N_KV_HEADS, HEAD_DIM)
    v = (h @ wv + bv).reshape(B, S, N_KV_HEADS, HEAD_DIM)
    cos, sin = rope_tables(S)
    q = apply_partial_rope(q, cos, sin)
    k = apply_partial_rope(k, cos, sin)
    q = q.reshape(B, NB, WINDOW, N_KV_HEADS, Q_PER_KV, HEAD_DIM)
    k = k.reshape(B, NB, WINDOW, N_KV_HEADS, HEAD_DIM)
    v = v.reshape(B, NB, WINDOW, N_KV_HEADS, HEAD_DIM)
    pad = ((0, 0), (1, 0), (0, 0), (0, 0), (0, 0))
    kb = jnp.concatenate([jnp.pad(k, pad)[:, :-1], k], axis=2)
    vb = jnp.concatenate([jnp.pad(v, pad)[:, :-1], v], axis=2)
    scores = jnp.einsum('bnqkgd,bnskd->bnkgqs', q, kb).astype(jnp.float32) * (HEAD_DIM ** -0.5)
    qpos = jnp.arange(WINDOW)[:, None] + WINDOW
    kpos = jnp.arange(2 * WINDOW)[None, :]
    diff = qpos - kpos
    band = (diff >= 0) & (diff < WINDOW)
    has_prev = (jnp.arange(NB) > 0)[:, None, None]
    valid = jnp.where(has_prev, band[None], (band & (kpos >= WINDOW))[None])
    scores = jnp.where(valid[None, :, None, None], scores, -jnp.inf)
    sink = sinks.astype(jnp.float32).reshape(N_KV_HEADS, Q_PER_KV)[None, None, :, :, None, None]
    m = jnp.maximum(jnp.max(scores, axis=-1, keepdims=True), sink)
    e = jnp.exp(scores - m)
    denom = jnp.sum(e, axis=-1, keepdims=True) + jnp.exp(sink - m)
    probs = (e / denom).astype(vb.dtype)
    o = jnp.einsum('bnkgqs,bnskd->bnqkgd', probs, vb).reshape(B, S, N_Q_HEADS * HEAD_DIM)
    return o @ wo + bo


def per_layer_embedding(x, p_i, g, w_gate, w_proj):
    gate = jax.nn.sigmoid(rms_norm(x, g) @ w_gate)
    return gate * (p_i @ w_proj)


def setup_inputs(seed: int = 0) -> dict:
    key = jax.random.key(seed)
    ks = iter(jax.random.split(key, 40))
    f32 = jnp.float32

    def dense(shape, scale=1.0):
        return jax.random.normal(next(ks), shape, f32) * (scale * shape[-2] ** -0.5)

    def gain(shape):
        return 1.0 + 0.02 * jax.random.normal(next(ks), shape, f32)

    def bias(shape, s=0.02):
        return s * jax.random.normal(next(ks), shape, f32)

    out_scale = (2.0 * DEPTH) ** -0.5
    NA, NBL = N_GMLP_LAYERS, N_SWA_LAYERS
    QW = N_Q_HEADS * HEAD_DIM
    KW = N_KV_HEADS * HEAD_DIM
    return {
        "x": jax.random.normal(next(ks), (BATCH, SEQ, D_MODEL), f32),
        "p": jax.random.normal(next(ks), (DEPTH, BATCH, SEQ, PLE_DIM), f32),
        "ffn1_norm": gain((DEPTH, D_MODEL)),
        "ffn1_w1": dense((DEPTH, D_MODEL, D_FF)),
        "ffn1_w3": dense((DEPTH, D_MODEL, D_FF)),
        "ffn1_w2": dense((DEPTH, D_FF, D_MODEL), out_scale),
        "mix_norm": gain((DEPTH, D_MODEL)),
        "ffn2_norm": gain((DEPTH, D_MODEL)),
        "ffn2_w1": dense((DEPTH, D_MODEL, D_FF)),
        "ffn2_w3": dense((DEPTH, D_MODEL, D_FF)),
        "ffn2_w2": dense((DEPTH, D_FF, D_MODEL), out_scale),
        "ple_norm": gain((DEPTH, D_MODEL)),
        "ple_w_gate": dense((DEPTH, D_MODEL, D_MODEL)),
        "ple_w_proj": dense((DEPTH, PLE_DIM, D_MODEL), out_scale),
        "gmlp_w_in": dense((NA, D_MODEL, 2 * GMLP_WIDTH)),
        "gmlp_ln_g": gain((NA, GMLP_WIDTH)),
        "gmlp_ln_b": bias((NA, GMLP_WIDTH)),
        "gmlp_w_s": dense((NA, GMLP_GROUPS, CHUNK, CHUNK), 0.5),
        "gmlp_b_s": 1.0 + bias((NA, GMLP_GROUPS, CHUNK)),
        "gmlp_w_out": dense((NA, GMLP_WIDTH, D_MODEL), out_scale),
        "swa_wq": dense((NBL, D_MODEL, QW)),
        "swa_bq": bias((NBL, QW)),
        "swa_wk": dense((NBL, D_MODEL, KW)),
        "swa_bk": bias((NBL, KW)),
        "swa_wv": dense((NBL, D_MODEL, KW)),
        "swa_bv": bias((NBL, KW)),
        "swa_sinks": 0.5 * jax.random.normal(next(ks), (NBL, N_Q_HEADS), f32),
        "swa_wo": dense((NBL, QW, D_MODEL), out_scale),
        "swa_bo": bias((NBL, D_MODEL)),
        "final_norm": gain((D_MODEL,)),
    }


def reference(x, p, ffn1_norm, ffn1_w1, ffn1_w3, ffn1_w2, mix_norm,
              ffn2_norm, ffn2_w1, ffn2_w3, ffn2_w2,
              ple_norm, ple_w_gate, ple_w_proj,
              gmlp_w_in, gmlp_ln_g, gmlp_ln_b, gmlp_w_s, gmlp_b_s, gmlp_w_out,
              swa_wq, swa_bq, swa_wk, swa_bk, swa_wv, swa_bv, swa_sinks, swa_wo, swa_bo,
              final_norm):
    for i in range(DEPTH):
        x = x + FFN_RESIDUAL_WEIGHT * swiglu(rms_norm(x, ffn1_norm[i]), ffn1_w1[i], ffn1_w3[i], ffn1_w2[i])
        h = rms_norm(x, mix_norm[i])
        j = i // N_MIXERS
        if i % N_MIXERS == 0:
            x = x + gmlp_chunk_mixer(h, gmlp_w_in[j], gmlp_ln_g[j], gmlp_ln_b[j],
                                     gmlp_w_s[j], gmlp_b_s[j], gmlp_w_out[j])
        else:
            x = x + swa_sink_attention(h, swa_wq[j], swa_bq[j], swa_wk[j], swa_bk[j],
                                       swa_wv[j], swa_bv[j], swa_sinks[j], swa_wo[j], swa_bo[j])
        x = x + FFN_RESIDUAL_WEIGHT * swiglu(rms_norm(x, ffn2_norm[i]), ffn2_w1[i], ffn2_w3[i], ffn2_w2[i])
        x = x + per_layer_embedding(x, p[i], ple_norm[i], ple_w_gate[i], ple_w_proj[i])
    return rms_norm(x, final_norm)
```

```python
import contextlib
import numpy as np
import concourse.bass as bass
import concourse.mybir as mybir
from concourse.bass_utils import run_bass_kernel_spmd

F32 = mybir.dt.float32
BF16 = mybir.dt.bfloat16
U8 = mybir.dt.uint8
AF = mybir.ActivationFunctionType
ALU = mybir.AluOpType
AX = mybir.AxisListType

D = 2048
DFF = 5632
NC16 = 16
NFFC = 44
NTOK = 2176
NREAL = 2048
TILES = [(0, 640), (640, 512), (1152, 512), (1664, 512)]
TMAX = 640
NSLOT = 7
SLOTC = 4096
N_CORES = 8

CV_G = 0
CV_BQ = 144
CV_BK = 160
CV_BO = 164
CV_EPSR = 180
CV_EPSL = 181
CV_SINK = 182
NV = 216
CB_ONES = 0
CB_ID = 128
CB_ROT = 256
CB_MN = 384
CB_M0 = 640
CB_TRIL = 896
CB_BV = 1024
NCB = 1280

COMPUTE = ("pe", "act", "dve", "pool")
ALLENG = COMPUTE + ("sp",)


class Buf:
    __slots__ = ("name", "w", "r")

    def __init__(self, name):
        self.name = name
        self.w = None
        self.r = {}


class Op:
    __slots__ = ("fn", "deps", "inc", "dma_sem", "waits", "incval")

    def __init__(self, fn, deps, dma_sem):
        self.fn = fn
        self.deps = deps
        self.inc = False
        self.dma_sem = dma_sem
        self.waits = None
        self.incval = None


class Tracker:
    def __init__(self, nc):
        self.nc = nc
        self.ops = {e: [] for e in ALLENG}
        self.dma_cnt = {}
        self.bar = {e: set() for e in ALLENG}

    def op(self, eng, fn, reads=(), writes=(), dma=None):
        nd = set()
        for b in reads:
            d = b.w
            if d is not None:
                if d[0] == "c" and d[1] == eng and dma is None and eng == "pe":
                    pass
                else:
                    nd.add(d)
        for b in writes:
            d = b.w
            if d is not None and not (d[0] == "c" and d[1] == eng and dma is None):
                nd.add(d)
            for d in b.r.values():
                if not (d[0] == "c" and d[1] == eng and dma is None):
                    nd.add(d)
        if self.bar[eng]:
            nd |= self.bar[eng]
            self.bar[eng] = set()
        idx = len(self.ops[eng])
        self.ops[eng].append(Op(fn, nd, dma))
        if dma is None:
            ev = ("c", eng, idx)
            rkey = eng
        else:
            self.dma_cnt[dma] = self.dma_cnt.get(dma, 0) + 16
            ev = ("d", dma, self.dma_cnt[dma])
            rkey = ("d", dma)
        for b in writes:
            b.w = ev
            b.r = {}
        for b in reads:
            if b.w is not ev:
                b.r[rkey] = ev
        return ev

    def barrier(self, engines=("pe", "act", "dve", "sp")):
        evs = set()
        for e in COMPUTE:
            i = len(self.ops[e]) - 1
            while i >= 0 and self.ops[e][i].dma_sem is not None:
                i -= 1
            if i >= 0:
                evs.add(("c", e, i))
        for e in engines:
            self.bar[e] |= {x for x in evs if x[1] != e}

    def finalize(self):
        for e, lst in self.ops.items():
            seen = {}
            for o in lst:
                w = {}
                for d in o.deps:
                    key = (d[0], d[1])
                    val = d[2]
                    if seen.get(key, -1) >= val:
                        continue
                    if w.get(key, -1) < val:
                        w[key] = val
                for key, val in w.items():
                    seen[key] = val
                    if key[0] == "c":
                        self.ops[key[1]][val].inc = True
                o.waits = w
        self.nincs = {}
        for e, lst in self.ops.items():
            c = 0
            for o in lst:
                if o.inc:
                    c += 1
                    o.incval = c
            self.nincs[e] = c

    def emit(self, final_waits=()):
        nc = self.nc
        self.finalize()
        with contextlib.ExitStack() as st:
            esem = {e: st.enter_context(nc.semaphore("s_" + e)) for e in COMPUTE}
            dsem = {k: st.enter_context(nc.semaphore("d_" + str(k))) for k in self.dma_cnt}
            block = st.enter_context(nc.Block())
            engobj = {"pe": "tensor", "act": "scalar", "dve": "vector", "pool": "gpsimd", "sp": "sync"}

            def run(e, eng):
                for o in self.ops[e]:
                    for key, val in o.waits.items():
                        if key[0] == "c":
                            eng.wait_ge(esem[key[1]], self.ops[key[1]][val].incval)
                        else:
                            eng.wait_ge(dsem[key[1]], val)
                    ins = o.fn(eng)
                    if o.dma_sem is not None:
                        ins.then_inc(dsem[o.dma_sem], 16)
                    elif o.inc:
                        ins.then_inc(esem[e], 1)
                if e == "sp":
                    for k in final_waits:
                        eng.wait_ge(dsem[k], self.dma_cnt[k])

            for e in ALLENG:
                getattr(block, engobj[e])(lambda eng, e=e: run(e, eng))


def pass_units():
    U = []

    def ffn(name, l):
        for g in range(22):
            U.append((name + "_w1", l, 0, D, g * 256, g * 256 + 256))
            U.append((name + "_w3", l, 0, D, g * 256, g * 256 + 256))
        for dc in range(16):
            for h in range(2):
                U.append((name + "_w2", l, h * 2816, h * 2816 + 2816, dc * 128, dc * 128 + 128))

    def ple(l):
        U.append(("ple_w_proj", l, 0, 256, 0, 2048))
        for g in range(8):
            U.append(("ple_w_gate", l, 0, D, g * 256, g * 256 + 256))

    for l in range(2):
        ffn("ffn1", l)
        if l == 0:
            for cg in range(16):
                U.append(("gmlp_w_in", 0, 0, D, 4096 + cg * 256, 4096 + cg * 256 + 256))
            U.append(("lng_b", 0, 0, 128, 0, 4096))
            U.append(("lnb_b", 0, 0, 128, 0, 4096))
            U.append(("wsT", 0, 0, 128, 0, 2048))
            U.append(("bs_b", 0, 0, 128, 0, 4096))
            for cu in range(16):
                U.append(("gmlp_w_in", 0, 0, D, cu * 256, cu * 256 + 256))
            for dc in range(16):
                U.append(("gmlp_w_out", 0, 0, 4096, dc * 128, dc * 128 + 128))
        else:
            for g in range(8):
                U.append(("swa_wq", 0, 0, D, g * 256, g * 256 + 256))
            for g in range(2):
                U.append(("wk_dup", 0, 0, D, g * 256, g * 256 + 256))
            U.append(("swa_wv", 0, 0, D, 0, 256))
            for g in range(8):
                U.append(("swa_wo", 0, 0, D, g * 256, g * 256 + 256))
        ffn("ffn2", l)
        ple(l)
    return U


def unit_cols(u):
    return ((u[3] - u[2]) // 128) * (u[5] - u[4])


def subtiles(lo, hi):
    n = hi - lo
    if n <= 512:
        return [(lo, hi)]
    nb = n // 128
    first = (nb + 1) // 2
    return [(lo, lo + first * 128), (lo + first * 128, hi)]


def build(n_stages=9, debug_raw=False):
    units = pass_units()
    offs = np.cumsum([0] + [unit_cols(u) for u in units])
    passcols = int(offs[-1])

    nc = bass.Bass("TRN2", target_bir_lowering=False)
    xT = nc.dram_tensor("xT", [D, NTOK], F32, kind="ExternalInput").ap()
    pT = nc.dram_tensor("pT", [2, 256, NTOK], F32, kind="ExternalInput").ap()
    wst = nc.dram_tensor("wst", [128, passcols], F32, kind="ExternalInput").ap()
    cvec_d = nc.dram_tensor("cvec", [128, NV], F32, kind="ExternalInput").ap()
    cbf_d = nc.dram_tensor("cbf", [128, NCB], F32, kind="ExternalInput").ap()
    rope_d = nc.dram_tensor("rope", [2, 128, NTOK], F32, kind="ExternalInput").ap()
    outT = nc.dram_tensor("outT", [D, NREAL], F32, kind="ExternalOutput").ap()

    with contextlib.ExitStack() as st:
        T = Tracker(nc)

        def sb(name, shape, dt):
            return st.enter_context(nc.sbuf_tensor(name, shape, dt))

        X = sb("X", [128, NC16, TMAX], F32)
        XN = sb("XN", [128, NC16, TMAX], BF16)
        R = sb("R", [128, 65536], U8)
        SL = [sb("SL%d" % i, [128, SLOTC], BF16) for i in range(NSLOT)]
        PT = sb("PT", [128, 2, TMAX], BF16)
        RS = sb("RS", [128, TMAX], F32)
        TMPA = [sb("TMPA%d" % i, [128, 512], F32) for i in range(2)]
        TMPB = [sb("TMPB%d" % i, [128, 512], F32) for i in range(2)]
        CV = sb("CV", [128, NV], F32)
        NSK = sb("NSK", [128, 32], F32)
        CB = sb("CB", [128, NCB], BF16)
        KC = sb("KC", [128, 4, 128], BF16)
        VC = sb("VC", [128, 256], BF16)
        SM = sb("SM", [128, 64], F32)
        BNS = sb("BNS", [128, 48], F32)
        SMA = [sb("SMA%d" % i, [128, 32], F32) for i in range(3)]
        PS = st.enter_context(nc.psum_tensor("PS", [128, 4096], F32))

        bX = [Buf("X%d" % c) for c in range(NC16)]
        bXN = [Buf("XN%d" % c) for c in range(NC16)]
        bSL = [Buf("SL%d" % i) for i in range(NSLOT)]
        bPT = Buf("PT")
        bRS = Buf("RS")
        bTA = [Buf("TA0"), Buf("TA1")]
        bTB = [Buf("TB0"), Buf("TB1")]
        bCV = Buf("CV")
        bNSK = Buf("NSK")
        bCB = Buf("CB")
        bKC = Buf("KC")
        bVC = Buf("VC")
        bSM = Buf("SM")
        bBNS = Buf("BNS")
        bSMA = [Buf("SMA%d" % i) for i in range(3)]
        bPS = [Buf("PS%d" % i) for i in range(8)]
        bOUT = Buf("OUT")

        H = R[:, 0:NFFC * TMAX * 2].bitcast(BF16).rearrange("p (c t) -> p c t", c=NFFC)
        bH = [Buf("H%d" % c) for c in range(NFFC)]
        PJ = R[:, 0:NC16 * TMAX * 4].bitcast(F32).rearrange("p (c t) -> p c t", c=NC16)
        bPJ = [Buf("PJ%d" % c) for c in range(NC16)]
        VR = R[:, 0:5 * 4096 * 2].bitcast(BF16).rearrange("p (b c) -> p b c", b=5)
        bVR = [Buf("VR%d" % b) for b in range(5)]
        OST = R[:, 0:NC16 * 512 * 4].bitcast(F32).rearrange("p (c t) -> p c t", c=NC16)
        o = 0
        QT = R[:, o:o + 16 * 512 * 2].bitcast(BF16).rearrange("p (c t) -> p c t", c=16); o += 16 * 512 * 2
        KTL = R[:, o:o + 4 * 640 * 2].bitcast(BF16).rearrange("p (c t) -> p c t", c=4); o += 4 * 640 * 2
        KTH = R[:, o:o + 4 * 640 * 2].bitcast(BF16).rearrange("p (c t) -> p c t", c=4); o += 4 * 640 * 2
        VT = R[:, o:o + 5 * 256 * 2].bitcast(BF16).rearrange("p (b c) -> p b c", b=5); o += 5 * 256 * 2
        EE = [R[:, o + i * 4096:o + (i + 1) * 4096].bitcast(F32).rearrange("p (h s) -> p h s", h=4) for i in range(3)]; o += 12288
        PP = [R[:, o + i * 2048:o + (i + 1) * 2048].bitcast(BF16).rearrange("p (h s) -> p h s", h=4) for i in range(3)]; o += 6144
        PTS = [R[:, o + i * 2048:o + (i + 1) * 2048].bitcast(BF16).rearrange("p (h s) -> p h s", h=8) for i in range(3)]; o += 6144
        COS = R[:, o:o + 640 * 4].bitcast(F32); o += 2560
        SIN = R[:, o:o + 640 * 4].bitcast(F32); o += 2560
        QTMP = [R[:, o + i * 1024:o + (i + 1) * 1024].bitcast(BF16) for i in range(2)]; o += 2048
        assert o <= 65536, o
        bQT = [Buf("QT%d" % c) for c in range(16)]
        bKTL = [Buf("KTL%d" % c) for c in range(4)]
        bKTH = [Buf("KTH%d" % c) for c in range(4)]
        bVT = [Buf("VT%d" % b) for b in range(5)]
        bEE = [Buf("EE%d" % i) for i in range(3)]
        bPP = [Buf("PP%d" % i) for i in range(3)]
        bPTS = [Buf("PTS%d" % i) for i in range(3)]
        bCOS = Buf("COS")
        bSIN = Buf("SIN")
        bQTMP = [Buf("QTMP0"), Buf("QTMP1")]

        state = {"bank": 0, "slot": 0, "unit": 0, "tmp": 0}

        def bank():
            b = state["bank"]
            state["bank"] = (b + 1) % 8
            return b

        def bankpair():
            b = state["bank"]
            if b % 2:
                b = (b + 1) % 8
            state["bank"] = (b + 2) % 8
            return b, b + 1

        def psv(b, n):
            return PS[:, b * 512:b * 512 + n]

        def tmpi():
            i = state["tmp"]
            state["tmp"] = 1 - i
            return i

        def wunit(expect_key):
            ui = state["unit"] % len(units)
            u = units[ui]
            assert u[0] == expect_key, (u, expect_key)
            state["unit"] += 1
            s = state["slot"]
            state["slot"] = (s + 1) % NSLOT
            ncols = unit_cols(u)
            off = int(offs[ui])
            T.op("pool", lambda e: e.dma_start(out=SL[s][:, 0:ncols], in_=wst[:, off:off + ncols]),
                 writes=[bSL[s]], dma="w%d" % s)
            kc = (u[3] - u[2]) // 128
            return SL[s][:, 0:ncols].rearrange("p (k n) -> p k n", k=kc), bSL[s]

        def mm(out, lhsT, rhs, start, stop, reads, bk):
            T.op("pe", lambda e: e.matmul(out, lhsT=lhsT, rhs=rhs, start=start, stop=stop),
                 reads=reads, writes=[bPS[bk]])

        T.op("sp", lambda e: e.dma_start(out=CV[:], in_=cvec_d), writes=[bCV], dma="cv")
        T.op("pool", lambda e: e.dma_start(out=CB[:], in_=cbf_d), writes=[bCB], dma="cb")
        T.op("dve", lambda e: e.tensor_scalar(out=NSK[:], in0=CV[:, CV_SINK:CV_SINK + 32], scalar1=-1.0, scalar2=None,
                                              op0=ALU.mult), reads=[bCV], writes=[bNSK])
        ONES = CB[:, CB_ONES:CB_ONES + 128]
        IDENT = CB[:, CB_ID:CB_ID + 128]
        ROT = CB[:, CB_ROT:CB_ROT + 128]

        def norm(gi, lo, hi):
            for c4 in range(4):
                cs = slice(4 * c4, 4 * c4 + 4)
                T.op("act", lambda e, cs=cs: e.activation(out=XN[:, cs, lo:hi], in_=X[:, cs, lo:hi], func=AF.Square),
                     reads=bX[cs], writes=bXN[cs])
            for (a, b) in subtiles(lo, hi):
                bk = bank()
                for c in range(NC16):
                    mm(psv(bk, b - a), ONES, XN[:, c, a:b], c == 0, c == NC16 - 1, [bCB, bXN[c]], bk)
                T.op("act", lambda e, a=a, b=b, bk=bk: e.activation(out=RS[:, a:b], in_=psv(bk, b - a), func=AF.Sqrt,
                                                                    bias=CV[:, CV_EPSR:CV_EPSR + 1], scale=1.0 / D),
                     reads=[bCV], writes=[bRS, bPS[bk]])
            T.op("dve", lambda e: e.reciprocal(out=RS[:, lo:hi], in_=RS[:, lo:hi]), reads=[bRS], writes=[bRS])
            for c in range(NC16):
                T.op("dve", lambda e, c=c: e.scalar_tensor_tensor(out=XN[:, c, lo:hi], in0=X[:, c, lo:hi],
                                                                  scalar=CV[:, CV_G + gi * 16 + c:CV_G + gi * 16 + c + 1],
                                                                  in1=RS[:, lo:hi], op0=ALU.mult, op1=ALU.mult),
                     reads=[bX[c], bRS, bCV], writes=[bXN[c]])

        def ffn(name, gi, lo, hi):
            norm(gi, lo, hi)
            subs = subtiles(lo, hi)
            for g in range(22):
                u1, b1 = wunit(name + "_w1")
                u3, b3 = wunit(name + "_w3")
                for j in range(2):
                    ffc = g * 2 + j
                    for (a, b) in subs:
                        n = b - a
                        bA = bank()
                        bB = bank()
                        for kc in range(NC16):
                            mm(psv(bA, n), u1[:, kc, j * 128:(j + 1) * 128], XN[:, kc, a:b], kc == 0, kc == 15, [b1, bXN[kc]], bA)
                        for kc in range(NC16):
                            mm(psv(bB, n), u3[:, kc, j * 128:(j + 1) * 128], XN[:, kc, a:b], kc == 0, kc == 15, [b3, bXN[kc]], bB)
                        i = tmpi()
                        T.op("act", lambda e, i=i, bA=bA, n=n: e.activation(out=TMPA[i][:, 0:n], in_=psv(bA, n), func=AF.Silu),
                             writes=[bTA[i], bPS[bA]])
                        T.op("dve", lambda e, i=i, bB=bB, n=n, ffc=ffc, a=a, b=b: e.tensor_tensor(
                            out=H[:, ffc, a:b], in0=TMPA[i][:, 0:n], in1=psv(bB, n), op=ALU.mult),
                            reads=[bTA[i]], writes=[bH[ffc], bPS[bB]])
            for dc in range(NC16):
                ua, ba = wunit(name + "_w2")
                ub, bb = wunit(name + "_w2")
                for (a, b) in subs:
                    n = b - a
                    bY = bank()
                    for ffc in range(NFFC):
                        uu, bu = (ua, ba) if ffc < 22 else (ub, bb)
                        mm(psv(bY, n), uu[:, ffc % 22, :], H[:, ffc, a:b], ffc == 0, ffc == NFFC - 1, [bu, bH[ffc]], bY)
                    T.op("dve", lambda e, dc=dc, a=a, b=b, bY=bY, n=n: e.scalar_tensor_tensor(
                        out=X[:, dc, a:b], in0=psv(bY, n), scalar=0.5, in1=X[:, dc, a:b], op0=ALU.mult, op1=ALU.add),
                        reads=[bX[dc]], writes=[bX[dc], bPS[bY]])

        def ple(l, gi, lo, hi, t0):
            norm(gi, lo, hi)
            subs = subtiles(lo, hi)
            for kc in range(2):
                T.op("pool", lambda e, kc=kc: e.dma_start(out=PT[:, kc, lo:hi], in_=pT[l, kc * 128:(kc + 1) * 128, t0 + lo:t0 + hi]),
                     writes=[bPT], dma="pt")
            up, bp = wunit("ple_w_proj")
            for dc in range(NC16):
                for (a, b) in subs:
                    n = b - a
                    bk = bank()
                    for kc in range(2):
                        mm(psv(bk, n), up[:, kc, dc * 128:(dc + 1) * 128], PT[:, kc, a:b], kc == 0, kc == 1, [bp, bPT], bk)
                    T.op("act", lambda e, dc=dc, a=a, b=b, bk=bk, n=n: e.activation(out=PJ[:, dc, a:b], in_=psv(bk, n), func=AF.Copy),
                         writes=[bPJ[dc], bPS[bk]])
            for g in range(8):
                ug, bg = wunit("ple_w_gate")
                for j in range(2):
                    dc = g * 2 + j
                    for (a, b) in subs:
                        n = b - a
                        bk = bank()
                        for kc in range(NC16):
                            mm(psv(bk, n), ug[:, kc, j * 128:(j + 1) * 128], XN[:, kc, a:b], kc == 0, kc == 15, [bg, bXN[kc]], bk)
                        i = tmpi()
                        T.op("act", lambda e, i=i, bk=bk, n=n: e.activation(out=TMPA[i][:, 0:n], in_=psv(bk, n), func=AF.Sigmoid),
                             writes=[bTA[i], bPS[bk]])
                        T.op("dve", lambda e, i=i, dc=dc, a=a, b=b, n=n: e.tensor_tensor(
                            out=TMPB[i][:, 0:n], in0=TMPA[i][:, 0:n], in1=PJ[:, dc, a:b], op=ALU.mult),
                            reads=[bTA[i], bPJ[dc]], writes=[bTB[i]])
                        T.op("dve", lambda e, i=i, dc=dc, a=a, b=b, n=n: e.tensor_tensor(
                            out=X[:, dc, a:b], in0=X[:, dc, a:b], in1=TMPB[i][:, 0:n], op=ALU.add),
                            reads=[bTB[i], bX[dc]], writes=[bX[dc]])

        def gmlp(gi, lo, hi):
            norm(gi, lo, hi)
            subs = subtiles(lo, hi)
            blocks = list(range(lo // 128, hi // 128))
            for cg in range(16):
                uv, bv = wunit("gmlp_w_in")
                for tb in blocks:
                    bk = bank()
                    for kc in range(NC16):
                        mm(psv(bk, 256), XN[:, kc, tb * 128:(tb + 1) * 128], uv[:, kc, :], kc == 0, kc == 15, [bv, bXN[kc]], bk)
                    T.op("act", lambda e, tb=tb, cg=cg, bk=bk: e.activation(out=VR[:, tb, cg * 256:(cg + 1) * 256], in_=psv(bk, 256), func=AF.Gelu),
                         writes=[bVR[tb], bPS[bk]])
            ug, bg = wunit("lng_b")
            ub, bb = wunit("lnb_b")
            uw, bw = wunit("wsT")
            us, bs = wunit("bs_b")
            for g in range(16):
                T.op("dve", lambda e, g=g: e.tensor_tensor(out=uw[:, 0, g * 128:(g + 1) * 128], in0=uw[:, 0, g * 128:(g + 1) * 128],
                                                           in1=CB[:, CB_TRIL:CB_TRIL + 128], op=ALU.mult),
                     reads=[bCB, bw], writes=[bw])
            for tb in blocks:
                for k in range(8):
                    T.op("dve", lambda e, tb=tb, k=k: e.bn_stats(out=BNS[:, k * 6:(k + 1) * 6], in_=VR[:, tb, k * 512:(k + 1) * 512]),
                         reads=[bVR[tb]], writes=[bBNS])
                T.op("dve", lambda e: e.bn_aggr(out=SM[:, 0:2], in_=BNS[:, 0:48]), reads=[bBNS], writes=[bSM])
                T.op("act", lambda e: e.activation(out=SM[:, 2:3], in_=SM[:, 1:2], func=AF.Sqrt, bias=CV[:, CV_EPSL:CV_EPSL + 1], scale=1.0),
                     reads=[bSM, bCV], writes=[bSM])
                T.op("dve", lambda e: e.reciprocal(out=SM[:, 3:4], in_=SM[:, 2:3]), reads=[bSM], writes=[bSM])
                T.op("dve", lambda e, tb=tb: e.tensor_scalar(out=VR[:, tb, :], in0=VR[:, tb, :], scalar1=SM[:, 0:1], scalar2=SM[:, 3:4],
                                                             op0=ALU.subtract, op1=ALU.mult), reads=[bSM, bVR[tb]], writes=[bVR[tb]])
                T.op("dve", lambda e, tb=tb: e.tensor_tensor(out=VR[:, tb, :], in0=VR[:, tb, :], in1=ug[:, 0, :], op=ALU.mult),
                     reads=[bg, bVR[tb]], writes=[bVR[tb]])
                T.op("dve", lambda e, tb=tb: e.tensor_tensor(out=VR[:, tb, :], in0=VR[:, tb, :], in1=ub[:, 0, :], op=ALU.add),
                     reads=[bb, bVR[tb]], writes=[bVR[tb]])
                for c4 in range(8):
                    bk = bank()
                    for cc in range(4):
                        c = c4 * 4 + cc
                        mm(PS[:, bk * 512 + cc * 128:bk * 512 + (cc + 1) * 128], VR[:, tb, c * 128:(c + 1) * 128],
                           uw[:, 0, (c // 2) * 128:(c // 2 + 1) * 128], True, True, [bVR[tb], bw], bk)
                    T.op("dve", lambda e, tb=tb, c4=c4, bk=bk: e.tensor_tensor(out=VR[:, tb, c4 * 512:(c4 + 1) * 512], in0=psv(bk, 512),
                                                                               in1=us[:, 0, c4 * 512:(c4 + 1) * 512], op=ALU.add),
                         reads=[bs], writes=[bVR[tb], bPS[bk]])
            for cu in range(16):
                uu, bu = wunit("gmlp_w_in")
                for j in range(2):
                    c = cu * 2 + j
                    for (a, b) in subs:
                        n = b - a
                        bk = bank()
                        for kc in range(NC16):
                            mm(psv(bk, n), uu[:, kc, j * 128:(j + 1) * 128], XN[:, kc, a:b], kc == 0, kc == 15, [bu, bXN[kc]], bk)
                        i = tmpi()
                        T.op("act", lambda e, i=i, bk=bk, n=n: e.activation(out=TMPA[i][:, 0:n], in_=psv(bk, n), func=AF.Gelu),
                             writes=[bTA[i], bPS[bk]])
                        sv = VR[:, a // 128:b // 128, c * 128:(c + 1) * 128]
                        T.op("dve", lambda e, i=i, n=n, sv=sv: e.tensor_tensor(
                            out=sv, in0=TMPA[i][:, 0:n].rearrange("p (b t) -> p b t", t=128), in1=sv, op=ALU.mult),
                            reads=[bTA[i]] + bVR[a // 128:b // 128], writes=bVR[a // 128:b // 128])
            for dc in range(NC16):
                uo, bo = wunit("gmlp_w_out")
                for (a, b) in subs:
                    n = b - a
                    bk = bank()
                    for c in range(32):
                        mm(psv(bk, n).rearrange("p (b t) -> p b t", t=128), uo[:, c, :], VR[:, a // 128:b // 128, c * 128:(c + 1) * 128],
                           c == 0, c == 31, [bo] + bVR[a // 128:b // 128], bk)
                    T.op("dve", lambda e, dc=dc, a=a, b=b, bk=bk, n=n: e.tensor_tensor(
                        out=X[:, dc, a:b], in0=psv(bk, n), in1=X[:, dc, a:b], op=ALU.add),
                        reads=[bX[dc]], writes=[bX[dc], bPS[bk]])

        def rope_proj(uw_, bw_, j, bias_col, dst, bdst, dlo, src_cols, tab_cols, dst2=None, bdst2=None):
            a, b = src_cols
            n = b - a
            ta, tb_ = tab_cols
            bk = bank()
            for kc in range(NC16):
                mm(psv(bk, n), uw_[:, kc, j * 128:(j + 1) * 128], XN[:, kc, a:b], kc == 0, kc == 15, [bw_, bXN[kc]], bk)
            i = tmpi()
            T.op("act", lambda e: e.activation(out=QTMP[i][:, 0:n], in_=psv(bk, n), func=AF.Identity,
                                               bias=CV[:, bias_col:bias_col + 1], scale=1.0),
                 reads=[bCV], writes=[bQTMP[i], bPS[bk]])
            bk2 = bank()
            mm(psv(bk2, n), ROT, QTMP[i][:, 0:n], True, True, [bCB, bQTMP[i]], bk2)
            T.op("dve", lambda e: e.tensor_tensor(out=TMPA[i][:, 0:n], in0=QTMP[i][:, 0:n], in1=COS[:, ta:tb_], op=ALU.mult),
                 reads=[bQTMP[i], bCOS], writes=[bTA[i]])
            T.op("dve", lambda e: e.tensor_tensor(out=TMPB[i][:, 0:n], in0=psv(bk2, n), in1=SIN[:, ta:tb_], op=ALU.mult),
                 reads=[bSIN], writes=[bTB[i], bPS[bk2]])
            if dst2 is None:
                T.op("dve", lambda e: e.tensor_tensor(out=dst[:, dlo:dlo + n], in0=TMPA[i][:, 0:n], in1=TMPB[i][:, 0:n], op=ALU.add),
                     reads=[bTA[i], bTB[i]], writes=[bdst])
            else:
                T.op("dve", lambda e: e.tensor_tensor(out=dst[0:64, dlo:dlo + n], in0=TMPA[i][0:64, 0:n], in1=TMPB[i][0:64, 0:n], op=ALU.add),
                     reads=[bTA[i], bTB[i]], writes=[bdst])
                T.op("dve", lambda e: e.tensor_tensor(out=dst2[64:128, dlo:dlo + n], in0=TMPA[i][64:128, 0:n], in1=TMPB[i][64:128, 0:n], op=ALU.add),
                     reads=[bTA[i], bTB[i]], writes=[bdst2])

        def swa(gi, T_, q0, t0, ti):
            lo = 0 if ti == 0 else 0
            norm(gi, lo, T_)
            nq = T_ - q0
            nkb = 5
            kcomp0 = 0 if ti == 0 else 1
            T.op("sp", lambda e: e.dma_start(out=COS[:, 0:T_], in_=rope_d[0, :, t0:t0 + T_]), writes=[bCOS], dma="cos")
            T.op("sp", lambda e: e.dma_start(out=SIN[:, 0:T_], in_=rope_d[1, :, t0:t0 + T_]), writes=[bSIN], dma="sin")
            for g in range(8):
                uq, bq_ = wunit("swa_wq")
                for j in range(2):
                    c = g * 2 + j
                    rope_proj(uq, bq_, j, CV_BQ + c, QT[:, c, :], bQT[c], 0, (q0, T_), (q0, T_))
            T.op("dve", lambda e: e.memset(KTL[64:128, :, :], 0.0), writes=bKTL)
            T.op("dve", lambda e: e.memset(KTH[0:64, :, :], 0.0), writes=bKTH)
            if ti > 0:
                T.op("dve", lambda e: e.tensor_copy(out=KTL[0:64, :, 0:128], in_=KC[0:64]), reads=[bKC], writes=bKTL)
                T.op("dve", lambda e: e.tensor_copy(out=KTH[64:128, :, 0:128], in_=KC[64:128]), reads=[bKC], writes=bKTH)
                T.op("dve", lambda e: e.tensor_copy(out=VT[:, 0, :], in_=VC[:]), reads=[bVC], writes=[bVT[0]])
            for g2 in range(2):
                uk, bk_ = wunit("wk_dup")
                for j in range(2):
                    g = g2 * 2 + j
                    for (a, b) in subtiles(0, T_):
                        rope_proj(uk, bk_, j, CV_BK + g, KTL[:, g, :], bKTL[g], kcomp0 * 128 + a, (a, b), (a, b),
                                  dst2=KTH[:, g, :], bdst2=bKTH[g])
            uv, bv_ = wunit("swa_wv")
            for tb in range(T_ // 128):
                kb = kcomp0 + tb
                bk = bank()
                for kc in range(NC16):
                    mm(psv(bk, 256), XN[:, kc, tb * 128:(tb + 1) * 128], uv[:, kc, :], kc == 0, kc == 15, [bv_, bXN[kc]], bk)
                T.op("dve", lambda e, kb=kb, bk=bk: e.tensor_tensor(out=VT[:, kb, :], in0=psv(bk, 256), in1=CB[:, CB_BV:CB_BV + 256], op=ALU.add),
                     reads=[bCB], writes=[bVT[kb], bPS[bk]])
            T.barrier(("pe", "act", "dve"))
            units_ = [(jq, g, hh) for jq in range(4) for g in range(4) for hh in range(2)]
            NU_ = len(units_)
            BT, BO = 6, 7

            def stA(u):
                jq, g, hh = units_[u]
                w = u % 3
                mcol = CB_M0 if (ti == 0 and jq == 0) else CB_MN
                for i4 in range(4):
                    h = g * 8 + hh * 4 + i4
                    c, half = h // 2, h % 2
                    bk = 2 * w + (i4 // 2)
                    outp = PS[:, bk * 512 + (i4 % 2) * 256:bk * 512 + (i4 % 2) * 256 + 256]
                    kt_, bkt_ = (KTL, bKTL) if half == 0 else (KTH, bKTH)
                    mm(outp, IDENT, CB[:, mcol:mcol + 256], True, False, [bCB], bk)
                    mm(outp, QT[:, c, jq * 128:(jq + 1) * 128], kt_[:, g, jq * 128:jq * 128 + 256], False, True,
                       [bQT[c], bkt_[g]], bk)

            def stB1(u):
                jq, g, hh = units_[u]
                w = u % 3
                b0, b1 = 2 * w, 2 * w + 1
                sc = PS[:, b0 * 512:b0 * 512 + 1024].rearrange("p (h s) -> p h s", h=4)
                S_, bS_ = SMA[w], bSMA[w]
                hs = g * 8 + hh * 4
                T.op("dve", lambda e: e.tensor_reduce(out=S_[:, 0:4], in_=sc, axis=AX.X, op=ALU.max),
                     writes=[bS_, bPS[b0], bPS[b1]])
                T.op("dve", lambda e: e.scalar_tensor_tensor(out=S_[:, 4:8], in0=S_[:, 0:4], scalar=-0.125,
                                                             in1=NSK[:, hs:hs + 4], op0=ALU.mult, op1=ALU.min),
                     reads=[bS_, bNSK], writes=[bS_])
                T.op("dve", lambda e: e.tensor_tensor(out=S_[:, 12:16], in0=S_[:, 4:8],
                                                      in1=CV[:, CV_SINK + hs:CV_SINK + hs + 4], op=ALU.add),
                     reads=[bS_, bCV], writes=[bS_])
                for i4 in range(4):
                    T.op("act", lambda e, i4=i4: e.activation(
                        out=EE[w][:, i4, :], in_=sc[:, i4, :], func=AF.Exp, bias=S_[:, 4 + i4:5 + i4], scale=0.125,
                        accum_out=S_[:, 8 + i4:9 + i4]),
                        reads=[bS_], writes=[bEE[w], bS_, bPS[b0], bPS[b1]])
                T.op("act", lambda e: e.activation(out=S_[:, 16:20], in_=S_[:, 12:16], func=AF.Exp),
                     reads=[bS_], writes=[bS_])

            def stB2(u):
                w = u % 3
                S_, bS_ = SMA[w], bSMA[w]
                T.op("dve", lambda e: e.tensor_tensor(out=S_[:, 20:24], in0=S_[:, 16:20], in1=S_[:, 8:12], op=ALU.add),
                     reads=[bS_], writes=[bS_])
                T.op("dve", lambda e: e.reciprocal(out=S_[:, 24:28], in_=S_[:, 20:24]), reads=[bS_], writes=[bS_])
                for i4 in range(4):
                    T.op("dve", lambda e, i4=i4: e.tensor_scalar(
                        out=PP[w][:, i4, :], in0=EE[w][:, i4, :], scalar1=S_[:, 24 + i4:25 + i4], scalar2=None, op0=ALU.mult),
                        reads=[bEE[w], bS_], writes=[bPP[w]])

            def stC(u):
                w = u % 3
                ptp = PS[:, BT * 512:BT * 512 + 512].bitcast(BF16).rearrange("p (h s) -> p h s", h=8)
                for i4 in range(4):
                    for kb in range(2):
                        T.op("pe", lambda e, i4=i4, kb=kb: e.transpose(
                            out=ptp[:, i4 * 2 + kb, :], in_=PP[w][:, i4, kb * 128:(kb + 1) * 128], identity=IDENT),
                            reads=[bPP[w], bCB], writes=[bPS[BT]])
                T.op("act", lambda e: e.activation(out=PTS[w][:], in_=ptp, func=AF.Copy), writes=[bPTS[w], bPS[BT]])

            def stD(u):
                jq, g, hh = units_[u]
                w = u % 3
                hs = g * 8 + hh * 4
                for i4 in range(4):
                    half = (hs + i4) % 2
                    for kb in range(2):
                        mm(PS[half * 64:half * 64 + 64, BO * 512 + (i4 // 2) * 128:BO * 512 + (i4 // 2) * 128 + 128],
                           VT[:, jq + kb, g * 64:(g + 1) * 64], PTS[w][:, i4 * 2 + kb, :], kb == 0, kb == 1,
                           [bVT[jq + kb], bPTS[w]], BO)

            def stE(u):
                jq, g, hh = units_[u]
                c0 = (g * 8 + hh * 4) // 2
                T.op("dve", lambda e: e.tensor_copy(
                    out=XN[:, c0:c0 + 2, q0 + jq * 128:q0 + (jq + 1) * 128],
                    in_=PS[:, BO * 512:BO * 512 + 256].rearrange("p (c t) -> p c t", c=2)),
                    writes=[bXN[c0], bXN[c0 + 1], bPS[BO]])

            stA(0)
            stA(1)
            stB1(0)
            for u in range(NU_):
                if u + 2 < NU_:
                    stA(u + 2)
                if u + 1 < NU_:
                    stB1(u + 1)
                stB2(u)
                if u >= 1:
                    stE(u - 1)
                stC(u)
                stD(u)
            stE(NU_ - 1)
            T.op("dve", lambda e: e.tensor_copy(out=KC[0:64], in_=KTL[0:64, :, 512:640]), reads=bKTL, writes=[bKC])
            T.op("dve", lambda e: e.tensor_copy(out=KC[64:128], in_=KTH[64:128, :, 512:640]), reads=bKTH, writes=[bKC])
            T.op("dve", lambda e: e.tensor_copy(out=VC[:], in_=VT[:, 4, :]), reads=[bVT[4]], writes=[bVC])
            subs = subtiles(q0, T_)
            for g in range(8):
                uo, bo2 = wunit("swa_wo")
                for j in range(2):
                    dc = g * 2 + j
                    for (a, b) in subs:
                        n = b - a
                        bk = bank()
                        for kc in range(NC16):
                            mm(psv(bk, n), uo[:, kc, j * 128:(j + 1) * 128], XN[:, kc, a:b], kc == 0, kc == 15, [bo2, bXN[kc]], bk)
                        T.op("dve", lambda e, dc=dc, a=a, b=b, bk=bk, n=n: e.scalar_tensor_tensor(
                            out=X[:, dc, a:b], in0=psv(bk, n), scalar=CV[:, CV_BO + dc:CV_BO + dc + 1], in1=X[:, dc, a:b],
                            op0=ALU.add, op1=ALU.add), reads=[bX[dc], bCV], writes=[bX[dc], bPS[bk]])

        def do_tile(ti, t0, T_, out_off):
            q0 = 128 if ti == 0 else 0
            T.barrier()
            for c4 in range(4):
                cs = slice(4 * c4, 4 * c4 + 4)
                T.op("sp", lambda e, cs=cs, c4=c4: e.dma_start(
                    out=X[:, cs, 0:T_], in_=xT[c4 * 512:(c4 + 1) * 512, t0:t0 + T_].rearrange("(c p) t -> p c t", p=128)),
                    writes=bX[cs], dma="xin%d" % c4)
            stage = 0

            def go():
                nonlocal stage
                stage += 1
                return stage <= n_stages

            if go():
                T.barrier(); ffn("ffn1", 0, 0, T_)
            if go():
                T.barrier(); gmlp(1, 0, T_)
            if go():
                T.barrier(); ffn("ffn2", 2, 0, T_)
            if go():
                T.barrier(); ple(0, 3, 0, T_, t0)
            if go():
                T.barrier(); ffn("ffn1", 4, 0, T_)
            if go():
                T.barrier(); swa(5, T_, q0, t0, ti)
            if go():
                T.barrier(); ffn("ffn2", 6, q0, T_)
            if go():
                T.barrier(); ple(1, 7, q0, T_, t0)
            state["unit"] = (ti + 1) * len(units)
            T.barrier()
            nq = T_ - q0
            if go() and not debug_raw:
                norm(8, q0, T_)
                for c in range(NC16):
                    T.op("dve", lambda e, c=c: e.scalar_tensor_tensor(out=OST[:, c, 0:nq], in0=X[:, c, q0:T_],
                                                                      scalar=CV[:, CV_G + 8 * 16 + c:CV_G + 8 * 16 + c + 1],
                                                                      in1=RS[:, q0:T_], op0=ALU.mult, op1=ALU.mult),
                         reads=[bX[c], bRS, bCV], writes=[bOUT])
            else:
                for c in range(NC16):
                    T.op("dve", lambda e, c=c: e.tensor_copy(out=OST[:, c, 0:nq], in_=X[:, c, q0:T_]), reads=[bX[c]], writes=[bOUT])
            for c4 in range(4):
                cs = slice(4 * c4, 4 * c4 + 4)
                T.op("sp", lambda e, cs=cs, c4=c4, oo=out_off: e.dma_start(
                    out=outT[c4 * 512:(c4 + 1) * 512, oo:oo + nq].rearrange("(c p) t -> p c t", p=128), in_=OST[:, cs, 0:nq]),
                    reads=[bOUT] + bH, dma="out")
        oo_ = 0
        for ti_, (t0_, TT_) in enumerate(TILES):
            do_tile(ti_, t0_, TT_, oo_)
            oo_ += 512
        T.emit(final_waits=["out"])
    return nc, T


def _kn(W, r0, r1, c0, c1):
    kc = (r1 - r0) // 128
    return np.ascontiguousarray(W[r0:r1, c0:c1].reshape(kc, 128, c1 - c0).transpose(1, 0, 2)).reshape(128, kc * (c1 - c0))


def host_prepare(inp):
    f32 = np.float32
    mats = {}
    for k in ("ffn1_w1", "ffn1_w3", "ffn1_w2", "ffn2_w1", "ffn2_w3", "ffn2_w2", "ple_w_gate", "ple_w_proj",
              "gmlp_w_in", "gmlp_w_out", "swa_wq", "swa_wv", "swa_wo"):
        mats[k] = np.asarray(inp[k], dtype=f32)
    wk = np.asarray(inp["swa_wk"], dtype=f32)[0]
    wk_dup = np.concatenate([wk[:, (g // 2) * 64:(g // 2) * 64 + 64] for g in range(8)], axis=1)
    mats["wk_dup"] = wk_dup[None]
    mats["lng_b"] = np.broadcast_to(np.asarray(inp["gmlp_ln_g"], f32)[0][None, :], (128, 4096))[None]
    mats["lnb_b"] = np.broadcast_to(np.asarray(inp["gmlp_ln_b"], f32)[0][None, :], (128, 4096))[None]
    ws = np.asarray(inp["gmlp_w_s"], f32)[0]
    mats["wsT"] = np.ascontiguousarray(ws.transpose(2, 0, 1)).reshape(128, 16 * 128)[None]
    bs = np.asarray(inp["gmlp_b_s"], f32)[0]
    bs32 = np.repeat(bs, 2, axis=0).reshape(1, 32 * 128)
    mats["bs_b"] = np.broadcast_to(bs32, (128, 4096))[None]
    units = pass_units()
    total = sum(unit_cols(u) for u in units)
    wst = np.empty((128, total), dtype=f32)
    off = 0
    for u in units:
        n = unit_cols(u)
        wst[:, off:off + n] = _kn(mats[u[0]][u[1]], u[2], u[3], u[4], u[5])
        off += n

    def pvec(v):
        return np.asarray(v, f32).reshape(16, 128).T

    cvec = np.zeros((128, NV), f32)
    gl = [inp["ffn1_norm"][0], inp["mix_norm"][0], inp["ffn2_norm"][0], inp["ple_norm"][0],
          inp["ffn1_norm"][1], inp["mix_norm"][1], inp["ffn2_norm"][1], inp["ple_norm"][1], inp["final_norm"]]
    for i, g in enumerate(gl):
        cvec[:, CV_G + i * 16:CV_G + (i + 1) * 16] = pvec(g)
    cvec[:, CV_BQ:CV_BQ + 16] = pvec(inp["swa_bq"][0])
    bk = np.asarray(inp["swa_bk"], f32)[0]
    bk_dup = np.concatenate([bk[(g // 2) * 64:(g // 2) * 64 + 64] for g in range(8)])
    cvec[:, CV_BK:CV_BK + 4] = bk_dup.reshape(4, 128).T
    cvec[:, CV_BO:CV_BO + 16] = pvec(inp["swa_bo"][0])
    cvec[:, CV_EPSR] = 1e-6
    cvec[:, CV_EPSL] = 1e-5
    cvec[:, CV_SINK:CV_SINK + 32] = np.asarray(inp["swa_sinks"], f32)[0][None, :]

    cb = np.zeros((2, 128, NCB), f32)
    pidx = np.arange(128)
    cb[:, :, CB_ONES:CB_ONES + 128] = 1.0
    cb[:, pidx, CB_ID + pidx] = 1.0
    for m in range(128):
        d = m % 64
        if d < 8:
            cb[:, m + 8, CB_ROT + m] = 1.0
        elif d < 16:
            cb[:, m - 8, CB_ROT + m] = 1.0
    NEG = -30000.0
    i = pidx[:, None]
    j = np.arange(128)[None, :]
    prev_ok = j > i
    cur_ok = j <= i
    mN = np.concatenate([np.where(prev_ok, 0.0, NEG), np.where(cur_ok, 0.0, NEG)], axis=1)
    m0 = np.concatenate([np.full((128, 128), NEG), np.where(cur_ok, 0.0, NEG)], axis=1)
    cb[:, :, CB_MN:CB_MN + 256] = mN
    cb[0, :, CB_M0:CB_M0 + 256] = m0
    cb[1, :, CB_M0:CB_M0 + 256] = mN
    cb[:, :, CB_TRIL:CB_TRIL + 128] = (i <= j).astype(f32)
    cb[:, :, CB_BV:CB_BV + 256] = np.asarray(inp["swa_bv"], f32)[0][None, :]

    x = np.asarray(inp["x"], f32)
    p = np.asarray(inp["p"], f32)
    inv_freq = (np.float32(500000.0) ** (-(np.arange(0, 16, 2, dtype=f32)) / np.float32(16))).astype(f32)
    in_maps = []
    for core in range(N_CORES):
        b, half = core // 2, core % 2
        tok0 = half * NREAL
        xT = np.zeros((D, NTOK), f32)
        pTc = np.zeros((2, 256, NTOK), f32)
        if half == 1:
            xT[:, :] = x[b, tok0 - 128:tok0 + NREAL].T
            pTc[:, :, :] = p[:, b, tok0 - 128:tok0 + NREAL].transpose(0, 2, 1)
        else:
            xT[:, 128:] = x[b, 0:NREAL].T
            pTc[:, :, 128:] = p[:, b, 0:NREAL].transpose(0, 2, 1)
        pos = (np.arange(NTOK) + tok0 - 128).astype(f32)
        ang = pos[:, None] * inv_freq[None, :]
        cs, sn = np.cos(ang).astype(f32), np.sin(ang).astype(f32)
        rope = np.zeros((2, 128, NTOK), f32)
        rope[0] = 1.0
        for pp in range(128):
            d = pp % 64
            if d < 16:
                rope[0, pp] = cs[:, d % 8]
                rope[1, pp] = -sn[:, d % 8] if d < 8 else sn[:, d % 8]
        in_maps.append({"xT": xT, "pT": pTc, "wst": wst, "cvec": cvec, "cbf": cb[half], "rope": rope})
    return in_maps


_CACHE = {}


def kernel(**inputs):
    in_maps = host_prepare(inputs)
    if "nc" not in _CACHE:
        _CACHE["nc"] = build()[0]
    nc = _CACHE["nc"]
    res = run_bass_kernel_spmd(nc, in_maps, core_ids=list(range(N_CORES)))
    out = np.empty((4, 4096, D), np.float32)
    for core in range(N_CORES):
        b, half = core // 2, core % 2
        out[b, half * NREAL:(half + 1) * NREAL, :] = res.results[core]["outT"].T
    return out
```

```python
import contextlib
import numpy as np
import concourse.bass as bass
import concourse.mybir as mybir
from concourse.bass_utils import run_bass_kernel_spmd

F32 = mybir.dt.float32
BF16 = mybir.dt.bfloat16
U8 = mybir.dt.uint8
AF = mybir.ActivationFunctionType
ALU = mybir.AluOpType
AX = mybir.AxisListType

D = 2048
DFF = 5632
NC16 = 16
NFFC = 44
NTOK = 2176
NREAL = 2048
TILES = [(0, 640), (640, 512), (1152, 512), (1664, 512)]
TMAX = 640
NSLOT = 7
SLOTC = 4096
N_CORES = 8

CV_G = 0
CV_BQ = 144
CV_BK = 160
CV_BO = 164
CV_EPSR = 180
CV_EPSL = 181
CV_SINK = 182
NV = 216
CB_ONES = 0
CB_ID = 128
CB_ROT = 256
CB_MN = 384
CB_M0 = 640
CB_TRIL = 896
CB_BV = 1024
NCB = 1280

COMPUTE = ("pe", "act", "dve", "pool")
ALLENG = COMPUTE + ("sp",)


class Buf:
    __slots__ = ("name", "w", "r")

    def __init__(self, name):
        self.name = name
        self.w = None
        self.r = {}


class Op:
    __slots__ = ("fn", "deps", "inc", "dma_sem", "waits", "incval")

    def __init__(self, fn, deps, dma_sem):
        self.fn = fn
        self.deps = deps
        self.inc = False
        self.dma_sem = dma_sem
        self.waits = None
        self.incval = None


class Tracker:
    def __init__(self, nc):
        self.nc = nc
        self.ops = {e: [] for e in ALLENG}
        self.dma_cnt = {}
        self.bar = {e: set() for e in ALLENG}

    def op(self, eng, fn, reads=(), writes=(), dma=None):
        nd = set()
        for b in reads:
            d = b.w
            if d is not None:
                if d[0] == "c" and d[1] == eng and dma is None and eng == "pe":
                    pass
                else:
                    nd.add(d)
        for b in writes:
            d = b.w
            if d is not None and not (d[0] == "c" and d[1] == eng and dma is None):
                nd.add(d)
            for d in b.r.values():
                if not (d[0] == "c" and d[1] == eng and dma is None):
                    nd.add(d)
        if self.bar[eng]:
            nd |= self.bar[eng]
            self.bar[eng] = set()
        idx = len(self.ops[eng])
        self.ops[eng].append(Op(fn, nd, dma))
        if dma is None:
            ev = ("c", eng, idx)
            rkey = eng
        else:
            self.dma_cnt[dma] = self.dma_cnt.get(dma, 0) + 16
            ev = ("d", dma, self.dma_cnt[dma])
            rkey = ("d", dma)
        for b in writes:
            b.w = ev
            b.r = {}
        for b in reads:
            if b.w is not ev:
                b.r[rkey] = ev
        return ev

    def barrier(self, engines=("pe", "act", "dve", "sp")):
        evs = set()
        for e in COMPUTE:
            i = len(self.ops[e]) - 1
            while i >= 0 and self.ops[e][i].dma_sem is not None:
                i -= 1
            if i >= 0:
                evs.add(("c", e, i))
        for e in engines:
            self.bar[e] |= {x for x in evs if x[1] != e}

    def finalize(self):
        for e, lst in self.ops.items():
            seen = {}
            for o in lst:
                w = {}
                for d in o.deps:
                    key = (d[0], d[1])
                    val = d[2]
                    if seen.get(key, -1) >= val:
                        continue
                    if w.get(key, -1) < val:
                        w[key] = val
                for key, val in w.items():
                    seen[key] = val
                    if key[0] == "c":
                        self.ops[key[1]][val].inc = True
                o.waits = w
        self.nincs = {}
        for e, lst in self.ops.items():
            c = 0
            for o in lst:
                if o.inc:
                    c += 1
                    o.incval = c
            self.nincs[e] = c

    def emit(self, final_waits=()):
        nc = self.nc
        self.finalize()
        with contextlib.ExitStack() as st:
            esem = {e: st.enter_context(nc.semaphore("s_" + e)) for e in COMPUTE}
            dsem = {k: st.enter_context(nc.semaphore("d_" + str(k))) for k in self.dma_cnt}
            block = st.enter_context(nc.Block())
            engobj = {"pe": "tensor", "act": "scalar", "dve": "vector", "pool": "gpsimd", "sp": "sync"}

            def run(e, eng):
                for o in self.ops[e]:
                    for key, val in o.waits.items():
                        if key[0] == "c":
                            eng.wait_ge(esem[key[1]], self.ops[key[1]][val].incval)
                        else:
                            eng.wait_ge(dsem[key[1]], val)
                    ins = o.fn(eng)
                    if o.dma_sem is not None:
                        ins.then_inc(dsem[o.dma_sem], 16)
                    elif o.inc:
                        ins.then_inc(esem[e], 1)
                if e == "sp":
                    for k in final_waits:
                        eng.wait_ge(dsem[k], self.dma_cnt[k])

            for e in ALLENG:
                getattr(block, engobj[e])(lambda eng, e=e: run(e, eng))


def pass_units():
    U = []

    def ffn(name, l):
        for g in range(22):
            U.append((name + "_w1", l, 0, D, g * 256, g * 256 + 256))
            U.append((name + "_w3", l, 0, D, g * 256, g * 256 + 256))
        for dc in range(16):
            for h in range(2):
                U.append((name + "_w2", l, h * 2816, h * 2816 + 2816, dc * 128, dc * 128 + 128))

    def ple(l):
        U.append(("ple_w_proj", l, 0, 256, 0, 2048))
        for g in range(8):
            U.append(("ple_w_gate", l, 0, D, g * 256, g * 256 + 256))

    for l in range(2):
        ffn("ffn1", l)
        if l == 0:
            for cg in range(16):
                U.append(("gmlp_w_in", 0, 0, D, 4096 + cg * 256, 4096 + cg * 256 + 256))
            U.append(("lng_b", 0, 0, 128, 0, 4096))
            U.append(("lnb_b", 0, 0, 128, 0, 4096))
            U.append(("wsT", 0, 0, 128, 0, 2048))
            U.append(("bs_b", 0, 0, 128, 0, 4096))
            for cu in range(16):
                U.append(("gmlp_w_in", 0, 0, D, cu * 256, cu * 256 + 256))
            for dc in range(16):
                U.append(("gmlp_w_out", 0, 0, 4096, dc * 128, dc * 128 + 128))
        else:
            for g in range(8):
                U.append(("swa_wq", 0, 0, D, g * 256, g * 256 + 256))
            for g in range(2):
                U.append(("wk_dup", 0, 0, D, g * 256, g * 256 + 256))
            U.append(("swa_wv", 0, 0, D, 0, 256))
            for g in range(8):
                U.append(("swa_wo", 0, 0, D, g * 256, g * 256 + 256))
        ffn("ffn2", l)
        ple(l)
    return U


def unit_cols(u):
    return ((u[3] - u[2]) // 128) * (u[5] - u[4])


def subtiles(lo, hi):
    n = hi - lo
    if n <= 512:
        return [(lo, hi)]
    nb = n // 128
    first = (nb + 1) // 2
    return [(lo, lo + first * 128), (lo + first * 128, hi)]


def build(n_stages=9, debug_raw=False):
    units = pass_units()
    offs = np.cumsum([0] + [unit_cols(u) for u in units])
    passcols = int(offs[-1])

    nc = bass.Bass("TRN2", target_bir_lowering=False)
    xT = nc.dram_tensor("xT", [D, NTOK], F32, kind="ExternalInput").ap()
    pT = nc.dram_tensor("pT", [2, 256, NTOK], F32, kind="ExternalInput").ap()
    wst = nc.dram_tensor("wst", [128, passcols], F32, kind="ExternalInput").ap()
    cvec_d = nc.dram_tensor("cvec", [128, NV], F32, kind="ExternalInput").ap()
    cbf_d = nc.dram_tensor("cbf", [128, NCB], F32, kind="ExternalInput").ap()
    rope_d = nc.dram_tensor("rope", [2, 128, NTOK], F32, kind="ExternalInput").ap()
    outT = nc.dram_tensor("outT", [D, NREAL], F32, kind="ExternalOutput").ap()

    with contextlib.ExitStack() as st:
        T = Tracker(nc)

        def sb(name, shape, dt):
            return st.enter_context(nc.sbuf_tensor(name, shape, dt))

        X = sb("X", [128, NC16, TMAX], F32)
        XN = sb("XN", [128, NC16, TMAX], BF16)
        R = sb("R", [128, 65536], U8)
        SL = [sb("SL%d" % i, [128, SLOTC], BF16) for i in range(NSLOT)]
        PT = sb("PT", [128, 2, TMAX], BF16)
        RS = sb("RS", [128, TMAX], F32)
        TMPA = [sb("TMPA%d" % i, [128, 512], F32) for i in range(2)]
        TMPB = [sb("TMPB%d" % i, [128, 512], F32) for i in range(2)]
        CV = sb("CV", [128, NV], F32)
        NSK = sb("NSK", [128, 32], F32)
        CB = sb("CB", [128, NCB], BF16)
        KC = sb("KC", [128, 4, 128], BF16)
        VC = sb("VC", [128, 256], BF16)
        SM = sb("SM", [128, 64], F32)
        BNS = sb("BNS", [128, 48], F32)
        SMA = [sb("SMA%d" % i, [128, 32], F32) for i in range(3)]
        PS = st.enter_context(nc.psum_tensor("PS", [128, 4096], F32))

        bX = [Buf("X%d" % c) for c in range(NC16)]
        bXN = [Buf("XN%d" % c) for c in range(NC16)]
        bSL = [Buf("SL%d" % i) for i in range(NSLOT)]
        bPT = Buf("PT")
        bRS = Buf("RS")
        bTA = [Buf("TA0"), Buf("TA1")]
        bTB = [Buf("TB0"), Buf("TB1")]
        bCV = Buf("CV")
        bNSK = Buf("NSK")
        bCB = Buf("CB")
        bKC = Buf("KC")
        bVC = Buf("VC")
        bSM = Buf("SM")
        bBNS = Buf("BNS")
        bSMA = [Buf("SMA%d" % i) for i in range(3)]
        bPS = [Buf("PS%d" % i) for i in range(8)]
        bOUT = Buf("OUT")

        H = R[:, 0:NFFC * TMAX * 2].bitcast(BF16).rearrange("p (c t) -> p c t", c=NFFC)
        bH = [Buf("H%d" % c) for c in range(NFFC)]
        PJ = R[:, 0:NC16 * TMAX * 4].bitcast(F32).rearrange("p (c t) -> p c t", c=NC16)
        bPJ = [Buf("PJ%d" % c) for c in range(NC16)]
        VR = R[:, 0:5 * 4096 * 2].bitcast(BF16).rearrange("p (b c) -> p b c", b=5)
        bVR = [Buf("VR%d" % b) for b in range(5)]
        OST = R[:, 0:NC16 * 512 * 4].bitcast(F32).rearrange("p (c t) -> p c t", c=NC16)
        o = 0
        QT = R[:, o:o + 16 * 512 * 2].bitcast(BF16).rearrange("p (c t) -> p c t", c=16); o += 16 * 512 * 2
        KTL = R[:, o:o + 4 * 640 * 2].bitcast(BF16).rearrange("p (c t) -> p c t", c=4); o += 4 * 640 * 2
        KTH = R[:, o:o + 4 * 640 * 2].bitcast(BF16).rearrange("p (c t) -> p c t", c=4); o += 4 * 640 * 2
        VT = R[:, o:o + 5 * 256 * 2].bitcast(BF16).rearrange("p (b c) -> p b c", b=5); o += 5 * 256 * 2
        EE = [R[:, o + i * 4096:o + (i + 1) * 4096].bitcast(F32).rearrange("p (h s) -> p h s", h=4) for i in range(3)]; o += 12288
        PP = [R[:, o + i * 2048:o + (i + 1) * 2048].bitcast(BF16).rearrange("p (h s) -> p h s", h=4) for i in range(3)]; o += 6144
        PTS = [R[:, o + i * 2048:o + (i + 1) * 2048].bitcast(BF16).rearrange("p (h s) -> p h s", h=8) for i in range(3)]; o += 6144
        COS = R[:, o:o + 640 * 4].bitcast(F32); o += 2560
        SIN = R[:, o:o + 640 * 4].bitcast(F32); o += 2560
        QTMP = [R[:, o + i * 1024:o + (i + 1) * 1024].bitcast(BF16) for i in range(2)]; o += 2048
        assert o <= 65536, o
        bQT = [Buf("QT%d" % c) for c in range(16)]
        bKTL = [Buf("KTL%d" % c) for c in range(4)]
        bKTH = [Buf("KTH%d" % c) for c in range(4)]
        bVT = [Buf("VT%d" % b) for b in range(5)]
        bEE = [Buf("EE%d" % i) for i in range(3)]
        bPP = [Buf("PP%d" % i) for i in range(3)]
        bPTS = [Buf("PTS%d" % i) for i in range(3)]
        bCOS = Buf("COS")
        bSIN = Buf("SIN")
        bQTMP = [Buf("QTMP0"), Buf("QTMP1")]

        state = {"bank": 0, "slot": 0, "unit": 0, "tmp": 0}

        def bank():
            b = state["bank"]
            state["bank"] = (b + 1) % 8
            return b

        def bankpair():
            b = state["bank"]
            if b % 2:
                b = (b + 1) % 8
            state["bank"] = (b + 2) % 8
            return b, b + 1

        def psv(b, n):
            return PS[:, b * 512:b * 512 + n]

        def tmpi():
            i = state["tmp"]
            state["tmp"] = 1 - i
            return i

        def wunit(expect_key):
            ui = state["unit"] % len(units)
            u = units[ui]
            assert u[0] == expect_key, (u, expect_key)
            state["unit"] += 1
            s = state["slot"]
            state["slot"] = (s + 1) % NSLOT
            ncols = unit_cols(u)
            off = int(offs[ui])
            T.op("pool", lambda e: e.dma_start(out=SL[s][:, 0:ncols], in_=wst[:, off:off + ncols]),
                 writes=[bSL[s]], dma="w%d" % s)
            kc = (u[3] - u[2]) // 128
            return SL[s][:, 0:ncols].rearrange("p (k n) -> p k n", k=kc), bSL[s]

        def mm(out, lhsT, rhs, start, stop, reads, bk):
            T.op("pe", lambda e: e.matmul(out, lhsT=lhsT, rhs=rhs, start=start, stop=stop),
                 reads=reads, writes=[bPS[bk]])

        def mm_kc_outer(groups):
            nk = len(groups[0][2])
            for kc in range(nk):
                for (out, bk, lst) in groups:
                    lhsT, rhs, reads = lst[kc]
                    mm(out, lhsT, rhs, kc == 0, kc == nk - 1, reads, bk)

        T.op("sp", lambda e: e.dma_start(out=CV[:], in_=cvec_d), writes=[bCV], dma="cv")
        T.op("pool", lambda e: e.dma_start(out=CB[:], in_=cbf_d), writes=[bCB], dma="cb")
        T.op("dve", lambda e: e.tensor_scalar(out=NSK[:], in0=CV[:, CV_SINK:CV_SINK + 32], scalar1=-1.0, scalar2=None,
                                              op0=ALU.mult), reads=[bCV], writes=[bNSK])
        ONES = CB[:, CB_ONES:CB_ONES + 128]
        IDENT = CB[:, CB_ID:CB_ID + 128]
        ROT = CB[:, CB_ROT:CB_ROT + 128]

        def norm(gi, lo, hi, stats_only=False):
            for c4 in range(4):
                cs = slice(4 * c4, 4 * c4 + 4)
                T.op("act", lambda e, cs=cs: e.activation(out=XN[:, cs, lo:hi], in_=X[:, cs, lo:hi], func=AF.Square),
                     reads=bX[cs], writes=bXN[cs])
            for (a, b) in subtiles(lo, hi):
                bk = bank()
                for c in range(NC16):
                    mm(psv(bk, b - a), ONES, XN[:, c, a:b], c == 0, c == NC16 - 1, [bCB, bXN[c]], bk)
                T.op("act", lambda e, a=a, b=b, bk=bk: e.activation(out=RS[:, a:b], in_=psv(bk, b - a), func=AF.Sqrt,
                                                                    bias=CV[:, CV_EPSR:CV_EPSR + 1], scale=1.0 / D),
                     reads=[bCV], writes=[bRS, bPS[bk]])
            T.op("dve", lambda e: e.reciprocal(out=RS[:, lo:hi], in_=RS[:, lo:hi]), reads=[bRS], writes=[bRS])
            if stats_only:
                return
            for c in range(NC16):
                T.op("dve", lambda e, c=c: e.scalar_tensor_tensor(out=XN[:, c, lo:hi], in0=X[:, c, lo:hi],
                                                                  scalar=CV[:, CV_G + gi * 16 + c:CV_G + gi * 16 + c + 1],
                                                                  in1=RS[:, lo:hi], op0=ALU.mult, op1=ALU.mult),
                     reads=[bX[c], bRS, bCV], writes=[bXN[c]])

        def ffn(name, gi, lo, hi):
            norm(gi, lo, hi)
            subs = subtiles(lo, hi)
            for g in range(22):
                u1, b1 = wunit(name + "_w1")
                u3, b3 = wunit(name + "_w3")
                pre = {}
                if g == 0:
                    a, b = subs[0]
                    n = b - a
                    grp = []
                    for j in range(2):
                        bA = bank()
                        bB = bank()
                        pre[j] = (bA, bB)
                        grp.append((psv(bA, n), bA, [(u1[:, kc, j * 128:(j + 1) * 128], XN[:, kc, a:b], [b1, bXN[kc]]) for kc in range(NC16)]))
                        grp.append((psv(bB, n), bB, [(u3[:, kc, j * 128:(j + 1) * 128], XN[:, kc, a:b], [b3, bXN[kc]]) for kc in range(NC16)]))
                    mm_kc_outer(grp)
                for j in range(2):
                    ffc = g * 2 + j
                    for si, (a, b) in enumerate(subs):
                        n = b - a
                        if g == 0 and si == 0:
                            bA, bB = pre[j]
                        else:
                            bA = bank()
                            bB = bank()
                            for kc in range(NC16):
                                mm(psv(bA, n), u1[:, kc, j * 128:(j + 1) * 128], XN[:, kc, a:b], kc == 0, kc == 15, [b1, bXN[kc]], bA)
                            for kc in range(NC16):
                                mm(psv(bB, n), u3[:, kc, j * 128:(j + 1) * 128], XN[:, kc, a:b], kc == 0, kc == 15, [b3, bXN[kc]], bB)
                        i = tmpi()
                        T.op("act", lambda e, i=i, bA=bA, n=n: e.activation(out=TMPA[i][:, 0:n], in_=psv(bA, n), func=AF.Silu),
                             writes=[bTA[i], bPS[bA]])
                        T.op("dve", lambda e, i=i, bB=bB, n=n, ffc=ffc, a=a, b=b: e.tensor_tensor(
                            out=H[:, ffc, a:b], in0=TMPA[i][:, 0:n], in1=psv(bB, n), op=ALU.mult),
                            reads=[bTA[i]], writes=[bH[ffc], bPS[bB]])
            for dc in range(NC16):
                ua, ba = wunit(name + "_w2")
                ub, bb = wunit(name + "_w2")
                for (a, b) in subs:
                    n = b - a
                    bY = bank()
                    for ffc in range(NFFC):
                        uu, bu = (ua, ba) if ffc < 22 else (ub, bb)
                        mm(psv(bY, n), uu[:, ffc % 22, :], H[:, ffc, a:b], ffc == 0, ffc == NFFC - 1, [bu, bH[ffc]], bY)
                    T.op("dve", lambda e, dc=dc, a=a, b=b, bY=bY, n=n: e.scalar_tensor_tensor(
                        out=X[:, dc, a:b], in0=psv(bY, n), scalar=0.5, in1=X[:, dc, a:b], op0=ALU.mult, op1=ALU.add),
                        reads=[bX[dc]], writes=[bX[dc], bPS[bY]])

        def ple(l, gi, lo, hi, t0):
            subs = subtiles(lo, hi)
            for kc in range(2):
                T.op("pool", lambda e, kc=kc: e.dma_start(out=PT[:, kc, lo:hi], in_=pT[l, kc * 128:(kc + 1) * 128, t0 + lo:t0 + hi]),
                     writes=[bPT], dma="pt")
            up, bp = wunit("ple_w_proj")
            for dc in range(NC16):
                for (a, b) in subs:
                    n = b - a
                    bk = bank()
                    for kc in range(2):
                        mm(psv(bk, n), up[:, kc, dc * 128:(dc + 1) * 128], PT[:, kc, a:b], kc == 0, kc == 1, [bp, bPT], bk)
                    T.op("act", lambda e, dc=dc, a=a, b=b, bk=bk, n=n: e.activation(out=PJ[:, dc, a:b], in_=psv(bk, n), func=AF.Copy),
                         writes=[bPJ[dc], bPS[bk]])
            norm(gi, lo, hi)
            for g in range(8):
                ug, bg = wunit("ple_w_gate")
                pre = {}
                if g == 0:
                    grp = []
                    for j in range(2):
                        for (a, b) in subs:
                            bk = bank()
                            pre[(j, a)] = bk
                            grp.append((psv(bk, b - a), bk, [(ug[:, kc, j * 128:(j + 1) * 128], XN[:, kc, a:b], [bg, bXN[kc]]) for kc in range(NC16)]))
                    mm_kc_outer(grp)
                for j in range(2):
                    dc = g * 2 + j
                    for (a, b) in subs:
                        n = b - a
                        if g == 0:
                            bk = pre[(j, a)]
                        else:
                            bk = bank()
                            for kc in range(NC16):
                                mm(psv(bk, n), ug[:, kc, j * 128:(j + 1) * 128], XN[:, kc, a:b], kc == 0, kc == 15, [bg, bXN[kc]], bk)
                        i = tmpi()
                        T.op("act", lambda e, i=i, bk=bk, n=n: e.activation(out=TMPA[i][:, 0:n], in_=psv(bk, n), func=AF.Sigmoid),
                             writes=[bTA[i], bPS[bk]])
                        T.op("dve", lambda e, i=i, dc=dc, a=a, b=b, n=n: e.tensor_tensor(
                            out=TMPB[i][:, 0:n], in0=TMPA[i][:, 0:n], in1=PJ[:, dc, a:b], op=ALU.mult),
                            reads=[bTA[i], bPJ[dc]], writes=[bTB[i]])
                        T.op("dve", lambda e, i=i, dc=dc, a=a, b=b, n=n: e.tensor_tensor(
                            out=X[:, dc, a:b], in0=X[:, dc, a:b], in1=TMPB[i][:, 0:n], op=ALU.add),
                            reads=[bTB[i], bX[dc]], writes=[bX[dc]])

        def gmlp(gi, lo, hi):
            norm(gi, lo, hi)
            subs = subtiles(lo, hi)
            blocks = list(range(lo // 128, hi // 128))
            for cg in range(16):
                uv, bv = wunit("gmlp_w_in")
                pre = {}
                if cg == 0:
                    grp = []
                    for tb in blocks:
                        bk = bank()
                        pre[tb] = bk
                        grp.append((psv(bk, 256), bk, [(XN[:, kc, tb * 128:(tb + 1) * 128], uv[:, kc, :], [bv, bXN[kc]]) for kc in range(NC16)]))
                    mm_kc_outer(grp)
                for tb in blocks:
                    if cg == 0:
                        bk = pre[tb]
                    else:
                        bk = bank()
                        for kc in range(NC16):
                            mm(psv(bk, 256), XN[:, kc, tb * 128:(tb + 1) * 128], uv[:, kc, :], kc == 0, kc == 15, [bv, bXN[kc]], bk)
                    T.op("act", lambda e, tb=tb, cg=cg, bk=bk: e.activation(out=VR[:, tb, cg * 256:(cg + 1) * 256], in_=psv(bk, 256), func=AF.Gelu),
                         writes=[bVR[tb], bPS[bk]])
            ug, bg = wunit("lng_b")
            ub, bb = wunit("lnb_b")
            uw, bw = wunit("wsT")
            us, bs = wunit("bs_b")
            for g in range(16):
                T.op("dve", lambda e, g=g: e.tensor_tensor(out=uw[:, 0, g * 128:(g + 1) * 128], in0=uw[:, 0, g * 128:(g + 1) * 128],
                                                           in1=CB[:, CB_TRIL:CB_TRIL + 128], op=ALU.mult),
                     reads=[bCB, bw], writes=[bw])
            for tb in blocks:
                for k in range(8):
                    T.op("dve", lambda e, tb=tb, k=k: e.bn_stats(out=BNS[:, k * 6:(k + 1) * 6], in_=VR[:, tb, k * 512:(k + 1) * 512]),
                         reads=[bVR[tb]], writes=[bBNS])
                T.op("dve", lambda e: e.bn_aggr(out=SM[:, 0:2], in_=BNS[:, 0:48]), reads=[bBNS], writes=[bSM])
                T.op("act", lambda e: e.activation(out=SM[:, 2:3], in_=SM[:, 1:2], func=AF.Sqrt, bias=CV[:, CV_EPSL:CV_EPSL + 1], scale=1.0),
                     reads=[bSM, bCV], writes=[bSM])
                T.op("dve", lambda e: e.reciprocal(out=SM[:, 3:4], in_=SM[:, 2:3]), reads=[bSM], writes=[bSM])
                T.op("dve", lambda e, tb=tb: e.tensor_scalar(out=VR[:, tb, :], in0=VR[:, tb, :], scalar1=SM[:, 0:1], scalar2=SM[:, 3:4],
                                                             op0=ALU.subtract, op1=ALU.mult), reads=[bSM, bVR[tb]], writes=[bVR[tb]])
                T.op("dve", lambda e, tb=tb: e.tensor_tensor(out=VR[:, tb, :], in0=VR[:, tb, :], in1=ug[:, 0, :], op=ALU.mult),
                     reads=[bg, bVR[tb]], writes=[bVR[tb]])
                T.op("dve", lambda e, tb=tb: e.tensor_tensor(out=VR[:, tb, :], in0=VR[:, tb, :], in1=ub[:, 0, :], op=ALU.add),
                     reads=[bb, bVR[tb]], writes=[bVR[tb]])
                for c4 in range(8):
                    bk = bank()
                    for cc in range(4):
                        c = c4 * 4 + cc
                        mm(PS[:, bk * 512 + cc * 128:bk * 512 + (cc + 1) * 128], VR[:, tb, c * 128:(c + 1) * 128],
                           uw[:, 0, (c // 2) * 128:(c // 2 + 1) * 128], True, True, [bVR[tb], bw], bk)
                    T.op("dve", lambda e, tb=tb, c4=c4, bk=bk: e.tensor_tensor(out=VR[:, tb, c4 * 512:(c4 + 1) * 512], in0=psv(bk, 512),
                                                                               in1=us[:, 0, c4 * 512:(c4 + 1) * 512], op=ALU.add),
                         reads=[bs], writes=[bVR[tb], bPS[bk]])
            for cu in range(16):
                uu, bu = wunit("gmlp_w_in")
                for j in range(2):
                    c = cu * 2 + j
                    for (a, b) in subs:
                        n = b - a
                        bk = bank()
                        for kc in range(NC16):
                            mm(psv(bk, n), uu[:, kc, j * 128:(j + 1) * 128], XN[:, kc, a:b], kc == 0, kc == 15, [bu, bXN[kc]], bk)
                        i = tmpi()
                        T.op("act", lambda e, i=i, bk=bk, n=n: e.activation(out=TMPA[i][:, 0:n], in_=psv(bk, n), func=AF.Gelu),
                             writes=[bTA[i], bPS[bk]])
                        sv = VR[:, a // 128:b // 128, c * 128:(c + 1) * 128]
                        T.op("dve", lambda e, i=i, n=n, sv=sv: e.tensor_tensor(
                            out=sv, in0=TMPA[i][:, 0:n].rearrange("p (b t) -> p b t", t=128), in1=sv, op=ALU.mult),
                            reads=[bTA[i]] + bVR[a // 128:b // 128], writes=bVR[a // 128:b // 128])
            for dc in range(NC16):
                uo, bo = wunit("gmlp_w_out")
                for (a, b) in subs:
                    n = b - a
                    bk = bank()
                    for c in range(32):
                        mm(psv(bk, n).rearrange("p (b t) -> p b t", t=128), uo[:, c, :], VR[:, a // 128:b // 128, c * 128:(c + 1) * 128],
                           c == 0, c == 31, [bo] + bVR[a // 128:b // 128], bk)
                    T.op("dve", lambda e, dc=dc, a=a, b=b, bk=bk, n=n: e.tensor_tensor(
                        out=X[:, dc, a:b], in0=psv(bk, n), in1=X[:, dc, a:b], op=ALU.add),
                        reads=[bX[dc]], writes=[bX[dc], bPS[bk]])

        def rope_proj(uw_, bw_, j, bias_col, dst, bdst, dlo, src_cols, tab_cols, dst2=None, bdst2=None):
            a, b = src_cols
            n = b - a
            ta, tb_ = tab_cols
            bk = bank()
            for kc in range(NC16):
                mm(psv(bk, n), uw_[:, kc, j * 128:(j + 1) * 128], XN[:, kc, a:b], kc == 0, kc == 15, [bw_, bXN[kc]], bk)
            i = tmpi()
            T.op("act", lambda e: e.activation(out=QTMP[i][:, 0:n], in_=psv(bk, n), func=AF.Identity,
                                               bias=CV[:, bias_col:bias_col + 1], scale=1.0),
                 reads=[bCV], writes=[bQTMP[i], bPS[bk]])
            bk2 = bank()
            mm(psv(bk2, n), ROT, QTMP[i][:, 0:n], True, True, [bCB, bQTMP[i]], bk2)
            T.op("dve", lambda e: e.tensor_tensor(out=TMPA[i][:, 0:n], in0=QTMP[i][:, 0:n], in1=COS[:, ta:tb_], op=ALU.mult),
                 reads=[bQTMP[i], bCOS], writes=[bTA[i]])
            T.op("dve", lambda e: e.tensor_tensor(out=TMPB[i][:, 0:n], in0=psv(bk2, n), in1=SIN[:, ta:tb_], op=ALU.mult),
                 reads=[bSIN], writes=[bTB[i], bPS[bk2]])
            if dst2 is None:
                T.op("dve", lambda e: e.tensor_tensor(out=dst[:, dlo:dlo + n], in0=TMPA[i][:, 0:n], in1=TMPB[i][:, 0:n], op=ALU.add),
                     reads=[bTA[i], bTB[i]], writes=[bdst])
            else:
                T.op("dve", lambda e: e.tensor_tensor(out=dst[0:64, dlo:dlo + n], in0=TMPA[i][0:64, 0:n], in1=TMPB[i][0:64, 0:n], op=ALU.add),
                     reads=[bTA[i], bTB[i]], writes=[bdst])
                T.op("dve", lambda e: e.tensor_tensor(out=dst2[64:128, dlo:dlo + n], in0=TMPA[i][64:128, 0:n], in1=TMPB[i][64:128, 0:n], op=ALU.add),
                     reads=[bTA[i], bTB[i]], writes=[bdst2])

        def swa(gi, T_, q0, t0, ti):
            lo = 0 if ti == 0 else 0
            norm(gi, lo, T_)
            nq = T_ - q0
            nkb = 5
            kcomp0 = 0 if ti == 0 else 1
            T.op("sp", lambda e: e.dma_start(out=COS[:, 0:T_], in_=rope_d[0, :, t0:t0 + T_]), writes=[bCOS], dma="cos")
            T.op("sp", lambda e: e.dma_start(out=SIN[:, 0:T_], in_=rope_d[1, :, t0:t0 + T_]), writes=[bSIN], dma="sin")
            for g in range(8):
                uq, bq_ = wunit("swa_wq")
                for j in range(2):
                    c = g * 2 + j
                    rope_proj(uq, bq_, j, CV_BQ + c, QT[:, c, :], bQT[c], 0, (q0, T_), (q0, T_))
            T.op("dve", lambda e: e.memset(KTL[64:128, :, :], 0.0), writes=bKTL)
            T.op("dve", lambda e: e.memset(KTH[0:64, :, :], 0.0), writes=bKTH)
            if ti > 0:
                T.op("dve", lambda e: e.tensor_copy(out=KTL[0:64, :, 0:128], in_=KC[0:64]), reads=[bKC], writes=bKTL)
                T.op("dve", lambda e: e.tensor_copy(out=KTH[64:128, :, 0:128], in_=KC[64:128]), reads=[bKC], writes=bKTH)
                T.op("dve", lambda e: e.tensor_copy(out=VT[:, 0, :], in_=VC[:]), reads=[bVC], writes=[bVT[0]])
            for g2 in range(2):
                uk, bk_ = wunit("wk_dup")
                for j in range(2):
                    g = g2 * 2 + j
                    for (a, b) in subtiles(0, T_):
                        rope_proj(uk, bk_, j, CV_BK + g, KTL[:, g, :], bKTL[g], kcomp0 * 128 + a, (a, b), (a, b),
                                  dst2=KTH[:, g, :], bdst2=bKTH[g])
            uv, bv_ = wunit("swa_wv")
            for tb in range(T_ // 128):
                kb = kcomp0 + tb
                bk = bank()
                for kc in range(NC16):
                    mm(psv(bk, 256), XN[:, kc, tb * 128:(tb + 1) * 128], uv[:, kc, :], kc == 0, kc == 15, [bv_, bXN[kc]], bk)
                T.op("dve", lambda e, kb=kb, bk=bk: e.tensor_tensor(out=VT[:, kb, :], in0=psv(bk, 256), in1=CB[:, CB_BV:CB_BV + 256], op=ALU.add),
                     reads=[bCB], writes=[bVT[kb], bPS[bk]])
            T.barrier(("pe", "act", "dve"))
            units_ = [(jq, g, hh) for jq in range(4) for g in range(4) for hh in range(2)]
            NU_ = len(units_)
            BT, BO = 6, 7

            def stA(u):
                jq, g, hh = units_[u]
                w = u % 3
                mcol = CB_M0 if (ti == 0 and jq == 0) else CB_MN
                for i4 in range(4):
                    h = g * 8 + hh * 4 + i4
                    c, half = h // 2, h % 2
                    bk = 2 * w + (i4 // 2)
                    outp = PS[:, bk * 512 + (i4 % 2) * 256:bk * 512 + (i4 % 2) * 256 + 256]
                    kt_, bkt_ = (KTL, bKTL) if half == 0 else (KTH, bKTH)
                    mm(outp, IDENT, CB[:, mcol:mcol + 256], True, False, [bCB], bk)
                    mm(outp, QT[:, c, jq * 128:(jq + 1) * 128], kt_[:, g, jq * 128:jq * 128 + 256], False, True,
                       [bQT[c], bkt_[g]], bk)

            def stB1(u):
                jq, g, hh = units_[u]
                w = u % 3
                b0, b1 = 2 * w, 2 * w + 1
                sc = PS[:, b0 * 512:b0 * 512 + 1024].rearrange("p (h s) -> p h s", h=4)
                S_, bS_ = SMA[w], bSMA[w]
                hs = g * 8 + hh * 4
                T.op("dve", lambda e: e.tensor_reduce(out=S_[:, 0:4], in_=sc, axis=AX.X, op=ALU.max),
                     writes=[bS_, bPS[b0], bPS[b1]])
                T.op("dve", lambda e: e.scalar_tensor_tensor(out=S_[:, 4:8], in0=S_[:, 0:4], scalar=-0.125,
                                                             in1=NSK[:, hs:hs + 4], op0=ALU.mult, op1=ALU.min),
                     reads=[bS_, bNSK], writes=[bS_])
                T.op("dve", lambda e: e.tensor_tensor(out=S_[:, 12:16], in0=S_[:, 4:8],
                                                      in1=CV[:, CV_SINK + hs:CV_SINK + hs + 4], op=ALU.add),
                     reads=[bS_, bCV], writes=[bS_])
                for i4 in range(4):
                    T.op("act", lambda e, i4=i4: e.activation(
                        out=EE[w][:, i4, :], in_=sc[:, i4, :], func=AF.Exp, bias=S_[:, 4 + i4:5 + i4], scale=0.125,
                        accum_out=S_[:, 8 + i4:9 + i4]),
                        reads=[bS_], writes=[bEE[w], bS_, bPS[b0], bPS[b1]])
                T.op("act", lambda e: e.activation(out=S_[:, 16:20], in_=S_[:, 12:16], func=AF.Exp),
                     reads=[bS_], writes=[bS_])

            def stB2(u):
                w = u % 3
                S_, bS_ = SMA[w], bSMA[w]
                T.op("dve", lambda e: e.tensor_tensor(out=S_[:, 20:24], in0=S_[:, 16:20], in1=S_[:, 8:12], op=ALU.add),
                     reads=[bS_], writes=[bS_])
                T.op("dve", lambda e: e.reciprocal(out=S_[:, 24:28], in_=S_[:, 20:24]), reads=[bS_], writes=[bS_])
                for i4 in range(4):
                    T.op("dve", lambda e, i4=i4: e.tensor_scalar(
                        out=PP[w][:, i4, :], in0=EE[w][:, i4, :], scalar1=S_[:, 24 + i4:25 + i4], scalar2=None, op0=ALU.mult),
                        reads=[bEE[w], bS_], writes=[bPP[w]])

            def stC(u):
                w = u % 3
                ptp = PS[:, BT * 512:BT * 512 + 512].bitcast(BF16).rearrange("p (h s) -> p h s", h=8)
                for i4 in range(4):
                    for kb in range(2):
                        T.op("pe", lambda e, i4=i4, kb=kb: e.transpose(
                            out=ptp[:, i4 * 2 + kb, :], in_=PP[w][:, i4, kb * 128:(kb + 1) * 128], identity=IDENT),
                            reads=[bPP[w], bCB], writes=[bPS[BT]])
                T.op("act", lambda e: e.activation(out=PTS[w][:], in_=ptp, func=AF.Copy), writes=[bPTS[w], bPS[BT]])

            def stD(u):
                jq, g, hh = units_[u]
                w = u % 3
                hs = g * 8 + hh * 4
                for i4 in range(4):
                    half = (hs + i4) % 2
                    for kb in range(2):
                        mm(PS[half * 64:half * 64 + 64, BO * 512 + (i4 // 2) * 128:BO * 512 + (i4 // 2) * 128 + 128],
                           VT[:, jq + kb, g * 64:(g + 1) * 64], PTS[w][:, i4 * 2 + kb, :], kb == 0, kb == 1,
                           [bVT[jq + kb], bPTS[w]], BO)

            def stE(u):
                jq, g, hh = units_[u]
                c0 = (g * 8 + hh * 4) // 2
                T.op("dve", lambda e: e.tensor_copy(
                    out=XN[:, c0:c0 + 2, q0 + jq * 128:q0 + (jq + 1) * 128],
                    in_=PS[:, BO * 512:BO * 512 + 256].rearrange("p (c t) -> p c t", c=2)),
                    writes=[bXN[c0], bXN[c0 + 1], bPS[BO]])

            stA(0)
            stA(1)
            stB1(0)
            for u in range(NU_):
                if u + 2 < NU_:
                    stA(u + 2)
                if u + 1 < NU_:
                    stB1(u + 1)
                stB2(u)
                if u >= 1:
                    stE(u - 1)
                stC(u)
                stD(u)
            stE(NU_ - 1)
            T.op("dve", lambda e: e.tensor_copy(out=KC[0:64], in_=KTL[0:64, :, 512:640]), reads=bKTL, writes=[bKC])
            T.op("dve", lambda e: e.tensor_copy(out=KC[64:128], in_=KTH[64:128, :, 512:640]), reads=bKTH, writes=[bKC])
            T.op("dve", lambda e: e.tensor_copy(out=VC[:], in_=VT[:, 4, :]), reads=[bVT[4]], writes=[bVC])
            subs = subtiles(q0, T_)
            for g in range(8):
                uo, bo2 = wunit("swa_wo")
                for j in range(2):
                    dc = g * 2 + j
                    for (a, b) in subs:
                        n = b - a
                        bk = bank()
                        for kc in range(NC16):
                            mm(psv(bk, n), uo[:, kc, j * 128:(j + 1) * 128], XN[:, kc, a:b], kc == 0, kc == 15, [bo2, bXN[kc]], bk)
                        T.op("dve", lambda e, dc=dc, a=a, b=b, bk=bk, n=n: e.scalar_tensor_tensor(
                            out=X[:, dc, a:b], in0=psv(bk, n), scalar=CV[:, CV_BO + dc:CV_BO + dc + 1], in1=X[:, dc, a:b],
                            op0=ALU.add, op1=ALU.add), reads=[bX[dc], bCV], writes=[bX[dc], bPS[bk]])

        def do_tile(ti, t0, T_, out_off):
            q0 = 128 if ti == 0 else 0
            T.barrier(("pe", "act", "dve"))
            for c4 in range(4):
                cs = slice(4 * c4, 4 * c4 + 4)
                T.op("sp", lambda e, cs=cs, c4=c4: e.dma_start(
                    out=X[:, cs, 0:T_], in_=xT[c4 * 512:(c4 + 1) * 512, t0:t0 + T_].rearrange("(c p) t -> p c t", p=128)),
                    writes=bX[cs], dma="xin%d" % c4)
            stage = 0

            def go():
                nonlocal stage
                stage += 1
                return stage <= n_stages

            if go():
                T.barrier(); ffn("ffn1", 0, 0, T_)
            if go():
                T.barrier(); gmlp(1, 0, T_)
            if go():
                T.barrier(); ffn("ffn2", 2, 0, T_)
            if go():
                T.barrier(); ple(0, 3, 0, T_, t0)
            if go():
                T.barrier(); ffn("ffn1", 4, 0, T_)
            if go():
                T.barrier(); swa(5, T_, q0, t0, ti)
            if go():
                T.barrier(); ffn("ffn2", 6, q0, T_)
            if go():
                T.barrier(); ple(1, 7, q0, T_, t0)
            state["unit"] = (ti + 1) * len(units)
            T.barrier()
            nq = T_ - q0
            if go() and not debug_raw:
                norm(8, q0, T_, stats_only=True)
                for c in range(NC16):
                    T.op("dve", lambda e, c=c: e.scalar_tensor_tensor(out=OST[:, c, 0:nq], in0=X[:, c, q0:T_],
                                                                      scalar=CV[:, CV_G + 8 * 16 + c:CV_G + 8 * 16 + c + 1],
                                                                      in1=RS[:, q0:T_], op0=ALU.mult, op1=ALU.mult),
                         reads=[bX[c], bRS, bCV], writes=[bOUT])
            else:
                for c in range(NC16):
                    T.op("dve", lambda e, c=c: e.tensor_copy(out=OST[:, c, 0:nq], in_=X[:, c, q0:T_]), reads=[bX[c]], writes=[bOUT])
            for c4 in range(4):
                cs = slice(4 * c4, 4 * c4 + 4)
                T.op("sp", lambda e, cs=cs, c4=c4, oo=out_off: e.dma_start(
                    out=outT[c4 * 512:(c4 + 1) * 512, oo:oo + nq].rearrange("(c p) t -> p c t", p=128), in_=OST[:, cs, 0:nq]),
                    reads=[bOUT] + bH, dma="out")
        oo_ = 0
        for ti_, (t0_, TT_) in enumerate(TILES):
            do_tile(ti_, t0_, TT_, oo_)
            oo_ += 512
        T.emit(final_waits=["out"])
    return nc, T


def _kn(W, r0, r1, c0, c1):
    kc = (r1 - r0) // 128
    return np.ascontiguousarray(W[r0:r1, c0:c1].reshape(kc, 128, c1 - c0).transpose(1, 0, 2)).reshape(128, kc * (c1 - c0))


def host_prepare(inp):
    f32 = np.float32
    mats = {}
    for k in ("ffn1_w1", "ffn1_w3", "ffn1_w2", "ffn2_w1", "ffn2_w3", "ffn2_w2", "ple_w_gate", "ple_w_proj",
              "gmlp_w_in", "gmlp_w_out", "swa_wq", "swa_wv", "swa_wo"):
        mats[k] = np.asarray(inp[k], dtype=f32)
    wk = np.asarray(inp["swa_wk"], dtype=f32)[0]
    wk_dup = np.concatenate([wk[:, (g // 2) * 64:(g // 2) * 64 + 64] for g in range(8)], axis=1)
    mats["wk_dup"] = wk_dup[None]
    mats["lng_b"] = np.broadcast_to(np.asarray(inp["gmlp_ln_g"], f32)[0][None, :], (128, 4096))[None]
    mats["lnb_b"] = np.broadcast_to(np.asarray(inp["gmlp_ln_b"], f32)[0][None, :], (128, 4096))[None]
    ws = np.asarray(inp["gmlp_w_s"], f32)[0]
    mats["wsT"] = np.ascontiguousarray(ws.transpose(2, 0, 1)).reshape(128, 16 * 128)[None]
    bs = np.asarray(inp["gmlp_b_s"], f32)[0]
    bs32 = np.repeat(bs, 2, axis=0).reshape(1, 32 * 128)
    mats["bs_b"] = np.broadcast_to(bs32, (128, 4096))[None]
    units = pass_units()
    total = sum(unit_cols(u) for u in units)
    wst = np.empty((128, total), dtype=f32)
    off = 0
    for u in units:
        n = unit_cols(u)
        wst[:, off:off + n] = _kn(mats[u[0]][u[1]], u[2], u[3], u[4], u[5])
        off += n

    def pvec(v):
        return np.asarray(v, f32).reshape(16, 128).T

    cvec = np.zeros((128, NV), f32)
    gl = [inp["ffn1_norm"][0], inp["mix_norm"][0], inp["ffn2_norm"][0], inp["ple_norm"][0],
          inp["ffn1_norm"][1], inp["mix_norm"][1], inp["ffn2_norm"][1], inp["ple_norm"][1], inp["final_norm"]]
    for i, g in enumerate(gl):
        cvec[:, CV_G + i * 16:CV_G + (i + 1) * 16] = pvec(g)
    cvec[:, CV_BQ:CV_BQ + 16] = pvec(inp["swa_bq"][0])
    bk = np.asarray(inp["swa_bk"], f32)[0]
    bk_dup = np.concatenate([bk[(g // 2) * 64:(g // 2) * 64 + 64] for g in range(8)])
    cvec[:, CV_BK:CV_BK + 4] = bk_dup.reshape(4, 128).T
    cvec[:, CV_BO:CV_BO + 16] = pvec(inp["swa_bo"][0])
    cvec[:, CV_EPSR] = 1e-6
    cvec[:, CV_EPSL] = 1e-5
    cvec[:, CV_SINK:CV_SINK + 32] = np.asarray(inp["swa_sinks"], f32)[0][None, :]

    cb = np.zeros((2, 128, NCB), f32)
    pidx = np.arange(128)
    cb[:, :, CB_ONES:CB_ONES + 128] = 1.0
    cb[:, pidx, CB_ID + pidx] = 1.0
    for m in range(128):
        d = m % 64
        if d < 8:
            cb[:, m + 8, CB_ROT + m] = 1.0
        elif d < 16:
            cb[:, m - 8, CB_ROT + m] = 1.0
    NEG = -30000.0
    i = pidx[:, None]
    j = np.arange(128)[None, :]
    prev_ok = j > i
    cur_ok = j <= i
    mN = np.concatenate([np.where(prev_ok, 0.0, NEG), np.where(cur_ok, 0.0, NEG)], axis=1)
    m0 = np.concatenate([np.full((128, 128), NEG), np.where(cur_ok, 0.0, NEG)], axis=1)
    cb[:, :, CB_MN:CB_MN + 256] = mN
    cb[0, :, CB_M0:CB_M0 + 256] = m0
    cb[1, :, CB_M0:CB_M0 + 256] = mN
    cb[:, :, CB_TRIL:CB_TRIL + 128] = (i <= j).astype(f32)
    cb[:, :, CB_BV:CB_BV + 256] = np.asarray(inp["swa_bv"], f32)[0][None, :]

    x = np.asarray(inp["x"], f32)
    p = np.asarray(inp["p"], f32)
    inv_freq = (np.float32(500000.0) ** (-(np.arange(0, 16, 2, dtype=f32)) / np.float32(16))).astype(f32)
    in_maps = []
    for core in range(N_CORES):
        b, half = core // 2, core % 2
        tok0 = half * NREAL
        xT = np.zeros((D, NTOK), f32)
        pTc = np.zeros((2, 256, NTOK), f32)
        if half == 1:
            xT[:, :] = x[b, tok0 - 128:tok0 + NREAL].T
            pTc[:, :, :] = p[:, b, tok0 - 128:tok0 + NREAL].transpose(0, 2, 1)
        else:
            xT[:, 128:] = x[b, 0:NREAL].T
            pTc[:, :, 128:] = p[:, b, 0:NREAL].transpose(0, 2, 1)
        pos = (np.arange(NTOK) + tok0 - 128).astype(f32)
        ang = pos[:, None] * inv_freq[None, :]
        cs, sn = np.cos(ang).astype(f32), np.sin(ang).astype(f32)
        rope = np.zeros((2, 128, NTOK), f32)
        rope[0] = 1.0
        for pp in range(128):
            d = pp % 64
            if d < 16:
                rope[0, pp] = cs[:, d % 8]
                rope[1, pp] = -sn[:, d % 8] if d < 8 else sn[:, d % 8]
        in_maps.append({"xT": xT, "pT": pTc, "wst": wst, "cvec": cvec, "cbf": cb[half], "rope": rope})
    return in_maps


_CACHE = {}


def kernel(**inputs):
    in_maps = host_prepare(inputs)
    if "nc" not in _CACHE:
        _CACHE["nc"] = build()[0]
    nc = _CACHE["nc"]
    res = run_bass_kernel_spmd(nc, in_maps, core_ids=list(range(N_CORES)))
    out = np.empty((4, 4096, D), np.float32)
    for core in range(N_CORES):
        b, half = core // 2, core % 2
        out[b, half * NREAL:(half + 1) * NREAL, :] = res.results[core]["outT"].T
    return out
```

```python
import contextlib
import numpy as np
import concourse.bass as bass
import concourse.mybir as mybir
from concourse.bass_utils import run_bass_kernel_spmd

F32 = mybir.dt.float32
BF16 = mybir.dt.bfloat16
U8 = mybir.dt.uint8
AF = mybir.ActivationFunctionType
ALU = mybir.AluOpType
AX = mybir.AxisListType

D = 2048
DFF = 5632
NC16 = 16
NFFC = 44
NTOK = 2176
NREAL = 2048
TILES = [(0, 640), (640, 512), (1152, 512), (1664, 512)]
TMAX = 640
NSLOT = 7
SLOTC = 4096
N_CORES = 8

CV_G = 0
CV_BQ = 144
CV_BK = 160
CV_BO = 164
CV_EPSR = 180
CV_EPSL = 181
CV_SINK = 182
NV = 216
CB_ONES = 0
CB_ID = 128
CB_ROT = 256
CB_MN = 384
CB_M0 = 640
CB_TRIL = 896
CB_BV = 1024
NCB = 1280

COMPUTE = ("pe", "act", "dve", "pool")
ALLENG = COMPUTE + ("sp",)


class Buf:
    __slots__ = ("name", "w", "r")

    def __init__(self, name):
        self.name = name
        self.w = None
        self.r = {}


class Op:
    __slots__ = ("fn", "deps", "inc", "dma_sem", "waits", "incval")

    def __init__(self, fn, deps, dma_sem):
        self.fn = fn
        self.deps = deps
        self.inc = False
        self.dma_sem = dma_sem
        self.waits = None
        self.incval = None


class Tracker:
    def __init__(self, nc):
        self.nc = nc
        self.ops = {e: [] for e in ALLENG}
        self.dma_cnt = {}
        self.bar = {e: set() for e in ALLENG}

    def op(self, eng, fn, reads=(), writes=(), dma=None):
        nd = set()
        for b in reads:
            d = b.w
            if d is not None:
                if d[0] == "c" and d[1] == eng and dma is None and eng == "pe":
                    pass
                else:
                    nd.add(d)
        for b in writes:
            d = b.w
            if d is not None and not (d[0] == "c" and d[1] == eng and dma is None):
                nd.add(d)
            for d in b.r.values():
                if not (d[0] == "c" and d[1] == eng and dma is None):
                    nd.add(d)
        if self.bar[eng]:
            nd |= self.bar[eng]
            self.bar[eng] = set()
        idx = len(self.ops[eng])
        self.ops[eng].append(Op(fn, nd, dma))
        if dma is None:
            ev = ("c", eng, idx)
            rkey = eng
        else:
            self.dma_cnt[dma] = self.dma_cnt.get(dma, 0) + 16
            ev = ("d", dma, self.dma_cnt[dma])
            rkey = ("d", dma)
        for b in writes:
            b.w = ev
            b.r = {}
        for b in reads:
            if b.w is not ev:
                b.r[rkey] = ev
        return ev

    def barrier(self, engines=("pe", "act", "dve", "sp")):
        evs = set()
        for e in COMPUTE:
            i = len(self.ops[e]) - 1
            while i >= 0 and self.ops[e][i].dma_sem is not None:
                i -= 1
            if i >= 0:
                evs.add(("c", e, i))
        for e in engines:
            self.bar[e] |= {x for x in evs if x[1] != e}

    def finalize(self):
        for e, lst in self.ops.items():
            seen = {}
            for o in lst:
                w = {}
                for d in o.deps:
                    key = (d[0], d[1])
                    val = d[2]
                    if seen.get(key, -1) >= val:
                        continue
                    if w.get(key, -1) < val:
                        w[key] = val
                for key, val in w.items():
                    seen[key] = val
                    if key[0] == "c":
                        self.ops[key[1]][val].inc = True
                o.waits = w
        self.nincs = {}
        for e, lst in self.ops.items():
            c = 0
            for o in lst:
                if o.inc:
                    c += 1
                    o.incval = c
            self.nincs[e] = c

    def emit(self, final_waits=()):
        nc = self.nc
        self.finalize()
        with contextlib.ExitStack() as st:
            esem = {e: st.enter_context(nc.semaphore("s_" + e)) for e in COMPUTE}
            dsem = {k: st.enter_context(nc.semaphore("d_" + str(k))) for k in self.dma_cnt}
            block = st.enter_context(nc.Block())
            engobj = {"pe": "tensor", "act": "scalar", "dve": "vector", "pool": "gpsimd", "sp": "sync"}

            def run(e, eng):
                for o in self.ops[e]:
                    for key, val in o.waits.items():
                        if key[0] == "c":
                            eng.wait_ge(esem[key[1]], self.ops[key[1]][val].incval)
                        else:
                            eng.wait_ge(dsem[key[1]], val)
                    ins = o.fn(eng)
                    if o.dma_sem is not None:
                        ins.then_inc(dsem[o.dma_sem], 16)
                    elif o.inc:
                        ins.then_inc(esem[e], 1)
                if e == "sp":
                    for k in final_waits:
                        eng.wait_ge(dsem[k], self.dma_cnt[k])

            for e in ALLENG:
                getattr(block, engobj[e])(lambda eng, e=e: run(e, eng))


def pass_units():
    U = []

    def ffn(name, l):
        for g in range(22):
            U.append((name + "_w1", l, 0, D, g * 256, g * 256 + 256))
            U.append((name + "_w3", l, 0, D, g * 256, g * 256 + 256))
        for dc in range(16):
            for h in range(2):
                U.append((name + "_w2", l, h * 2816, h * 2816 + 2816, dc * 128, dc * 128 + 128))

    def ple(l):
        U.append(("ple_w_proj", l, 0, 256, 0, 2048))
        for g in range(8):
            U.append(("ple_w_gate", l, 0, D, g * 256, g * 256 + 256))

    for l in range(2):
        ffn("ffn1", l)
        if l == 0:
            for cg in range(16):
                U.append(("gmlp_w_in", 0, 0, D, 4096 + cg * 256, 4096 + cg * 256 + 256))
            U.append(("lng_b", 0, 0, 128, 0, 4096))
            U.append(("lnb_b", 0, 0, 128, 0, 4096))
            U.append(("wsT", 0, 0, 128, 0, 2048))
            U.append(("bs_b", 0, 0, 128, 0, 4096))
            for cu in range(16):
                U.append(("gmlp_w_in", 0, 0, D, cu * 256, cu * 256 + 256))
            for dc in range(16):
                U.append(("gmlp_w_out", 0, 0, 4096, dc * 128, dc * 128 + 128))
        else:
            for g in range(8):
                U.append(("swa_wq", 0, 0, D, g * 256, g * 256 + 256))
            for g in range(2):
                U.append(("wk_dup", 0, 0, D, g * 256, g * 256 + 256))
            U.append(("swa_wv", 0, 0, D, 0, 256))
            for g in range(8):
                U.append(("swa_wo", 0, 0, D, g * 256, g * 256 + 256))
        ffn("ffn2", l)
        ple(l)
    return U


def unit_cols(u):
    return ((u[3] - u[2]) // 128) * (u[5] - u[4])


def subtiles(lo, hi):
    n = hi - lo
    if n <= 512:
        return [(lo, hi)]
    nb = n // 128
    first = (nb + 1) // 2
    return [(lo, lo + first * 128), (lo + first * 128, hi)]


def build(n_stages=9, debug_raw=False):
    units = pass_units()
    offs = np.cumsum([0] + [unit_cols(u) for u in units])
    passcols = int(offs[-1])

    nc = bass.Bass("TRN2", target_bir_lowering=False)
    xT = nc.dram_tensor("xT", [D, NTOK], F32, kind="ExternalInput").ap()
    pT = nc.dram_tensor("pT", [2, 256, NTOK], F32, kind="ExternalInput").ap()
    wst = nc.dram_tensor("wst", [128, passcols], F32, kind="ExternalInput").ap()
    cvec_d = nc.dram_tensor("cvec", [128, NV], F32, kind="ExternalInput").ap()
    cbf_d = nc.dram_tensor("cbf", [128, NCB], F32, kind="ExternalInput").ap()
    rope_d = nc.dram_tensor("rope", [2, 128, NTOK], F32, kind="ExternalInput").ap()
    outT = nc.dram_tensor("outT", [D, NREAL], F32, kind="ExternalOutput").ap()

    with contextlib.ExitStack() as st:
        T = Tracker(nc)

        def sb(name, shape, dt):
            return st.enter_context(nc.sbuf_tensor(name, shape, dt))

        X = sb("X", [128, NC16, TMAX], F32)
        XN = sb("XN", [128, NC16, TMAX], BF16)
        R = sb("R", [128, 65536], U8)
        SL = [sb("SL%d" % i, [128, SLOTC], BF16) for i in range(NSLOT)]
        PT = sb("PT", [128, 2, TMAX], BF16)
        RS = sb("RS", [128, TMAX], F32)
        TMPA = [sb("TMPA%d" % i, [128, 512], F32) for i in range(2)]
        TMPB = [sb("TMPB%d" % i, [128, 512], F32) for i in range(2)]
        CV = sb("CV", [128, NV], F32)
        NSK = sb("NSK", [128, 32], F32)
        CB = sb("CB", [128, NCB], BF16)
        KC = sb("KC", [128, 4, 128], BF16)
        VC = sb("VC", [128, 256], BF16)
        SM = sb("SM", [128, 64], F32)
        BNS = sb("BNS", [128, 48], F32)
        SMA = [sb("SMA%d" % i, [128, 32], F32) for i in range(3)]
        PS = st.enter_context(nc.psum_tensor("PS", [128, 4096], F32))

        bX = [Buf("X%d" % c) for c in range(NC16)]
        bXN = [Buf("XN%d" % c) for c in range(NC16)]
        bSL = [Buf("SL%d" % i) for i in range(NSLOT)]
        bPT = Buf("PT")
        bRS = Buf("RS")
        bTA = [Buf("TA0"), Buf("TA1")]
        bTB = [Buf("TB0"), Buf("TB1")]
        bCV = Buf("CV")
        bNSK = Buf("NSK")
        bCB = Buf("CB")
        bKC = Buf("KC")
        bVC = Buf("VC")
        bSM = Buf("SM")
        bBNS = Buf("BNS")
        bSMA = [Buf("SMA%d" % i) for i in range(3)]
        bPS = [Buf("PS%d" % i) for i in range(8)]
        bOUT = Buf("OUT")

        H = R[:, 0:NFFC * TMAX * 2].bitcast(BF16).rearrange("p (c t) -> p c t", c=NFFC)
        bH = [Buf("H%d" % c) for c in range(NFFC)]
        PJ = R[:, 0:NC16 * TMAX * 4].bitcast(F32).rearrange("p (c t) -> p c t", c=NC16)
        bPJ = [Buf("PJ%d" % c) for c in range(NC16)]
        VR = R[:, 0:5 * 4096 * 2].bitcast(BF16).rearrange("p (b c) -> p b c", b=5)
        bVR = [Buf("VR%d" % b) for b in range(5)]
        OST = R[:, 0:NC16 * 512 * 4].bitcast(F32).rearrange("p (c t) -> p c t", c=NC16)
        o = 0
        QT = R[:, o:o + 16 * 512 * 2].bitcast(BF16).rearrange("p (c t) -> p c t", c=16); o += 16 * 512 * 2
        KTL = R[:, o:o + 4 * 640 * 2].bitcast(BF16).rearrange("p (c t) -> p c t", c=4); o += 4 * 640 * 2
        KTH = R[:, o:o + 4 * 640 * 2].bitcast(BF16).rearrange("p (c t) -> p c t", c=4); o += 4 * 640 * 2
        VT = R[:, o:o + 5 * 256 * 2].bitcast(BF16).rearrange("p (b c) -> p b c", b=5); o += 5 * 256 * 2
        EE = [R[:, o + i * 4096:o + (i + 1) * 4096].bitcast(F32).rearrange("p (h s) -> p h s", h=4) for i in range(3)]; o += 12288
        PP = [R[:, o + i * 2048:o + (i + 1) * 2048].bitcast(BF16).rearrange("p (h s) -> p h s", h=4) for i in range(3)]; o += 6144
        PTS = [R[:, o + i * 2048:o + (i + 1) * 2048].bitcast(BF16).rearrange("p (h s) -> p h s", h=8) for i in range(3)]; o += 6144
        COS = R[:, o:o + 640 * 4].bitcast(F32); o += 2560
        SIN = R[:, o:o + 640 * 4].bitcast(F32); o += 2560
        QTMP = [R[:, o + i * 1024:o + (i + 1) * 1024].bitcast(BF16) for i in range(2)]; o += 2048
        assert o <= 65536, o
        bQT = [Buf("QT%d" % c) for c in range(16)]
        bKTL = [Buf("KTL%d" % c) for c in range(4)]
        bKTH = [Buf("KTH%d" % c) for c in range(4)]
        bVT = [Buf("VT%d" % b) for b in range(5)]
        bEE = [Buf("EE%d" % i) for i in range(3)]
        bPP = [Buf("PP%d" % i) for i in range(3)]
        bPTS = [Buf("PTS%d" % i) for i in range(3)]
        bCOS = Buf("COS")
        bSIN = Buf("SIN")
        bQTMP = [Buf("QTMP0"), Buf("QTMP1")]

        state = {"bank": 0, "slot": 0, "unit": 0, "tmp": 0}

        def bank():
            b = state["bank"]
            state["bank"] = (b + 1) % 8
            return b

        def bankpair():
            b = state["bank"]
            if b % 2:
                b = (b + 1) % 8
            state["bank"] = (b + 2) % 8
            return b, b + 1

        def psv(b, n):
            return PS[:, b * 512:b * 512 + n]

        def tmpi():
            i = state["tmp"]
            state["tmp"] = 1 - i
            return i

        def wunit(expect_key):
            ui = state["unit"] % len(units)
            u = units[ui]
            assert u[0] == expect_key, (u, expect_key)
            state["unit"] += 1
            s = state["slot"]
            state["slot"] = (s + 1) % NSLOT
            ncols = unit_cols(u)
            off = int(offs[ui])
            T.op("pool", lambda e: e.dma_start(out=SL[s][:, 0:ncols], in_=wst[:, off:off + ncols]),
                 writes=[bSL[s]], dma="w%d" % s)
            kc = (u[3] - u[2]) // 128
            return SL[s][:, 0:ncols].rearrange("p (k n) -> p k n", k=kc), bSL[s]

        def mm(out, lhsT, rhs, start, stop, reads, bk):
            T.op("pe", lambda e: e.matmul(out, lhsT=lhsT, rhs=rhs, start=start, stop=stop),
                 reads=reads, writes=[bPS[bk]])

        def mm_kc_outer(groups):
            nk = len(groups[0][2])
            for kc in range(nk):
                for (out, bk, lst) in groups:
                    lhsT, rhs, reads = lst[kc]
                    mm(out, lhsT, rhs, kc == 0, kc == nk - 1, reads, bk)

        T.op("sp", lambda e: e.dma_start(out=CV[:], in_=cvec_d), writes=[bCV], dma="cv")
        T.op("pool", lambda e: e.dma_start(out=CB[:], in_=cbf_d), writes=[bCB], dma="cb")
        T.op("dve", lambda e: e.tensor_scalar(out=NSK[:], in0=CV[:, CV_SINK:CV_SINK + 32], scalar1=-1.0, scalar2=None,
                                              op0=ALU.mult), reads=[bCV], writes=[bNSK])
        ONES = CB[:, CB_ONES:CB_ONES + 128]
        IDENT = CB[:, CB_ID:CB_ID + 128]
        ROT = CB[:, CB_ROT:CB_ROT + 128]

        def norm(gi, lo, hi, stats_only=False):
            for c4 in range(4):
                cs = slice(4 * c4, 4 * c4 + 4)
                if c4 % 2 == 0:
                    T.op("act", lambda e, cs=cs: e.activation(out=XN[:, cs, lo:hi], in_=X[:, cs, lo:hi], func=AF.Square),
                         reads=bX[cs], writes=bXN[cs])
                else:
                    T.op("dve", lambda e, cs=cs: e.tensor_tensor(out=XN[:, cs, lo:hi], in0=X[:, cs, lo:hi], in1=X[:, cs, lo:hi], op=ALU.mult),
                         reads=bX[cs], writes=bXN[cs])
            for (a, b) in subtiles(lo, hi):
                bk = bank()
                for c in range(NC16):
                    mm(psv(bk, b - a), ONES, XN[:, c, a:b], c == 0, c == NC16 - 1, [bCB, bXN[c]], bk)
                T.op("act", lambda e, a=a, b=b, bk=bk: e.activation(out=RS[:, a:b], in_=psv(bk, b - a), func=AF.Sqrt,
                                                                    bias=CV[:, CV_EPSR:CV_EPSR + 1], scale=1.0 / D),
                     reads=[bCV], writes=[bRS, bPS[bk]])
            T.op("dve", lambda e: e.reciprocal(out=RS[:, lo:hi], in_=RS[:, lo:hi]), reads=[bRS], writes=[bRS])
            if stats_only:
                return
            for c in range(NC16):
                T.op("dve", lambda e, c=c: e.scalar_tensor_tensor(out=XN[:, c, lo:hi], in0=X[:, c, lo:hi],
                                                                  scalar=CV[:, CV_G + gi * 16 + c:CV_G + gi * 16 + c + 1],
                                                                  in1=RS[:, lo:hi], op0=ALU.mult, op1=ALU.mult),
                     reads=[bX[c], bRS, bCV], writes=[bXN[c]])

        def ffn(name, gi, lo, hi):
            norm(gi, lo, hi)
            subs = subtiles(lo, hi)
            for g in range(22):
                u1, b1 = wunit(name + "_w1")
                u3, b3 = wunit(name + "_w3")
                pre = {}
                if g == 0:
                    a, b = subs[0]
                    n = b - a
                    grp = []
                    for j in range(2):
                        bA = bank()
                        bB = bank()
                        pre[j] = (bA, bB)
                        grp.append((psv(bA, n), bA, [(u1[:, kc, j * 128:(j + 1) * 128], XN[:, kc, a:b], [b1, bXN[kc]]) for kc in range(NC16)]))
                        grp.append((psv(bB, n), bB, [(u3[:, kc, j * 128:(j + 1) * 128], XN[:, kc, a:b], [b3, bXN[kc]]) for kc in range(NC16)]))
                    mm_kc_outer(grp)
                for j in range(2):
                    ffc = g * 2 + j
                    for si, (a, b) in enumerate(subs):
                        n = b - a
                        if g == 0 and si == 0:
                            bA, bB = pre[j]
                        else:
                            bA = bank()
                            bB = bank()
                            for kc in range(NC16):
                                mm(psv(bA, n), u1[:, kc, j * 128:(j + 1) * 128], XN[:, kc, a:b], kc == 0, kc == 15, [b1, bXN[kc]], bA)
                            for kc in range(NC16):
                                mm(psv(bB, n), u3[:, kc, j * 128:(j + 1) * 128], XN[:, kc, a:b], kc == 0, kc == 15, [b3, bXN[kc]], bB)
                        i = tmpi()
                        T.op("act", lambda e, i=i, bA=bA, n=n: e.activation(out=TMPA[i][:, 0:n], in_=psv(bA, n), func=AF.Silu),
                             writes=[bTA[i], bPS[bA]])
                        T.op("dve", lambda e, i=i, bB=bB, n=n, ffc=ffc, a=a, b=b: e.tensor_tensor(
                            out=H[:, ffc, a:b], in0=TMPA[i][:, 0:n], in1=psv(bB, n), op=ALU.mult),
                            reads=[bTA[i]], writes=[bH[ffc], bPS[bB]])
            for dc in range(NC16):
                ua, ba = wunit(name + "_w2")
                ub, bb = wunit(name + "_w2")
                for (a, b) in subs:
                    n = b - a
                    bY = bank()
                    for ffc in range(NFFC):
                        uu, bu = (ua, ba) if ffc < 22 else (ub, bb)
                        mm(psv(bY, n), uu[:, ffc % 22, :], H[:, ffc, a:b], ffc == 0, ffc == NFFC - 1, [bu, bH[ffc]], bY)
                    T.op("dve", lambda e, dc=dc, a=a, b=b, bY=bY, n=n: e.scalar_tensor_tensor(
                        out=X[:, dc, a:b], in0=psv(bY, n), scalar=0.5, in1=X[:, dc, a:b], op0=ALU.mult, op1=ALU.add),
                        reads=[bX[dc]], writes=[bX[dc], bPS[bY]])

        def ple(l, gi, lo, hi, t0):
            subs = subtiles(lo, hi)
            for kc in range(2):
                T.op("pool", lambda e, kc=kc: e.dma_start(out=PT[:, kc, lo:hi], in_=pT[l, kc * 128:(kc + 1) * 128, t0 + lo:t0 + hi]),
                     writes=[bPT], dma="pt")
            up, bp = wunit("ple_w_proj")
            for dc in range(NC16):
                for (a, b) in subs:
                    n = b - a
                    bk = bank()
                    for kc in range(2):
                        mm(psv(bk, n), up[:, kc, dc * 128:(dc + 1) * 128], PT[:, kc, a:b], kc == 0, kc == 1, [bp, bPT], bk)
                    T.op("act", lambda e, dc=dc, a=a, b=b, bk=bk, n=n: e.activation(out=PJ[:, dc, a:b], in_=psv(bk, n), func=AF.Copy),
                         writes=[bPJ[dc], bPS[bk]])
            norm(gi, lo, hi)
            for g in range(8):
                ug, bg = wunit("ple_w_gate")
                pre = {}
                if g == 0:
                    grp = []
                    for j in range(2):
                        for (a, b) in subs:
                            bk = bank()
                            pre[(j, a)] = bk
                            grp.append((psv(bk, b - a), bk, [(ug[:, kc, j * 128:(j + 1) * 128], XN[:, kc, a:b], [bg, bXN[kc]]) for kc in range(NC16)]))
                    mm_kc_outer(grp)
                for j in range(2):
                    dc = g * 2 + j
                    for (a, b) in subs:
                        n = b - a
                        if g == 0:
                            bk = pre[(j, a)]
                        else:
                            bk = bank()
                            for kc in range(NC16):
                                mm(psv(bk, n), ug[:, kc, j * 128:(j + 1) * 128], XN[:, kc, a:b], kc == 0, kc == 15, [bg, bXN[kc]], bk)
                        i = tmpi()
                        T.op("act", lambda e, i=i, bk=bk, n=n: e.activation(out=TMPA[i][:, 0:n], in_=psv(bk, n), func=AF.Sigmoid),
                             writes=[bTA[i], bPS[bk]])
                        T.op("dve", lambda e, i=i, dc=dc, a=a, b=b, n=n: e.tensor_tensor(
                            out=TMPB[i][:, 0:n], in0=TMPA[i][:, 0:n], in1=PJ[:, dc, a:b], op=ALU.mult),
                            reads=[bTA[i], bPJ[dc]], writes=[bTB[i]])
                        T.op("dve", lambda e, i=i, dc=dc, a=a, b=b, n=n: e.tensor_tensor(
                            out=X[:, dc, a:b], in0=X[:, dc, a:b], in1=TMPB[i][:, 0:n], op=ALU.add),
                            reads=[bTB[i], bX[dc]], writes=[bX[dc]])

        def gmlp(gi, lo, hi):
            norm(gi, lo, hi)
            subs = subtiles(lo, hi)
            blocks = list(range(lo // 128, hi // 128))
            for cg in range(16):
                uv, bv = wunit("gmlp_w_in")
                pre = {}
                if cg == 0:
                    grp = []
                    for tb in blocks:
                        bk = bank()
                        pre[tb] = bk
                        grp.append((psv(bk, 256), bk, [(XN[:, kc, tb * 128:(tb + 1) * 128], uv[:, kc, :], [bv, bXN[kc]]) for kc in range(NC16)]))
                    mm_kc_outer(grp)
                for tb in blocks:
                    if cg == 0:
                        bk = pre[tb]
                    else:
                        bk = bank()
                        for kc in range(NC16):
                            mm(psv(bk, 256), XN[:, kc, tb * 128:(tb + 1) * 128], uv[:, kc, :], kc == 0, kc == 15, [bv, bXN[kc]], bk)
                    T.op("act", lambda e, tb=tb, cg=cg, bk=bk: e.activation(out=VR[:, tb, cg * 256:(cg + 1) * 256], in_=psv(bk, 256), func=AF.Gelu),
                         writes=[bVR[tb], bPS[bk]])
            ug, bg = wunit("lng_b")
            ub, bb = wunit("lnb_b")
            uw, bw = wunit("wsT")
            us, bs = wunit("bs_b")
            for g in range(16):
                T.op("dve", lambda e, g=g: e.tensor_tensor(out=uw[:, 0, g * 128:(g + 1) * 128], in0=uw[:, 0, g * 128:(g + 1) * 128],
                                                           in1=CB[:, CB_TRIL:CB_TRIL + 128], op=ALU.mult),
                     reads=[bCB, bw], writes=[bw])
            def ln_block(tb):
                    for k in range(8):
                        T.op("dve", lambda e, tb=tb, k=k: e.bn_stats(out=BNS[:, k * 6:(k + 1) * 6], in_=VR[:, tb, k * 512:(k + 1) * 512]),
                             reads=[bVR[tb]], writes=[bBNS])
                    T.op("dve", lambda e: e.bn_aggr(out=SM[:, 0:2], in_=BNS[:, 0:48]), reads=[bBNS], writes=[bSM])
                    T.op("act", lambda e: e.activation(out=SM[:, 2:3], in_=SM[:, 1:2], func=AF.Sqrt, bias=CV[:, CV_EPSL:CV_EPSL + 1], scale=1.0),
                         reads=[bSM, bCV], writes=[bSM])
                    T.op("dve", lambda e: e.reciprocal(out=SM[:, 3:4], in_=SM[:, 2:3]), reads=[bSM], writes=[bSM])
                    T.op("dve", lambda e, tb=tb: e.tensor_scalar(out=VR[:, tb, :], in0=VR[:, tb, :], scalar1=SM[:, 0:1], scalar2=SM[:, 3:4],
                                                                 op0=ALU.subtract, op1=ALU.mult), reads=[bSM, bVR[tb]], writes=[bVR[tb]])
                    T.op("dve", lambda e, tb=tb: e.tensor_tensor(out=VR[:, tb, :], in0=VR[:, tb, :], in1=ug[:, 0, :], op=ALU.mult),
                         reads=[bg, bVR[tb]], writes=[bVR[tb]])
                    T.op("dve", lambda e, tb=tb: e.tensor_tensor(out=VR[:, tb, :], in0=VR[:, tb, :], in1=ub[:, 0, :], op=ALU.add),
                         reads=[bb, bVR[tb]], writes=[bVR[tb]])

            def spatial_block(tb):
                    for c4 in range(8):
                        bk = bank()
                        for cc in range(4):
                            c = c4 * 4 + cc
                            mm(PS[:, bk * 512 + cc * 128:bk * 512 + (cc + 1) * 128], VR[:, tb, c * 128:(c + 1) * 128],
                               uw[:, 0, (c // 2) * 128:(c // 2 + 1) * 128], True, True, [bVR[tb], bw], bk)
                        T.op("dve", lambda e, tb=tb, c4=c4, bk=bk: e.tensor_tensor(out=VR[:, tb, c4 * 512:(c4 + 1) * 512], in0=psv(bk, 512),
                                                                                   in1=us[:, 0, c4 * 512:(c4 + 1) * 512], op=ALU.add),
                             reads=[bs], writes=[bVR[tb], bPS[bk]])

            ln_block(blocks[0])
            for bi, tb in enumerate(blocks):
                if bi + 1 < len(blocks):
                    ln_block(blocks[bi + 1])
                spatial_block(tb)
            for cu in range(16):
                uu, bu = wunit("gmlp_w_in")
                for j in range(2):
                    c = cu * 2 + j
                    for (a, b) in subs:
                        n = b - a
                        bk = bank()
                        for kc in range(NC16):
                            mm(psv(bk, n), uu[:, kc, j * 128:(j + 1) * 128], XN[:, kc, a:b], kc == 0, kc == 15, [bu, bXN[kc]], bk)
                        i = tmpi()
                        T.op("act", lambda e, i=i, bk=bk, n=n: e.activation(out=TMPA[i][:, 0:n], in_=psv(bk, n), func=AF.Gelu),
                             writes=[bTA[i], bPS[bk]])
                        sv = VR[:, a // 128:b // 128, c * 128:(c + 1) * 128]
                        T.op("dve", lambda e, i=i, n=n, sv=sv: e.tensor_tensor(
                            out=sv, in0=TMPA[i][:, 0:n].rearrange("p (b t) -> p b t", t=128), in1=sv, op=ALU.mult),
                            reads=[bTA[i]] + bVR[a // 128:b // 128], writes=bVR[a // 128:b // 128])
            for dc in range(NC16):
                uo, bo = wunit("gmlp_w_out")
                for (a, b) in subs:
                    n = b - a
                    bk = bank()
                    for c in range(32):
                        mm(psv(bk, n).rearrange("p (b t) -> p b t", t=128), uo[:, c, :], VR[:, a // 128:b // 128, c * 128:(c + 1) * 128],
                           c == 0, c == 31, [bo] + bVR[a // 128:b // 128], bk)
                    T.op("dve", lambda e, dc=dc, a=a, b=b, bk=bk, n=n: e.tensor_tensor(
                        out=X[:, dc, a:b], in0=psv(bk, n), in1=X[:, dc, a:b], op=ALU.add),
                        reads=[bX[dc]], writes=[bX[dc], bPS[bk]])

        def rope_a(uw_, bw_, j, bias_col, src_cols):
            a, b = src_cols
            n = b - a
            bk = bank()
            for kc in range(NC16):
                mm(psv(bk, n), uw_[:, kc, j * 128:(j + 1) * 128], XN[:, kc, a:b], kc == 0, kc == 15, [bw_, bXN[kc]], bk)
            i = tmpi()
            T.op("act", lambda e: e.activation(out=QTMP[i][:, 0:n], in_=psv(bk, n), func=AF.Identity,
                                               bias=CV[:, bias_col:bias_col + 1], scale=1.0),
                 reads=[bCV], writes=[bQTMP[i], bPS[bk]])
            return (i, n)

        def rope_b(ctx, dst, bdst, dlo, tab_cols, dst2=None, bdst2=None):
            i, n = ctx
            ta, tb_ = tab_cols
            bk2 = bank()
            mm(psv(bk2, n), ROT, QTMP[i][:, 0:n], True, True, [bCB, bQTMP[i]], bk2)
            T.op("dve", lambda e: e.tensor_tensor(out=TMPA[i][:, 0:n], in0=QTMP[i][:, 0:n], in1=COS[:, ta:tb_], op=ALU.mult),
                 reads=[bQTMP[i], bCOS], writes=[bTA[i]])
            T.op("dve", lambda e: e.tensor_tensor(out=TMPB[i][:, 0:n], in0=psv(bk2, n), in1=SIN[:, ta:tb_], op=ALU.mult),
                 reads=[bSIN], writes=[bTB[i], bPS[bk2]])
            if dst2 is None:
                T.op("dve", lambda e: e.tensor_tensor(out=dst[:, dlo:dlo + n], in0=TMPA[i][:, 0:n], in1=TMPB[i][:, 0:n], op=ALU.add),
                     reads=[bTA[i], bTB[i]], writes=[bdst])
            else:
                T.op("dve", lambda e: e.tensor_tensor(out=dst[0:64, dlo:dlo + n], in0=TMPA[i][0:64, 0:n], in1=TMPB[i][0:64, 0:n], op=ALU.add),
                     reads=[bTA[i], bTB[i]], writes=[bdst])
                T.op("dve", lambda e: e.tensor_tensor(out=dst2[64:128, dlo:dlo + n], in0=TMPA[i][64:128, 0:n], in1=TMPB[i][64:128, 0:n], op=ALU.add),
                     reads=[bTA[i], bTB[i]], writes=[bdst2])

        def swa(gi, T_, q0, t0, ti):
            lo = 0 if ti == 0 else 0
            norm(gi, lo, T_)
            nq = T_ - q0
            nkb = 5
            kcomp0 = 0 if ti == 0 else 1
            T.op("sp", lambda e: e.dma_start(out=COS[:, 0:T_], in_=rope_d[0, :, t0:t0 + T_]), writes=[bCOS], dma="cos")
            T.op("sp", lambda e: e.dma_start(out=SIN[:, 0:T_], in_=rope_d[1, :, t0:t0 + T_]), writes=[bSIN], dma="sin")
            pend = []

            def flush():
                while pend:
                    ctx, args = pend.pop(0)
                    rope_b(ctx, *args[0], **args[1])

            for g in range(8):
                uq, bq_ = wunit("swa_wq")
                for j in range(2):
                    c = g * 2 + j
                    ctx = rope_a(uq, bq_, j, CV_BQ + c, (q0, T_))
                    flush()
                    pend.append((ctx, ((QT[:, c, :], bQT[c], 0, (q0, T_)), {})))
            T.op("dve", lambda e: e.memset(KTL[64:128, :, :], 0.0), writes=bKTL)
            T.op("dve", lambda e: e.memset(KTH[0:64, :, :], 0.0), writes=bKTH)
            if ti > 0:
                T.op("dve", lambda e: e.tensor_copy(out=KTL[0:64, :, 0:128], in_=KC[0:64]), reads=[bKC], writes=bKTL)
                T.op("dve", lambda e: e.tensor_copy(out=KTH[64:128, :, 0:128], in_=KC[64:128]), reads=[bKC], writes=bKTH)
                T.op("dve", lambda e: e.tensor_copy(out=VT[:, 0, :], in_=VC[:]), reads=[bVC], writes=[bVT[0]])
            for g2 in range(2):
                uk, bk_ = wunit("wk_dup")
                for j in range(2):
                    g = g2 * 2 + j
                    for (a, b) in subtiles(0, T_):
                        ctx = rope_a(uk, bk_, j, CV_BK + g, (a, b))
                        flush()
                        pend.append((ctx, ((KTL[:, g, :], bKTL[g], kcomp0 * 128 + a, (a, b)), dict(dst2=KTH[:, g, :], bdst2=bKTH[g]))))
            flush()
            uv, bv_ = wunit("swa_wv")
            for tb in range(T_ // 128):
                kb = kcomp0 + tb
                bk = bank()
                for kc in range(NC16):
                    mm(psv(bk, 256), XN[:, kc, tb * 128:(tb + 1) * 128], uv[:, kc, :], kc == 0, kc == 15, [bv_, bXN[kc]], bk)
                T.op("dve", lambda e, kb=kb, bk=bk: e.tensor_tensor(out=VT[:, kb, :], in0=psv(bk, 256), in1=CB[:, CB_BV:CB_BV + 256], op=ALU.add),
                     reads=[bCB], writes=[bVT[kb], bPS[bk]])
            T.barrier(("pe", "act", "dve"))
            units_ = [(jq, g, hh) for jq in range(4) for g in range(4) for hh in range(2)]
            NU_ = len(units_)
            BT, BO = 6, 7

            def stA(u):
                jq, g, hh = units_[u]
                w = u % 3
                mcol = CB_M0 if (ti == 0 and jq == 0) else CB_MN
                for i4 in range(4):
                    h = g * 8 + hh * 4 + i4
                    c, half = h // 2, h % 2
                    bk = 2 * w + (i4 // 2)
                    outp = PS[:, bk * 512 + (i4 % 2) * 256:bk * 512 + (i4 % 2) * 256 + 256]
                    kt_, bkt_ = (KTL, bKTL) if half == 0 else (KTH, bKTH)
                    mm(outp, IDENT, CB[:, mcol:mcol + 256], True, False, [bCB], bk)
                    mm(outp, QT[:, c, jq * 128:(jq + 1) * 128], kt_[:, g, jq * 128:jq * 128 + 256], False, True,
                       [bQT[c], bkt_[g]], bk)

            def stB1(u):
                jq, g, hh = units_[u]
                w = u % 3
                b0, b1 = 2 * w, 2 * w + 1
                sc = PS[:, b0 * 512:b0 * 512 + 1024].rearrange("p (h s) -> p h s", h=4)
                S_, bS_ = SMA[w], bSMA[w]
                hs = g * 8 + hh * 4
                T.op("dve", lambda e: e.tensor_reduce(out=S_[:, 0:4], in_=sc, axis=AX.X, op=ALU.max),
                     writes=[bS_, bPS[b0], bPS[b1]])
                T.op("dve", lambda e: e.scalar_tensor_tensor(out=S_[:, 4:8], in0=S_[:, 0:4], scalar=-0.125,
                                                             in1=NSK[:, hs:hs + 4], op0=ALU.mult, op1=ALU.min),
                     reads=[bS_, bNSK], writes=[bS_])
                T.op("dve", lambda e: e.tensor_tensor(out=S_[:, 12:16], in0=S_[:, 4:8],
                                                      in1=CV[:, CV_SINK + hs:CV_SINK + hs + 4], op=ALU.add),
                     reads=[bS_, bCV], writes=[bS_])
                for i4 in range(4):
                    T.op("act", lambda e, i4=i4: e.activation(
                        out=EE[w][:, i4, :], in_=sc[:, i4, :], func=AF.Exp, bias=S_[:, 4 + i4:5 + i4], scale=0.125,
                        accum_out=S_[:, 8 + i4:9 + i4]),
                        reads=[bS_], writes=[bEE[w], bS_, bPS[b0], bPS[b1]])
                T.op("act", lambda e: e.activation(out=S_[:, 16:20], in_=S_[:, 12:16], func=AF.Exp),
                     reads=[bS_], writes=[bS_])

            def stB2(u):
                w = u % 3
                S_, bS_ = SMA[w], bSMA[w]
                T.op("dve", lambda e: e.tensor_tensor(out=S_[:, 20:24], in0=S_[:, 16:20], in1=S_[:, 8:12], op=ALU.add),
                     reads=[bS_], writes=[bS_])
                T.op("dve", lambda e: e.reciprocal(out=S_[:, 24:28], in_=S_[:, 20:24]), reads=[bS_], writes=[bS_])
                for i4 in range(4):
                    T.op("dve", lambda e, i4=i4: e.tensor_scalar(
                        out=PP[w][:, i4, :], in0=EE[w][:, i4, :], scalar1=S_[:, 24 + i4:25 + i4], scalar2=None, op0=ALU.mult),
                        reads=[bEE[w], bS_], writes=[bPP[w]])

            def stC(u):
                w = u % 3
                ptp = PS[:, BT * 512:BT * 512 + 512].bitcast(BF16).rearrange("p (h s) -> p h s", h=8)
                for i4 in range(4):
                    for kb in range(2):
                        T.op("pe", lambda e, i4=i4, kb=kb: e.transpose(
                            out=ptp[:, i4 * 2 + kb, :], in_=PP[w][:, i4, kb * 128:(kb + 1) * 128], identity=IDENT),
                            reads=[bPP[w], bCB], writes=[bPS[BT]])
                T.op("act", lambda e: e.activation(out=PTS[w][:], in_=ptp, func=AF.Copy), writes=[bPTS[w], bPS[BT]])

            def stD(u):
                jq, g, hh = units_[u]
                w = u % 3
                hs = g * 8 + hh * 4
                for i4 in range(4):
                    half = (hs + i4) % 2
                    for kb in range(2):
                        mm(PS[half * 64:half * 64 + 64, BO * 512 + (i4 // 2) * 128:BO * 512 + (i4 // 2) * 128 + 128],
                           VT[:, jq + kb, g * 64:(g + 1) * 64], PTS[w][:, i4 * 2 + kb, :], kb == 0, kb == 1,
                           [bVT[jq + kb], bPTS[w]], BO)

            def stE(u):
                jq, g, hh = units_[u]
                c0 = (g * 8 + hh * 4) // 2
                T.op("dve", lambda e: e.tensor_copy(
                    out=XN[:, c0:c0 + 2, q0 + jq * 128:q0 + (jq + 1) * 128],
                    in_=PS[:, BO * 512:BO * 512 + 256].rearrange("p (c t) -> p c t", c=2)),
                    writes=[bXN[c0], bXN[c0 + 1], bPS[BO]])

            stA(0)
            stA(1)
            stB1(0)
            for u in range(NU_):
                if u + 2 < NU_:
                    stA(u + 2)
                if u + 1 < NU_:
                    stB1(u + 1)
                stB2(u)
                if u >= 1:
                    stE(u - 1)
                stC(u)
                stD(u)
            stE(NU_ - 1)
            T.op("dve", lambda e: e.tensor_copy(out=KC[0:64], in_=KTL[0:64, :, 512:640]), reads=bKTL, writes=[bKC])
            T.op("dve", lambda e: e.tensor_copy(out=KC[64:128], in_=KTH[64:128, :, 512:640]), reads=bKTH, writes=[bKC])
            T.op("dve", lambda e: e.tensor_copy(out=VC[:], in_=VT[:, 4, :]), reads=[bVT[4]], writes=[bVC])
            subs = subtiles(q0, T_)
            for g in range(8):
                uo, bo2 = wunit("swa_wo")
                for j in range(2):
                    dc = g * 2 + j
                    for (a, b) in subs:
                        n = b - a
                        bk = bank()
                        for kc in range(NC16):
                            mm(psv(bk, n), uo[:, kc, j * 128:(j + 1) * 128], XN[:, kc, a:b], kc == 0, kc == 15, [bo2, bXN[kc]], bk)
                        T.op("dve", lambda e, dc=dc, a=a, b=b, bk=bk, n=n: e.scalar_tensor_tensor(
                            out=X[:, dc, a:b], in0=psv(bk, n), scalar=CV[:, CV_BO + dc:CV_BO + dc + 1], in1=X[:, dc, a:b],
                            op0=ALU.add, op1=ALU.add), reads=[bX[dc], bCV], writes=[bX[dc], bPS[bk]])

        def do_tile(ti, t0, T_, out_off):
            q0 = 128 if ti == 0 else 0
            T.barrier(("pe", "act", "dve"))
            for c4 in range(4):
                cs = slice(4 * c4, 4 * c4 + 4)
                T.op("sp", lambda e, cs=cs, c4=c4: e.dma_start(
                    out=X[:, cs, 0:T_], in_=xT[c4 * 512:(c4 + 1) * 512, t0:t0 + T_].rearrange("(c p) t -> p c t", p=128)),
                    writes=bX[cs], dma="xin%d" % c4)
            stage = 0

            def go():
                nonlocal stage
                stage += 1
                return stage <= n_stages

            if go():
                T.barrier(); ffn("ffn1", 0, 0, T_)
            if go():
                T.barrier(); gmlp(1, 0, T_)
            if go():
                T.barrier(); ffn("ffn2", 2, 0, T_)
            if go():
                T.barrier(); ple(0, 3, 0, T_, t0)
            if go():
                T.barrier(); ffn("ffn1", 4, 0, T_)
            if go():
                T.barrier(); swa(5, T_, q0, t0, ti)
            if go():
                T.barrier(); ffn("ffn2", 6, q0, T_)
            if go():
                T.barrier(); ple(1, 7, q0, T_, t0)
            state["unit"] = (ti + 1) * len(units)
            T.barrier()
            nq = T_ - q0
            if go() and not debug_raw:
                norm(8, q0, T_, stats_only=True)
                for c in range(NC16):
                    T.op("dve", lambda e, c=c: e.scalar_tensor_tensor(out=OST[:, c, 0:nq], in0=X[:, c, q0:T_],
                                                                      scalar=CV[:, CV_G + 8 * 16 + c:CV_G + 8 * 16 + c + 1],
                                                                      in1=RS[:, q0:T_], op0=ALU.mult, op1=ALU.mult),
                         reads=[bX[c], bRS, bCV], writes=[bOUT])
            else:
                for c in range(NC16):
                    T.op("dve", lambda e, c=c: e.tensor_copy(out=OST[:, c, 0:nq], in_=X[:, c, q0:T_]), reads=[bX[c]], writes=[bOUT])
            for c4 in range(4):
                cs = slice(4 * c4, 4 * c4 + 4)
                T.op("sp", lambda e, cs=cs, c4=c4, oo=out_off: e.dma_start(
                    out=outT[c4 * 512:(c4 + 1) * 512, oo:oo + nq].rearrange("(c p) t -> p c t", p=128), in_=OST[:, cs, 0:nq]),
                    reads=[bOUT] + bH, dma="out")
        oo_ = 0
        for ti_, (t0_, TT_) in enumerate(TILES):
            do_tile(ti_, t0_, TT_, oo_)
            oo_ += 512
        T.emit(final_waits=["out"])
    return nc, T


def _kn(W, r0, r1, c0, c1):
    kc = (r1 - r0) // 128
    return np.ascontiguousarray(W[r0:r1, c0:c1].reshape(kc, 128, c1 - c0).transpose(1, 0, 2)).reshape(128, kc * (c1 - c0))


def host_prepare(inp):
    f32 = np.float32
    mats = {}
    for k in ("ffn1_w1", "ffn1_w3", "ffn1_w2", "ffn2_w1", "ffn2_w3", "ffn2_w2", "ple_w_gate", "ple_w_proj",
              "gmlp_w_in", "gmlp_w_out", "swa_wq", "swa_wv", "swa_wo"):
        mats[k] = np.asarray(inp[k], dtype=f32)
    wk = np.asarray(inp["swa_wk"], dtype=f32)[0]
    wk_dup = np.concatenate([wk[:, (g // 2) * 64:(g // 2) * 64 + 64] for g in range(8)], axis=1)
    mats["wk_dup"] = wk_dup[None]
    mats["lng_b"] = np.broadcast_to(np.asarray(inp["gmlp_ln_g"], f32)[0][None, :], (128, 4096))[None]
    mats["lnb_b"] = np.broadcast_to(np.asarray(inp["gmlp_ln_b"], f32)[0][None, :], (128, 4096))[None]
    ws = np.asarray(inp["gmlp_w_s"], f32)[0]
    mats["wsT"] = np.ascontiguousarray(ws.transpose(2, 0, 1)).reshape(128, 16 * 128)[None]
    bs = np.asarray(inp["gmlp_b_s"], f32)[0]
    bs32 = np.repeat(bs, 2, axis=0).reshape(1, 32 * 128)
    mats["bs_b"] = np.broadcast_to(bs32, (128, 4096))[None]
    units = pass_units()
    total = sum(unit_cols(u) for u in units)
    wst = np.empty((128, total), dtype=f32)
    off = 0
    for u in units:
        n = unit_cols(u)
        wst[:, off:off + n] = _kn(mats[u[0]][u[1]], u[2], u[3], u[4], u[5])
        off += n

    def pvec(v):
        return np.asarray(v, f32).reshape(16, 128).T

    cvec = np.zeros((128, NV), f32)
    gl = [inp["ffn1_norm"][0], inp["mix_norm"][0], inp["ffn2_norm"][0], inp["ple_norm"][0],
          inp["ffn1_norm"][1], inp["mix_norm"][1], inp["ffn2_norm"][1], inp["ple_norm"][1], inp["final_norm"]]
    for i, g in enumerate(gl):
        cvec[:, CV_G + i * 16:CV_G + (i + 1) * 16] = pvec(g)
    cvec[:, CV_BQ:CV_BQ + 16] = pvec(inp["swa_bq"][0])
    bk = np.asarray(inp["swa_bk"], f32)[0]
    bk_dup = np.concatenate([bk[(g // 2) * 64:(g // 2) * 64 + 64] for g in range(8)])
    cvec[:, CV_BK:CV_BK + 4] = bk_dup.reshape(4, 128).T
    cvec[:, CV_BO:CV_BO + 16] = pvec(inp["swa_bo"][0])
    cvec[:, CV_EPSR] = 1e-6
    cvec[:, CV_EPSL] = 1e-5
    cvec[:, CV_SINK:CV_SINK + 32] = np.asarray(inp["swa_sinks"], f32)[0][None, :]

    cb = np.zeros((2, 128, NCB), f32)
    pidx = np.arange(128)
    cb[:, :, CB_ONES:CB_ONES + 128] = 1.0
    cb[:, pidx, CB_ID + pidx] = 1.0
    for m in range(128):
        d = m % 64
        if d < 8:
            cb[:, m + 8, CB_ROT + m] = 1.0
        elif d < 16:
            cb[:, m - 8, CB_ROT + m] = 1.0
    NEG = -30000.0
    i = pidx[:, None]
    j = np.arange(128)[None, :]
    prev_ok = j > i
    cur_ok = j <= i
    mN = np.concatenate([np.where(prev_ok, 0.0, NEG), np.where(cur_ok, 0.0, NEG)], axis=1)
    m0 = np.concatenate([np.full((128, 128), NEG), np.where(cur_ok, 0.0, NEG)], axis=1)
    cb[:, :, CB_MN:CB_MN + 256] = mN
    cb[0, :, CB_M0:CB_M0 + 256] = m0
    cb[1, :, CB_M0:CB_M0 + 256] = mN
    cb[:, :, CB_TRIL:CB_TRIL + 128] = (i <= j).astype(f32)
    cb[:, :, CB_BV:CB_BV + 256] = np.asarray(inp["swa_bv"], f32)[0][None, :]

    x = np.asarray(inp["x"], f32)
    p = np.asarray(inp["p"], f32)
    inv_freq = (np.float32(500000.0) ** (-(np.arange(0, 16, 2, dtype=f32)) / np.float32(16))).astype(f32)
    in_maps = []
    for core in range(N_CORES):
        b, half = core // 2, core % 2
        tok0 = half * NREAL
        xT = np.zeros((D, NTOK), f32)
        pTc = np.zeros((2, 256, NTOK), f32)
        if half == 1:
            xT[:, :] = x[b, tok0 - 128:tok0 + NREAL].T
            pTc[:, :, :] = p[:, b, tok0 - 128:tok0 + NREAL].transpose(0, 2, 1)
        else:
            xT[:, 128:] = x[b, 0:NREAL].T
            pTc[:, :, 128:] = p[:, b, 0:NREAL].transpose(0, 2, 1)
        pos = (np.arange(NTOK) + tok0 - 128).astype(f32)
        ang = pos[:, None] * inv_freq[None, :]
        cs, sn = np.cos(ang).astype(f32), np.sin(ang).astype(f32)
        rope = np.zeros((2, 128, NTOK), f32)
        rope[0] = 1.0
        for pp in range(128):
            d = pp % 64
            if d < 16:
                rope[0, pp] = cs[:, d % 8]
                rope[1, pp] = -sn[:, d % 8] if d < 8 else sn[:, d % 8]
        in_maps.append({"xT": xT, "pT": pTc, "wst": wst, "cvec": cvec, "cbf": cb[half], "rope": rope})
    return in_maps


_CACHE = {}


def kernel(**inputs):
    in_maps = host_prepare(inputs)
    if "nc" not in _CACHE:
        _CACHE["nc"] = build()[0]
    nc = _CACHE["nc"]
    res = run_bass_kernel_spmd(nc, in_maps, core_ids=list(range(N_CORES)))
    out = np.empty((4, 4096, D), np.float32)
    for core in range(N_CORES):
        b, half = core // 2, core % 2
        out[b, half * NREAL:(half + 1) * NREAL, :] = res.results[core]["outT"].T
    return out
```

```python
import contextlib
import numpy as np
import concourse.bass as bass
import concourse.mybir as mybir
from concourse.bass_utils import run_bass_kernel_spmd

F32 = mybir.dt.float32
BF16 = mybir.dt.bfloat16
U8 = mybir.dt.uint8
AF = mybir.ActivationFunctionType
ALU = mybir.AluOpType
AX = mybir.AxisListType

D = 2048
DFF = 5632
NC16 = 16
NFFC = 44
NTOK = 2176
NREAL = 2048
TILES = [(0, 640), (640, 512), (1152, 512), (1664, 512)]
TMAX = 640
NSLOT = 7
SLOTC = 4096
N_CORES = 8

CV_G = 0
CV_BQ = 144
CV_BK = 160
CV_BO = 164
CV_EPSR = 180
CV_EPSL = 181
CV_SINK = 182
NV = 216
CB_ONES = 0
CB_ID = 128
CB_ROT = 256
CB_MN = 384
CB_M0 = 640
CB_TRIL = 896
CB_BV = 1024
NCB = 1280

COMPUTE = ("pe", "act", "dve", "pool")
ALLENG = COMPUTE + ("sp",)


class Buf:
    __slots__ = ("name", "w", "r")

    def __init__(self, name):
        self.name = name
        self.w = None
        self.r = {}


class Op:
    __slots__ = ("fn", "deps", "inc", "dma_sem", "waits", "incval")

    def __init__(self, fn, deps, dma_sem):
        self.fn = fn
        self.deps = deps
        self.inc = False
        self.dma_sem = dma_sem
        self.waits = None
        self.incval = None


class Tracker:
    def __init__(self, nc):
        self.nc = nc
        self.ops = {e: [] for e in ALLENG}
        self.dma_cnt = {}
        self.bar = {e: set() for e in ALLENG}

    def op(self, eng, fn, reads=(), writes=(), dma=None):
        nd = set()
        for b in reads:
            d = b.w
            if d is not None:
                if d[0] == "c" and d[1] == eng and dma is None and eng == "pe":
                    pass
                else:
                    nd.add(d)
        for b in writes:
            d = b.w
            if d is not None and not (d[0] == "c" and d[1] == eng and dma is None):
                nd.add(d)
            for d in b.r.values():
                if not (d[0] == "c" and d[1] == eng and dma is None):
                    nd.add(d)
        if self.bar[eng]:
            nd |= self.bar[eng]
            self.bar[eng] = set()
        idx = len(self.ops[eng])
        self.ops[eng].append(Op(fn, nd, dma))
        if dma is None:
            ev = ("c", eng, idx)
            rkey = eng
        else:
            self.dma_cnt[dma] = self.dma_cnt.get(dma, 0) + 16
            ev = ("d", dma, self.dma_cnt[dma])
            rkey = ("d", dma)
        for b in writes:
            b.w = ev
            b.r = {}
        for b in reads:
            if b.w is not ev:
                b.r[rkey] = ev
        return ev

    def barrier(self, engines=("pe", "act", "dve", "sp")):
        evs = set()
        for e in COMPUTE:
            i = len(self.ops[e]) - 1
            while i >= 0 and self.ops[e][i].dma_sem is not None:
                i -= 1
            if i >= 0:
                evs.add(("c", e, i))
        for e in engines:
            self.bar[e] |= {x for x in evs if x[1] != e}

    def finalize(self):
        for e, lst in self.ops.items():
            seen = {}
            for o in lst:
                w = {}
                for d in o.deps:
                    key = (d[0], d[1])
                    val = d[2]
                    if seen.get(key, -1) >= val:
                        continue
                    if w.get(key, -1) < val:
                        w[key] = val
                for key, val in w.items():
                    seen[key] = val
                    if key[0] == "c":
                        self.ops[key[1]][val].inc = True
                o.waits = w
        self.nincs = {}
        for e, lst in self.ops.items():
            c = 0
            for o in lst:
                if o.inc:
                    c += 1
                    o.incval = c
            self.nincs[e] = c

    def emit(self, final_waits=()):
        nc = self.nc
        self.finalize()
        with contextlib.ExitStack() as st:
            esem = {e: st.enter_context(nc.semaphore("s_" + e)) for e in COMPUTE}
            dsem = {k: st.enter_context(nc.semaphore("d_" + str(k))) for k in self.dma_cnt}
            block = st.enter_context(nc.Block())
            engobj = {"pe": "tensor", "act": "scalar", "dve": "vector", "pool": "gpsimd", "sp": "sync"}

            def run(e, eng):
                for o in self.ops[e]:
                    for key, val in o.waits.items():
                        if key[0] == "c":
                            eng.wait_ge(esem[key[1]], self.ops[key[1]][val].incval)
                        else:
                            eng.wait_ge(dsem[key[1]], val)
                    ins = o.fn(eng)
                    if o.dma_sem is not None:
                        ins.then_inc(dsem[o.dma_sem], 16)
                    elif o.inc:
                        ins.then_inc(esem[e], 1)
                if e == "sp":
                    for k in final_waits:
                        eng.wait_ge(dsem[k], self.dma_cnt[k])

            for e in ALLENG:
                getattr(block, engobj[e])(lambda eng, e=e: run(e, eng))


def pass_units():
    U = []

    def ffn(name, l):
        for g in range(22):
            U.append((name + "_w1", l, 0, D, g * 256, g * 256 + 256))
            U.append((name + "_w3", l, 0, D, g * 256, g * 256 + 256))
        for dc in range(16):
            for h in range(2):
                U.append((name + "_w2", l, h * 2816, h * 2816 + 2816, dc * 128, dc * 128 + 128))

    def ple(l):
        U.append(("ple_w_proj", l, 0, 256, 0, 2048))
        for g in range(8):
            U.append(("ple_w_gate", l, 0, D, g * 256, g * 256 + 256))

    for l in range(2):
        ffn("ffn1", l)
        if l == 0:
            for cg in range(16):
                U.append(("gmlp_w_in", 0, 0, D, 4096 + cg * 256, 4096 + cg * 256 + 256))
            U.append(("lng_b", 0, 0, 128, 0, 4096))
            U.append(("lnb_b", 0, 0, 128, 0, 4096))
            U.append(("wsT", 0, 0, 128, 0, 2048))
            U.append(("bs_b", 0, 0, 128, 0, 4096))
            for cu in range(16):
                U.append(("gmlp_w_in", 0, 0, D, cu * 256, cu * 256 + 256))
            for dc in range(16):
                U.append(("gmlp_w_out", 0, 0, 4096, dc * 128, dc * 128 + 128))
        else:
            for g in range(8):
                U.append(("swa_wq", 0, 0, D, g * 256, g * 256 + 256))
            for g in range(2):
                U.append(("wk_dup", 0, 0, D, g * 256, g * 256 + 256))
            U.append(("swa_wv", 0, 0, D, 0, 256))
            for g in range(8):
                U.append(("swa_wo", 0, 0, D, g * 256, g * 256 + 256))
        ffn("ffn2", l)
        ple(l)
    return U


def unit_cols(u):
    return ((u[3] - u[2]) // 128) * (u[5] - u[4])


def subtiles(lo, hi):
    n = hi - lo
    if n <= 512:
        return [(lo, hi)]
    nb = n // 128
    first = (nb + 1) // 2
    return [(lo, lo + first * 128), (lo + first * 128, hi)]


def build(n_stages=9, debug_raw=False):
    units = pass_units()
    offs = np.cumsum([0] + [unit_cols(u) for u in units])
    passcols = int(offs[-1])

    nc = bass.Bass("TRN2", target_bir_lowering=False)
    xT = nc.dram_tensor("xT", [D, NTOK], F32, kind="ExternalInput").ap()
    pT = nc.dram_tensor("pT", [2, 256, NTOK], F32, kind="ExternalInput").ap()
    wst = nc.dram_tensor("wst", [128, passcols], F32, kind="ExternalInput").ap()
    cvec_d = nc.dram_tensor("cvec", [128, NV], F32, kind="ExternalInput").ap()
    cbf_d = nc.dram_tensor("cbf", [128, NCB], F32, kind="ExternalInput").ap()
    rope_d = nc.dram_tensor("rope", [2, 128, NTOK], F32, kind="ExternalInput").ap()
    outT = nc.dram_tensor("outT", [D, NREAL], F32, kind="ExternalOutput").ap()

    with contextlib.ExitStack() as st:
        T = Tracker(nc)

        def sb(name, shape, dt):
            return st.enter_context(nc.sbuf_tensor(name, shape, dt))

        X = sb("X", [128, NC16, TMAX], F32)
        XN = sb("XN", [128, NC16, TMAX], BF16)
        R = sb("R", [128, 65536], U8)
        SL = [sb("SL%d" % i, [128, SLOTC], BF16) for i in range(NSLOT)]
        PT = sb("PT", [128, 2, TMAX], BF16)
        RS = sb("RS", [128, TMAX], F32)
        TMPA = [sb("TMPA%d" % i, [128, 512], F32) for i in range(2)]
        TMPB = [sb("TMPB%d" % i, [128, 512], F32) for i in range(2)]
        CV = sb("CV", [128, NV], F32)
        NSK = sb("NSK", [128, 32], F32)
        CB = sb("CB", [128, NCB], BF16)
        KC = sb("KC", [128, 4, 128], BF16)
        VC = sb("VC", [128, 256], BF16)
        SM = sb("SM", [128, 64], F32)
        BNS = sb("BNS", [128, 48], F32)
        SMA = [sb("SMA%d" % i, [128, 32], F32) for i in range(3)]
        PS = st.enter_context(nc.psum_tensor("PS", [128, 4096], F32))

        bX = [Buf("X%d" % c) for c in range(NC16)]
        bXN = [Buf("XN%d" % c) for c in range(NC16)]
        bSL = [Buf("SL%d" % i) for i in range(NSLOT)]
        bPT = Buf("PT")
        bRS = Buf("RS")
        bTA = [Buf("TA0"), Buf("TA1")]
        bTB = [Buf("TB0"), Buf("TB1")]
        bCV = Buf("CV")
        bNSK = Buf("NSK")
        bCB = Buf("CB")
        bKC = Buf("KC")
        bVC = Buf("VC")
        bSM = Buf("SM")
        bBNS = Buf("BNS")
        bSMA = [Buf("SMA%d" % i) for i in range(3)]
        bPS = [Buf("PS%d" % i) for i in range(8)]
        bOUT = Buf("OUT")

        H = R[:, 0:NFFC * TMAX * 2].bitcast(BF16).rearrange("p (c t) -> p c t", c=NFFC)
        bH = [Buf("H%d" % c) for c in range(NFFC)]
        PJ = R[:, 0:NC16 * TMAX * 4].bitcast(F32).rearrange("p (c t) -> p c t", c=NC16)
        bPJ = [Buf("PJ%d" % c) for c in range(NC16)]
        VR = R[:, 0:5 * 4096 * 2].bitcast(BF16).rearrange("p (b c) -> p b c", b=5)
        bVR = [Buf("VR%d" % b) for b in range(5)]
        OST = R[:, 0:NC16 * 512 * 4].bitcast(F32).rearrange("p (c t) -> p c t", c=NC16)
        o = 0
        QT = R[:, o:o + 16 * 512 * 2].bitcast(BF16).rearrange("p (c t) -> p c t", c=16); o += 16 * 512 * 2
        KTL = R[:, o:o + 4 * 640 * 2].bitcast(BF16).rearrange("p (c t) -> p c t", c=4); o += 4 * 640 * 2
        KTH = R[:, o:o + 4 * 640 * 2].bitcast(BF16).rearrange("p (c t) -> p c t", c=4); o += 4 * 640 * 2
        VT = R[:, o:o + 5 * 256 * 2].bitcast(BF16).rearrange("p (b c) -> p b c", b=5); o += 5 * 256 * 2
        EE = [R[:, o + i * 4096:o + (i + 1) * 4096].bitcast(F32).rearrange("p (h s) -> p h s", h=4) for i in range(3)]; o += 12288
        PP = [R[:, o + i * 2048:o + (i + 1) * 2048].bitcast(BF16).rearrange("p (h s) -> p h s", h=4) for i in range(3)]; o += 6144
        PTS = [R[:, o + i * 2048:o + (i + 1) * 2048].bitcast(BF16).rearrange("p (h s) -> p h s", h=8) for i in range(3)]; o += 6144
        COS = R[:, o:o + 640 * 4].bitcast(F32); o += 2560
        SIN = R[:, o:o + 640 * 4].bitcast(F32); o += 2560
        QTMP = [R[:, o + i * 1024:o + (i + 1) * 1024].bitcast(BF16) for i in range(2)]; o += 2048
        assert o <= 65536, o
        bQT = [Buf("QT%d" % c) for c in range(16)]
        bKTL = [Buf("KTL%d" % c) for c in range(4)]
        bKTH = [Buf("KTH%d" % c) for c in range(4)]
        bVT = [Buf("VT%d" % b) for b in range(5)]
        bEE = [Buf("EE%d" % i) for i in range(3)]
        bPP = [Buf("PP%d" % i) for i in range(3)]
        bPTS = [Buf("PTS%d" % i) for i in range(3)]
        bCOS = Buf("COS")
        bSIN = Buf("SIN")
        bQTMP = [Buf("QTMP0"), Buf("QTMP1")]

        state = {"bank": 0, "slot": 0, "unit": 0, "tmp": 0}

        def bank():
            b = state["bank"]
            state["bank"] = (b + 1) % 8
            return b

        def bankpair():
            b = state["bank"]
            if b % 2:
                b = (b + 1) % 8
            state["bank"] = (b + 2) % 8
            return b, b + 1

        def psv(b, n):
            return PS[:, b * 512:b * 512 + n]

        def tmpi():
            i = state["tmp"]
            state["tmp"] = 1 - i
            return i

        def wunit(expect_key):
            ui = state["unit"] % len(units)
            u = units[ui]
            assert u[0] == expect_key, (u, expect_key)
            state["unit"] += 1
            s = state["slot"]
            state["slot"] = (s + 1) % NSLOT
            ncols = unit_cols(u)
            off = int(offs[ui])
            T.op("pool", lambda e: e.dma_start(out=SL[s][:, 0:ncols], in_=wst[:, off:off + ncols]),
                 writes=[bSL[s]], dma="w%d" % s)
            kc = (u[3] - u[2]) // 128
            return SL[s][:, 0:ncols].rearrange("p (k n) -> p k n", k=kc), bSL[s]

        def mm(out, lhsT, rhs, start, stop, reads, bk):
            T.op("pe", lambda e: e.matmul(out, lhsT=lhsT, rhs=rhs, start=start, stop=stop),
                 reads=reads, writes=[bPS[bk]])

        def mm_kc_outer(groups):
            nk = len(groups[0][2])
            for kc in range(nk):
                for (out, bk, lst) in groups:
                    lhsT, rhs, reads = lst[kc]
                    mm(out, lhsT, rhs, kc == 0, kc == nk - 1, reads, bk)

        T.op("sp", lambda e: e.dma_start(out=CV[:], in_=cvec_d), writes=[bCV], dma="cv")
        T.op("pool", lambda e: e.dma_start(out=CB[:], in_=cbf_d), writes=[bCB], dma="cb")
        T.op("dve", lambda e: e.tensor_scalar(out=NSK[:], in0=CV[:, CV_SINK:CV_SINK + 32], scalar1=-1.0, scalar2=None,
                                              op0=ALU.mult), reads=[bCV], writes=[bNSK])
        ONES = CB[:, CB_ONES:CB_ONES + 128]
        IDENT = CB[:, CB_ID:CB_ID + 128]
        ROT = CB[:, CB_ROT:CB_ROT + 128]

        def norm(gi, lo, hi, stats_only=False):
            for c4 in range(4):
                cs = slice(4 * c4, 4 * c4 + 4)
                if c4 % 2 == 0:
                    T.op("act", lambda e, cs=cs: e.activation(out=XN[:, cs, lo:hi], in_=X[:, cs, lo:hi], func=AF.Square),
                         reads=bX[cs], writes=bXN[cs])
                else:
                    T.op("dve", lambda e, cs=cs: e.tensor_tensor(out=XN[:, cs, lo:hi], in0=X[:, cs, lo:hi], in1=X[:, cs, lo:hi], op=ALU.mult),
                         reads=bX[cs], writes=bXN[cs])
            for (a, b) in subtiles(lo, hi):
                bk = bank()
                for c in range(NC16):
                    mm(psv(bk, b - a), ONES, XN[:, c, a:b], c == 0, c == NC16 - 1, [bCB, bXN[c]], bk)
                T.op("act", lambda e, a=a, b=b, bk=bk: e.activation(out=RS[:, a:b], in_=psv(bk, b - a), func=AF.Sqrt,
                                                                    bias=CV[:, CV_EPSR:CV_EPSR + 1], scale=1.0 / D),
                     reads=[bCV], writes=[bRS, bPS[bk]])
            T.op("dve", lambda e: e.reciprocal(out=RS[:, lo:hi], in_=RS[:, lo:hi]), reads=[bRS], writes=[bRS])
            if stats_only:
                return
            for c in range(NC16):
                T.op("dve", lambda e, c=c: e.scalar_tensor_tensor(out=XN[:, c, lo:hi], in0=X[:, c, lo:hi],
                                                                  scalar=CV[:, CV_G + gi * 16 + c:CV_G + gi * 16 + c + 1],
                                                                  in1=RS[:, lo:hi], op0=ALU.mult, op1=ALU.mult),
                     reads=[bX[c], bRS, bCV], writes=[bXN[c]])

        def ffn(name, gi, lo, hi):
            for c in range(NC16):
                gcol = CV[:, CV_G + gi * 16 + c:CV_G + gi * 16 + c + 1]
                if c % 2 == 0:
                    T.op("act", lambda e, c=c, gcol=gcol: e.activation(out=XN[:, c, lo:hi], in_=X[:, c, lo:hi], func=AF.Copy, scale=gcol),
                         reads=[bX[c], bCV], writes=[bXN[c]])
                else:
                    T.op("dve", lambda e, c=c, gcol=gcol: e.tensor_scalar(out=XN[:, c, lo:hi], in0=X[:, c, lo:hi], scalar1=gcol, scalar2=None,
                                                                          op0=ALU.mult), reads=[bX[c], bCV], writes=[bXN[c]])
            SQ0 = 28
            for c4 in range(4):
                cs = slice(4 * c4, 4 * c4 + 4)
                hs_ = slice(SQ0 + 4 * c4, SQ0 + 4 * c4 + 4)
                if c4 % 2 == 0:
                    T.op("act", lambda e, cs=cs, hs_=hs_: e.activation(out=H[:, hs_, lo:hi], in_=X[:, cs, lo:hi], func=AF.Square),
                         reads=bX[cs], writes=bH[hs_])
                else:
                    T.op("dve", lambda e, cs=cs, hs_=hs_: e.tensor_tensor(out=H[:, hs_, lo:hi], in0=X[:, cs, lo:hi], in1=X[:, cs, lo:hi], op=ALU.mult),
                         reads=bX[cs], writes=bH[hs_])
            for (a, b) in subtiles(lo, hi):
                bk = bank()
                for c in range(NC16):
                    mm(psv(bk, b - a), ONES, H[:, SQ0 + c, a:b], c == 0, c == NC16 - 1, [bCB, bH[SQ0 + c]], bk)
                T.op("act", lambda e, a=a, b=b, bk=bk: e.activation(out=RS[:, a:b], in_=psv(bk, b - a), func=AF.Sqrt,
                                                                    bias=CV[:, CV_EPSR:CV_EPSR + 1], scale=1.0 / D),
                     reads=[bCV], writes=[bRS, bPS[bk]])
            T.op("dve", lambda e: e.reciprocal(out=RS[:, lo:hi], in_=RS[:, lo:hi]), reads=[bRS], writes=[bRS])
            subs = subtiles(lo, hi)
            for g in range(22):
                u1, b1 = wunit(name + "_w1")
                u3, b3 = wunit(name + "_w3")
                for j in range(2):
                    ffc = g * 2 + j
                    for (a, b) in subs:
                        n = b - a
                        bA = bank()
                        bB = bank()
                        for kc in range(NC16):
                            mm(psv(bA, n), u1[:, kc, j * 128:(j + 1) * 128], XN[:, kc, a:b], kc == 0, kc == 15, [b1, bXN[kc]], bA)
                        for kc in range(NC16):
                            mm(psv(bB, n), u3[:, kc, j * 128:(j + 1) * 128], XN[:, kc, a:b], kc == 0, kc == 15, [b3, bXN[kc]], bB)
                        i = tmpi()
                        T.op("dve", lambda e, i=i, bA=bA, n=n, a=a, b=b: e.tensor_tensor(
                            out=TMPA[i][:, 0:n], in0=psv(bA, n), in1=RS[:, a:b], op=ALU.mult),
                            reads=[bRS], writes=[bTA[i], bPS[bA]])
                        T.op("act", lambda e, i=i, n=n: e.activation(out=TMPA[i][:, 0:n], in_=TMPA[i][:, 0:n], func=AF.Silu),
                             reads=[bTA[i]], writes=[bTA[i]])
                        T.op("dve", lambda e, i=i, bB=bB, n=n, a=a, b=b: e.tensor_tensor(
                            out=TMPB[i][:, 0:n], in0=psv(bB, n), in1=RS[:, a:b], op=ALU.mult),
                            reads=[bRS], writes=[bTB[i], bPS[bB]])
                        T.op("dve", lambda e, i=i, n=n, ffc=ffc, a=a, b=b: e.tensor_tensor(
                            out=H[:, ffc, a:b], in0=TMPA[i][:, 0:n], in1=TMPB[i][:, 0:n], op=ALU.mult),
                            reads=[bTA[i], bTB[i]], writes=[bH[ffc]])
            for dc in range(NC16):
                ua, ba = wunit(name + "_w2")
                ub, bb = wunit(name + "_w2")
                for (a, b) in subs:
                    n = b - a
                    bY = bank()
                    for ffc in range(NFFC):
                        uu, bu = (ua, ba) if ffc < 22 else (ub, bb)
                        mm(psv(bY, n), uu[:, ffc % 22, :], H[:, ffc, a:b], ffc == 0, ffc == NFFC - 1, [bu, bH[ffc]], bY)
                    T.op("dve", lambda e, dc=dc, a=a, b=b, bY=bY, n=n: e.scalar_tensor_tensor(
                        out=X[:, dc, a:b], in0=psv(bY, n), scalar=0.5, in1=X[:, dc, a:b], op0=ALU.mult, op1=ALU.add),
                        reads=[bX[dc]], writes=[bX[dc], bPS[bY]])

        def ple(l, gi, lo, hi, t0):
            subs = subtiles(lo, hi)
            for kc in range(2):
                T.op("pool", lambda e, kc=kc: e.dma_start(out=PT[:, kc, lo:hi], in_=pT[l, kc * 128:(kc + 1) * 128, t0 + lo:t0 + hi]),
                     writes=[bPT], dma="pt")
            up, bp = wunit("ple_w_proj")
            for dc in range(NC16):
                for (a, b) in subs:
                    n = b - a
                    bk = bank()
                    for kc in range(2):
                        mm(psv(bk, n), up[:, kc, dc * 128:(dc + 1) * 128], PT[:, kc, a:b], kc == 0, kc == 1, [bp, bPT], bk)
                    T.op("act", lambda e, dc=dc, a=a, b=b, bk=bk, n=n: e.activation(out=PJ[:, dc, a:b], in_=psv(bk, n), func=AF.Copy),
                         writes=[bPJ[dc], bPS[bk]])
            norm(gi, lo, hi)
            for g in range(8):
                ug, bg = wunit("ple_w_gate")
                pre = {}
                if g == 0:
                    grp = []
                    for j in range(2):
                        for (a, b) in subs:
                            bk = bank()
                            pre[(j, a)] = bk
                            grp.append((psv(bk, b - a), bk, [(ug[:, kc, j * 128:(j + 1) * 128], XN[:, kc, a:b], [bg, bXN[kc]]) for kc in range(NC16)]))
                    mm_kc_outer(grp)
                for j in range(2):
                    dc = g * 2 + j
                    for (a, b) in subs:
                        n = b - a
                        if g == 0:
                            bk = pre[(j, a)]
                        else:
                            bk = bank()
                            for kc in range(NC16):
                                mm(psv(bk, n), ug[:, kc, j * 128:(j + 1) * 128], XN[:, kc, a:b], kc == 0, kc == 15, [bg, bXN[kc]], bk)
                        i = tmpi()
                        T.op("act", lambda e, i=i, bk=bk, n=n: e.activation(out=TMPA[i][:, 0:n], in_=psv(bk, n), func=AF.Sigmoid),
                             writes=[bTA[i], bPS[bk]])
                        T.op("dve", lambda e, i=i, dc=dc, a=a, b=b, n=n: e.tensor_tensor(
                            out=TMPB[i][:, 0:n], in0=TMPA[i][:, 0:n], in1=PJ[:, dc, a:b], op=ALU.mult),
                            reads=[bTA[i], bPJ[dc]], writes=[bTB[i]])
                        T.op("dve", lambda e, i=i, dc=dc, a=a, b=b, n=n: e.tensor_tensor(
                            out=X[:, dc, a:b], in0=X[:, dc, a:b], in1=TMPB[i][:, 0:n], op=ALU.add),
                            reads=[bTB[i], bX[dc]], writes=[bX[dc]])

        def gmlp(gi, lo, hi):
            norm(gi, lo, hi)
            subs = subtiles(lo, hi)
            blocks = list(range(lo // 128, hi // 128))
            for cg in range(16):
                uv, bv = wunit("gmlp_w_in")
                pre = {}
                if cg == 0:
                    grp = []
                    for tb in blocks:
                        bk = bank()
                        pre[tb] = bk
                        grp.append((psv(bk, 256), bk, [(XN[:, kc, tb * 128:(tb + 1) * 128], uv[:, kc, :], [bv, bXN[kc]]) for kc in range(NC16)]))
                    mm_kc_outer(grp)
                for tb in blocks:
                    if cg == 0:
                        bk = pre[tb]
                    else:
                        bk = bank()
                        for kc in range(NC16):
                            mm(psv(bk, 256), XN[:, kc, tb * 128:(tb + 1) * 128], uv[:, kc, :], kc == 0, kc == 15, [bv, bXN[kc]], bk)
                    T.op("act", lambda e, tb=tb, cg=cg, bk=bk: e.activation(out=VR[:, tb, cg * 256:(cg + 1) * 256], in_=psv(bk, 256), func=AF.Gelu),
                         writes=[bVR[tb], bPS[bk]])
            ug, bg = wunit("lng_b")
            ub, bb = wunit("lnb_b")
            uw, bw = wunit("wsT")
            us, bs = wunit("bs_b")
            for g in range(16):
                T.op("dve", lambda e, g=g: e.tensor_tensor(out=uw[:, 0, g * 128:(g + 1) * 128], in0=uw[:, 0, g * 128:(g + 1) * 128],
                                                           in1=CB[:, CB_TRIL:CB_TRIL + 128], op=ALU.mult),
                     reads=[bCB, bw], writes=[bw])
            def ln_block(tb):
                    for k in range(8):
                        T.op("dve", lambda e, tb=tb, k=k: e.bn_stats(out=BNS[:, k * 6:(k + 1) * 6], in_=VR[:, tb, k * 512:(k + 1) * 512]),
                             reads=[bVR[tb]], writes=[bBNS])
                    T.op("dve", lambda e: e.bn_aggr(out=SM[:, 0:2], in_=BNS[:, 0:48]), reads=[bBNS], writes=[bSM])
                    T.op("act", lambda e: e.activation(out=SM[:, 2:3], in_=SM[:, 1:2], func=AF.Sqrt, bias=CV[:, CV_EPSL:CV_EPSL + 1], scale=1.0),
                         reads=[bSM, bCV], writes=[bSM])
                    T.op("dve", lambda e: e.reciprocal(out=SM[:, 3:4], in_=SM[:, 2:3]), reads=[bSM], writes=[bSM])
                    T.op("dve", lambda e, tb=tb: e.tensor_scalar(out=VR[:, tb, :], in0=VR[:, tb, :], scalar1=SM[:, 0:1], scalar2=SM[:, 3:4],
                                                                 op0=ALU.subtract, op1=ALU.mult), reads=[bSM, bVR[tb]], writes=[bVR[tb]])
                    T.op("dve", lambda e, tb=tb: e.tensor_tensor(out=VR[:, tb, :], in0=VR[:, tb, :], in1=ug[:, 0, :], op=ALU.mult),
                         reads=[bg, bVR[tb]], writes=[bVR[tb]])
                    T.op("dve", lambda e, tb=tb: e.tensor_tensor(out=VR[:, tb, :], in0=VR[:, tb, :], in1=ub[:, 0, :], op=ALU.add),
                         reads=[bb, bVR[tb]], writes=[bVR[tb]])

            def spatial_block(tb):
                    for c4 in range(8):
                        bk = bank()
                        for cc in range(4):
                            c = c4 * 4 + cc
                            mm(PS[:, bk * 512 + cc * 128:bk * 512 + (cc + 1) * 128], VR[:, tb, c * 128:(c + 1) * 128],
                               uw[:, 0, (c // 2) * 128:(c // 2 + 1) * 128], True, True, [bVR[tb], bw], bk)
                        T.op("dve", lambda e, tb=tb, c4=c4, bk=bk: e.tensor_tensor(out=VR[:, tb, c4 * 512:(c4 + 1) * 512], in0=psv(bk, 512),
                                                                                   in1=us[:, 0, c4 * 512:(c4 + 1) * 512], op=ALU.add),
                             reads=[bs], writes=[bVR[tb], bPS[bk]])

            ln_block(blocks[0])
            for bi, tb in enumerate(blocks):
                if bi + 1 < len(blocks):
                    ln_block(blocks[bi + 1])
                spatial_block(tb)
            for cu in range(16):
                uu, bu = wunit("gmlp_w_in")
                for j in range(2):
                    c = cu * 2 + j
                    for (a, b) in subs:
                        n = b - a
                        bk = bank()
                        for kc in range(NC16):
                            mm(psv(bk, n), uu[:, kc, j * 128:(j + 1) * 128], XN[:, kc, a:b], kc == 0, kc == 15, [bu, bXN[kc]], bk)
                        i = tmpi()
                        T.op("act", lambda e, i=i, bk=bk, n=n: e.activation(out=TMPA[i][:, 0:n], in_=psv(bk, n), func=AF.Gelu),
                             writes=[bTA[i], bPS[bk]])
                        sv = VR[:, a // 128:b // 128, c * 128:(c + 1) * 128]
                        T.op("dve", lambda e, i=i, n=n, sv=sv: e.tensor_tensor(
                            out=sv, in0=TMPA[i][:, 0:n].rearrange("p (b t) -> p b t", t=128), in1=sv, op=ALU.mult),
                            reads=[bTA[i]] + bVR[a // 128:b // 128], writes=bVR[a // 128:b // 128])
            for dc in range(NC16):
                uo, bo = wunit("gmlp_w_out")
                for (a, b) in subs:
                    n = b - a
                    bk = bank()
                    for c in range(32):
                        mm(psv(bk, n).rearrange("p (b t) -> p b t", t=128), uo[:, c, :], VR[:, a // 128:b // 128, c * 128:(c + 1) * 128],
                           c == 0, c == 31, [bo] + bVR[a // 128:b // 128], bk)
                    T.op("dve", lambda e, dc=dc, a=a, b=b, bk=bk, n=n: e.tensor_tensor(
                        out=X[:, dc, a:b], in0=psv(bk, n), in1=X[:, dc, a:b], op=ALU.add),
                        reads=[bX[dc]], writes=[bX[dc], bPS[bk]])

        def rope_a(uw_, bw_, j, bias_col, src_cols):
            a, b = src_cols
            n = b - a
            bk = bank()
            for kc in range(NC16):
                mm(psv(bk, n), uw_[:, kc, j * 128:(j + 1) * 128], XN[:, kc, a:b], kc == 0, kc == 15, [bw_, bXN[kc]], bk)
            i = tmpi()
            T.op("act", lambda e: e.activation(out=QTMP[i][:, 0:n], in_=psv(bk, n), func=AF.Identity,
                                               bias=CV[:, bias_col:bias_col + 1], scale=1.0),
                 reads=[bCV], writes=[bQTMP[i], bPS[bk]])
            return (i, n)

        def rope_b(ctx, dst, bdst, dlo, tab_cols, dst2=None, bdst2=None):
            i, n = ctx
            ta, tb_ = tab_cols
            bk2 = bank()
            mm(psv(bk2, n), ROT, QTMP[i][:, 0:n], True, True, [bCB, bQTMP[i]], bk2)
            T.op("dve", lambda e: e.tensor_tensor(out=TMPA[i][:, 0:n], in0=QTMP[i][:, 0:n], in1=COS[:, ta:tb_], op=ALU.mult),
                 reads=[bQTMP[i], bCOS], writes=[bTA[i]])
            T.op("dve", lambda e: e.tensor_tensor(out=TMPB[i][:, 0:n], in0=psv(bk2, n), in1=SIN[:, ta:tb_], op=ALU.mult),
                 reads=[bSIN], writes=[bTB[i], bPS[bk2]])
            if dst2 is None:
                T.op("dve", lambda e: e.tensor_tensor(out=dst[:, dlo:dlo + n], in0=TMPA[i][:, 0:n], in1=TMPB[i][:, 0:n], op=ALU.add),
                     reads=[bTA[i], bTB[i]], writes=[bdst])
            else:
                T.op("dve", lambda e: e.tensor_tensor(out=dst[0:64, dlo:dlo + n], in0=TMPA[i][0:64, 0:n], in1=TMPB[i][0:64, 0:n], op=ALU.add),
                     reads=[bTA[i], bTB[i]], writes=[bdst])
                T.op("dve", lambda e: e.tensor_tensor(out=dst2[64:128, dlo:dlo + n], in0=TMPA[i][64:128, 0:n], in1=TMPB[i][64:128, 0:n], op=ALU.add),
                     reads=[bTA[i], bTB[i]], writes=[bdst2])

        def swa(gi, T_, q0, t0, ti):
            lo = 0 if ti == 0 else 0
            norm(gi, lo, T_)
            nq = T_ - q0
            nkb = 5
            kcomp0 = 0 if ti == 0 else 1
            T.op("sp", lambda e: e.dma_start(out=COS[:, 0:T_], in_=rope_d[0, :, t0:t0 + T_]), writes=[bCOS], dma="cos")
            T.op("sp", lambda e: e.dma_start(out=SIN[:, 0:T_], in_=rope_d[1, :, t0:t0 + T_]), writes=[bSIN], dma="sin")
            pend = []

            def flush():
                while pend:
                    ctx, args = pend.pop(0)
                    rope_b(ctx, *args[0], **args[1])

            for g in range(8):
                uq, bq_ = wunit("swa_wq")
                for j in range(2):
                    c = g * 2 + j
                    ctx = rope_a(uq, bq_, j, CV_BQ + c, (q0, T_))
                    flush()
                    pend.append((ctx, ((QT[:, c, :], bQT[c], 0, (q0, T_)), {})))
            T.op("dve", lambda e: e.memset(KTL[64:128, :, :], 0.0), writes=bKTL)
            T.op("dve", lambda e: e.memset(KTH[0:64, :, :], 0.0), writes=bKTH)
            if ti > 0:
                T.op("dve", lambda e: e.tensor_copy(out=KTL[0:64, :, 0:128], in_=KC[0:64]), reads=[bKC], writes=bKTL)
                T.op("dve", lambda e: e.tensor_copy(out=KTH[64:128, :, 0:128], in_=KC[64:128]), reads=[bKC], writes=bKTH)
                T.op("dve", lambda e: e.tensor_copy(out=VT[:, 0, :], in_=VC[:]), reads=[bVC], writes=[bVT[0]])
            for g2 in range(2):
                uk, bk_ = wunit("wk_dup")
                for j in range(2):
                    g = g2 * 2 + j
                    for (a, b) in subtiles(0, T_):
                        ctx = rope_a(uk, bk_, j, CV_BK + g, (a, b))
                        flush()
                        pend.append((ctx, ((KTL[:, g, :], bKTL[g], kcomp0 * 128 + a, (a, b)), dict(dst2=KTH[:, g, :], bdst2=bKTH[g]))))
            flush()
            uv, bv_ = wunit("swa_wv")
            for tb in range(T_ // 128):
                kb = kcomp0 + tb
                bk = bank()
                for kc in range(NC16):
                    mm(psv(bk, 256), XN[:, kc, tb * 128:(tb + 1) * 128], uv[:, kc, :], kc == 0, kc == 15, [bv_, bXN[kc]], bk)
                T.op("dve", lambda e, kb=kb, bk=bk: e.tensor_tensor(out=VT[:, kb, :], in0=psv(bk, 256), in1=CB[:, CB_BV:CB_BV + 256], op=ALU.add),
                     reads=[bCB], writes=[bVT[kb], bPS[bk]])
            T.barrier(("pe", "act", "dve"))
            units_ = [(jq, g, hh) for jq in range(4) for g in range(4) for hh in range(2)]
            NU_ = len(units_)
            BT, BO = 6, 7

            def stA(u):
                jq, g, hh = units_[u]
                w = u % 3
                mcol = CB_M0 if (ti == 0 and jq == 0) else CB_MN
                for i4 in range(4):
                    h = g * 8 + hh * 4 + i4
                    c, half = h // 2, h % 2
                    bk = 2 * w + (i4 // 2)
                    outp = PS[:, bk * 512 + (i4 % 2) * 256:bk * 512 + (i4 % 2) * 256 + 256]
                    kt_, bkt_ = (KTL, bKTL) if half == 0 else (KTH, bKTH)
                    mm(outp, IDENT, CB[:, mcol:mcol + 256], True, False, [bCB], bk)
                    mm(outp, QT[:, c, jq * 128:(jq + 1) * 128], kt_[:, g, jq * 128:jq * 128 + 256], False, True,
                       [bQT[c], bkt_[g]], bk)

            def stB1(u):
                jq, g, hh = units_[u]
                w = u % 3
                b0, b1 = 2 * w, 2 * w + 1
                sc = PS[:, b0 * 512:b0 * 512 + 1024].rearrange("p (h s) -> p h s", h=4)
                S_, bS_ = SMA[w], bSMA[w]
                hs = g * 8 + hh * 4
                T.op("dve", lambda e: e.tensor_reduce(out=S_[:, 0:4], in_=sc, axis=AX.X, op=ALU.max),
                     writes=[bS_, bPS[b0], bPS[b1]])
                T.op("dve", lambda e: e.scalar_tensor_tensor(out=S_[:, 4:8], in0=S_[:, 0:4], scalar=-0.125,
                                                             in1=NSK[:, hs:hs + 4], op0=ALU.mult, op1=ALU.min),
                     reads=[bS_, bNSK], writes=[bS_])
                T.op("dve", lambda e: e.tensor_tensor(out=S_[:, 12:16], in0=S_[:, 4:8],
                                                      in1=CV[:, CV_SINK + hs:CV_SINK + hs + 4], op=ALU.add),
                     reads=[bS_, bCV], writes=[bS_])
                for i4 in range(4):
                    T.op("act", lambda e, i4=i4: e.activation(
                        out=EE[w][:, i4, :], in_=sc[:, i4, :], func=AF.Exp, bias=S_[:, 4 + i4:5 + i4], scale=0.125,
                        accum_out=S_[:, 8 + i4:9 + i4]),
                        reads=[bS_], writes=[bEE[w], bS_, bPS[b0], bPS[b1]])
                T.op("act", lambda e: e.activation(out=S_[:, 16:20], in_=S_[:, 12:16], func=AF.Exp),
                     reads=[bS_], writes=[bS_])

            def stB2(u):
                w = u % 3
                S_, bS_ = SMA[w], bSMA[w]
                T.op("dve", lambda e: e.tensor_tensor(out=S_[:, 20:24], in0=S_[:, 16:20], in1=S_[:, 8:12], op=ALU.add),
                     reads=[bS_], writes=[bS_])
                T.op("dve", lambda e: e.reciprocal(out=S_[:, 24:28], in_=S_[:, 20:24]), reads=[bS_], writes=[bS_])
                for i4 in range(4):
                    T.op("dve", lambda e, i4=i4: e.tensor_scalar(
                        out=PP[w][:, i4, :], in0=EE[w][:, i4, :], scalar1=S_[:, 24 + i4:25 + i4], scalar2=None, op0=ALU.mult),
                        reads=[bEE[w], bS_], writes=[bPP[w]])

            def stC(u):
                w = u % 3
                ptp = PS[:, BT * 512:BT * 512 + 512].bitcast(BF16).rearrange("p (h s) -> p h s", h=8)
                for i4 in range(4):
                    for kb in range(2):
                        T.op("pe", lambda e, i4=i4, kb=kb: e.transpose(
                            out=ptp[:, i4 * 2 + kb, :], in_=PP[w][:, i4, kb * 128:(kb + 1) * 128], identity=IDENT),
                            reads=[bPP[w], bCB], writes=[bPS[BT]])
                T.op("act", lambda e: e.activation(out=PTS[w][:], in_=ptp, func=AF.Copy), writes=[bPTS[w], bPS[BT]])

            def stD(u):
                jq, g, hh = units_[u]
                w = u % 3
                hs = g * 8 + hh * 4
                for i4 in range(4):
                    half = (hs + i4) % 2
                    for kb in range(2):
                        mm(PS[half * 64:half * 64 + 64, BO * 512 + (i4 // 2) * 128:BO * 512 + (i4 // 2) * 128 + 128],
                           VT[:, jq + kb, g * 64:(g + 1) * 64], PTS[w][:, i4 * 2 + kb, :], kb == 0, kb == 1,
                           [bVT[jq + kb], bPTS[w]], BO)

            def stE(u):
                jq, g, hh = units_[u]
                c0 = (g * 8 + hh * 4) // 2
                T.op("dve", lambda e: e.tensor_copy(
                    out=XN[:, c0:c0 + 2, q0 + jq * 128:q0 + (jq + 1) * 128],
                    in_=PS[:, BO * 512:BO * 512 + 256].rearrange("p (c t) -> p c t", c=2)),
                    writes=[bXN[c0], bXN[c0 + 1], bPS[BO]])

            stA(0)
            stA(1)
            stB1(0)
            for u in range(NU_):
                if u + 2 < NU_:
                    stA(u + 2)
                if u + 1 < NU_:
                    stB1(u + 1)
                stB2(u)
                if u >= 1:
                    stE(u - 1)
                stC(u)
                stD(u)
            stE(NU_ - 1)
            T.op("dve", lambda e: e.tensor_copy(out=KC[0:64], in_=KTL[0:64, :, 512:640]), reads=bKTL, writes=[bKC])
            T.op("dve", lambda e: e.tensor_copy(out=KC[64:128], in_=KTH[64:128, :, 512:640]), reads=bKTH, writes=[bKC])
            T.op("dve", lambda e: e.tensor_copy(out=VC[:], in_=VT[:, 4, :]), reads=[bVT[4]], writes=[bVC])
            subs = subtiles(q0, T_)
            for g in range(8):
                uo, bo2 = wunit("swa_wo")
                for j in range(2):
                    dc = g * 2 + j
                    for (a, b) in subs:
                        n = b - a
                        bk = bank()
                        for kc in range(NC16):
                            mm(psv(bk, n), uo[:, kc, j * 128:(j + 1) * 128], XN[:, kc, a:b], kc == 0, kc == 15, [bo2, bXN[kc]], bk)
                        T.op("dve", lambda e, dc=dc, a=a, b=b, bk=bk, n=n: e.scalar_tensor_tensor(
                            out=X[:, dc, a:b], in0=psv(bk, n), scalar=CV[:, CV_BO + dc:CV_BO + dc + 1], in1=X[:, dc, a:b],
                            op0=ALU.add, op1=ALU.add), reads=[bX[dc], bCV], writes=[bX[dc], bPS[bk]])

        def do_tile(ti, t0, T_, out_off):
            q0 = 128 if ti == 0 else 0
            T.barrier(("pe", "act", "dve"))
            for c4 in range(4):
                cs = slice(4 * c4, 4 * c4 + 4)
                T.op("sp", lambda e, cs=cs, c4=c4: e.dma_start(
                    out=X[:, cs, 0:T_], in_=xT[c4 * 512:(c4 + 1) * 512, t0:t0 + T_].rearrange("(c p) t -> p c t", p=128)),
                    writes=bX[cs], dma="xin%d" % c4)
            stage = 0

            def go():
                nonlocal stage
                stage += 1
                return stage <= n_stages

            if go():
                T.barrier(); ffn("ffn1", 0, 0, T_)
            if go():
                T.barrier(); gmlp(1, 0, T_)
            if go():
                T.barrier(); ffn("ffn2", 2, 0, T_)
            if go():
                T.barrier(); ple(0, 3, 0, T_, t0)
            if go():
                T.barrier(); ffn("ffn1", 4, 0, T_)
            if go():
                T.barrier(); swa(5, T_, q0, t0, ti)
            if go():
                T.barrier(); ffn("ffn2", 6, q0, T_)
            if go():
                T.barrier(); ple(1, 7, q0, T_, t0)
            state["unit"] = (ti + 1) * len(units)
            T.barrier()
            nq = T_ - q0
            if go() and not debug_raw:
                norm(8, q0, T_, stats_only=True)
                for c in range(NC16):
                    T.op("dve", lambda e, c=c: e.scalar_tensor_tensor(out=OST[:, c, 0:nq], in0=X[:, c, q0:T_],
                                                                      scalar=CV[:, CV_G + 8 * 16 + c:CV_G + 8 * 16 + c + 1],
                                                                      in1=RS[:, q0:T_], op0=ALU.mult, op1=ALU.mult),
                         reads=[bX[c], bRS, bCV], writes=[bOUT])
            else:
                for c in range(NC16):
                    T.op("dve", lambda e, c=c: e.tensor_copy(out=OST[:, c, 0:nq], in_=X[:, c, q0:T_]), reads=[bX[c]], writes=[bOUT])
            for c4 in range(4):
                cs = slice(4 * c4, 4 * c4 + 4)
                T.op("sp", lambda e, cs=cs, c4=c4, oo=out_off: e.dma_start(
                    out=outT[c4 * 512:(c4 + 1) * 512, oo:oo + nq].rearrange("(c p) t -> p c t", p=128), in_=OST[:, cs, 0:nq]),
                    reads=[bOUT] + bH, dma="out")
        oo_ = 0
        for ti_, (t0_, TT_) in enumerate(TILES):
            do_tile(ti_, t0_, TT_, oo_)
            oo_ += 512
        T.emit(final_waits=["out"])
    return nc, T


def _kn(W, r0, r1, c0, c1):
    kc = (r1 - r0) // 128
    return np.ascontiguousarray(W[r0:r1, c0:c1].reshape(kc, 128, c1 - c0).transpose(1, 0, 2)).reshape(128, kc * (c1 - c0))


def host_prepare(inp):
    f32 = np.float32
    mats = {}
    for k in ("ffn1_w1", "ffn1_w3", "ffn1_w2", "ffn2_w1", "ffn2_w3", "ffn2_w2", "ple_w_gate", "ple_w_proj",
              "gmlp_w_in", "gmlp_w_out", "swa_wq", "swa_wv", "swa_wo"):
        mats[k] = np.asarray(inp[k], dtype=f32)
    wk = np.asarray(inp["swa_wk"], dtype=f32)[0]
    wk_dup = np.concatenate([wk[:, (g // 2) * 64:(g // 2) * 64 + 64] for g in range(8)], axis=1)
    mats["wk_dup"] = wk_dup[None]
    mats["lng_b"] = np.broadcast_to(np.asarray(inp["gmlp_ln_g"], f32)[0][None, :], (128, 4096))[None]
    mats["lnb_b"] = np.broadcast_to(np.asarray(inp["gmlp_ln_b"], f32)[0][None, :], (128, 4096))[None]
    ws = np.asarray(inp["gmlp_w_s"], f32)[0]
    mats["wsT"] = np.ascontiguousarray(ws.transpose(2, 0, 1)).reshape(128, 16 * 128)[None]
    bs = np.asarray(inp["gmlp_b_s"], f32)[0]
    bs32 = np.repeat(bs, 2, axis=0).reshape(1, 32 * 128)
    mats["bs_b"] = np.broadcast_to(bs32, (128, 4096))[None]
    units = pass_units()
    total = sum(unit_cols(u) for u in units)
    wst = np.empty((128, total), dtype=f32)
    off = 0
    for u in units:
        n = unit_cols(u)
        wst[:, off:off + n] = _kn(mats[u[0]][u[1]], u[2], u[3], u[4], u[5])
        off += n

    def pvec(v):
        return np.asarray(v, f32).reshape(16, 128).T

    cvec = np.zeros((128, NV), f32)
    gl = [inp["ffn1_norm"][0], inp["mix_norm"][0], inp["ffn2_norm"][0], inp["ple_norm"][0],
          inp["ffn1_norm"][1], inp["mix_norm"][1], inp["ffn2_norm"][1], inp["ple_norm"][1], inp["final_norm"]]
    for i, g in enumerate(gl):
        cvec[:, CV_G + i * 16:CV_G + (i + 1) * 16] = pvec(g)
    cvec[:, CV_BQ:CV_BQ + 16] = pvec(inp["swa_bq"][0])
    bk = np.asarray(inp["swa_bk"], f32)[0]
    bk_dup = np.concatenate([bk[(g // 2) * 64:(g // 2) * 64 + 64] for g in range(8)])
    cvec[:, CV_BK:CV_BK + 4] = bk_dup.reshape(4, 128).T
    cvec[:, CV_BO:CV_BO + 16] = pvec(inp["swa_bo"][0])
    cvec[:, CV_EPSR] = 1e-6
    cvec[:, CV_EPSL] = 1e-5
    cvec[:, CV_SINK:CV_SINK + 32] = np.asarray(inp["swa_sinks"], f32)[0][None, :]

    cb = np.zeros((2, 128, NCB), f32)
    pidx = np.arange(128)
    cb[:, :, CB_ONES:CB_ONES + 128] = 1.0
    cb[:, pidx, CB_ID + pidx] = 1.0
    for m in range(128):
        d = m % 64
        if d < 8:
            cb[:, m + 8, CB_ROT + m] = 1.0
        elif d < 16:
            cb[:, m - 8, CB_ROT + m] = 1.0
    NEG = -30000.0
    i = pidx[:, None]
    j = np.arange(128)[None, :]
    prev_ok = j > i
    cur_ok = j <= i
    mN = np.concatenate([np.where(prev_ok, 0.0, NEG), np.where(cur_ok, 0.0, NEG)], axis=1)
    m0 = np.concatenate([np.full((128, 128), NEG), np.where(cur_ok, 0.0, NEG)], axis=1)
    cb[:, :, CB_MN:CB_MN + 256] = mN
    cb[0, :, CB_M0:CB_M0 + 256] = m0
    cb[1, :, CB_M0:CB_M0 + 256] = mN
    cb[:, :, CB_TRIL:CB_TRIL + 128] = (i <= j).astype(f32)
    cb[:, :, CB_BV:CB_BV + 256] = np.asarray(inp["swa_bv"], f32)[0][None, :]

    x = np.asarray(inp["x"], f32)
    p = np.asarray(inp["p"], f32)
    inv_freq = (np.float32(500000.0) ** (-(np.arange(0, 16, 2, dtype=f32)) / np.float32(16))).astype(f32)
    in_maps = []
    for core in range(N_CORES):
        b, half = core // 2, core % 2
        tok0 = half * NREAL
        xT = np.zeros((D, NTOK), f32)
        pTc = np.zeros((2, 256, NTOK), f32)
        if half == 1:
            xT[:, :] = x[b, tok0 - 128:tok0 + NREAL].T
            pTc[:, :, :] = p[:, b, tok0 - 128:tok0 + NREAL].transpose(0, 2, 1)
        else:
            xT[:, 128:] = x[b, 0:NREAL].T
            pTc[:, :, 128:] = p[:, b, 0:NREAL].transpose(0, 2, 1)
        pos = (np.arange(NTOK) + tok0 - 128).astype(f32)
        ang = pos[:, None] * inv_freq[None, :]
        cs, sn = np.cos(ang).astype(f32), np.sin(ang).astype(f32)
        rope = np.zeros((2, 128, NTOK), f32)
        rope[0] = 1.0
        for pp in range(128):
            d = pp % 64
            if d < 16:
                rope[0, pp] = cs[:, d % 8]
                rope[1, pp] = -sn[:, d % 8] if d < 8 else sn[:, d % 8]
        in_maps.append({"xT": xT, "pT": pTc, "wst": wst, "cvec": cvec, "cbf": cb[half], "rope": rope})
    return in_maps


_CACHE = {}


def kernel(**inputs):
    in_maps = host_prepare(inputs)
    if "nc" not in _CACHE:
        _CACHE["nc"] = build()[0]
    nc = _CACHE["nc"]
    res = run_bass_kernel_spmd(nc, in_maps, core_ids=list(range(N_CORES)))
    out = np.empty((4, 4096, D), np.float32)
    for core in range(N_CORES):
        b, half = core // 2, core % 2
        out[b, half * NREAL:(half + 1) * NREAL, :] = res.results[core]["outT"].T
    return out
```
